# Optimizing a Trainium2 kernel written in Bass

```python
import math
import jax
import jax.numpy as jnp
from jax import lax
import numpy as np

D_MODEL = 2048
BATCH = 4
SEQ = 2048
DEPTH = 4

CHUNK = 64
EPS = 1e-6
N_BRANCH = 4
BRANCH_WIDTH = D_MODEL // N_BRANCH

DIFF_HEADS = 4
DIFF_QK_DIM = BRANCH_WIDTH // (2 * DIFF_HEADS)
DIFF_V_DIM = 2 * DIFF_QK_DIM
QUERY_BLOCK = 128

GLA_HEADS = 4
GLA_KEY_WIDTH = BRANCH_WIDTH // 2
GLA_DK = GLA_KEY_WIDTH // GLA_HEADS
GLA_DV = BRANCH_WIDTH // GLA_HEADS
GLA_GATE_RANK = 16
GLA_GATE_NORMALIZER = 16.0

CONV_CHANNELS = BRANCH_WIDTH
CONV_WIDTH = 31

SSD_HEAD_DIM = 64
SSD_HEADS = BRANCH_WIDTH // SSD_HEAD_DIM
SSD_GROUPS = 2
SSD_STATE = 128
SSD_CONV = 4
SSD_XBC = BRANCH_WIDTH + 2 * SSD_GROUPS * SSD_STATE

D_FF = 5632
FFN_CONV = 3

IN_SPLITS = (
    BRANCH_WIDTH, BRANCH_WIDTH, BRANCH_WIDTH,
    GLA_KEY_WIDTH, GLA_KEY_WIDTH, BRANCH_WIDTH, BRANCH_WIDTH, GLA_GATE_RANK,
    2 * CONV_CHANNELS,
    BRANCH_WIDTH, SSD_XBC, SSD_HEADS,
)
IN_COLS = sum(IN_SPLITS)

kernel_name = 'hybrid_gated_parallel_mixer_encoder'


def split_cols(u, sizes):
    offsets = np.cumsum(sizes)[:-1].tolist()
    return jnp.split(u, offsets, axis=-1)


def rms_norm(x, w, eps=EPS):
    xf = x.astype(jnp.float32)
    y = xf * lax.rsqrt(jnp.mean(xf * xf, axis=-1, keepdims=True) + eps)
    return (y * w.astype(jnp.float32)).astype(x.dtype)


def layer_norm(x, w, b, eps=EPS):
    xf = x.astype(jnp.float32)
    mu = jnp.mean(xf, axis=-1, keepdims=True)
    var = jnp.mean(jnp.square(xf - mu), axis=-1, keepdims=True)
    y = (xf - mu) * lax.rsqrt(var + eps)
    return (y * w.astype(jnp.float32) + b.astype(jnp.float32)).astype(x.dtype)


def causal_dwconv(x, w, b):
    width = w.shape[0]
    y = lax.conv_general_dilated(
        x, w[:, None, :].astype(x.dtype), window_strides=(1,), padding=[(width - 1, 0)],
        dimension_numbers=('NWC', 'WIO', 'NWC'), feature_group_count=x.shape[-1])
    return y + b


def diff_attention_branch(q, k, v, qk_gain, lam, subln, lambda_init):
    b, s, _ = q.shape
    q = rms_norm(q.reshape(b, s, 2 * DIFF_HEADS, DIFF_QK_DIM), qk_gain[0]) * (DIFF_QK_DIM ** -0.5)
    k = rms_norm(k.reshape(b, s, 2 * DIFF_HEADS, DIFF_QK_DIM), qk_gain[1])
    v = v.reshape(b, s, DIFF_HEADS, DIFF_V_DIM)
    lam = lam.astype(jnp.float32)
    lam_full = jnp.exp(jnp.sum(lam[0] * lam[1])) - jnp.exp(jnp.sum(lam[2] * lam[3])) + lambda_init
    outs = []
    for i in range(s // QUERY_BLOCK):
        q0 = i * QUERY_BLOCK
        kv_len = q0 + QUERY_BLOCK
        scores = jnp.einsum('bqhd,bkhd->bhqk', q[:, q0:kv_len], k[:, :kv_len]).astype(jnp.float32)
        q_chunk = (q0 + jnp.arange(QUERY_BLOCK)) // CHUNK
        k_chunk = jnp.arange(kv_len) // CHUNK
        scores = jnp.where(k_chunk[None, :] <= q_chunk[:, None], scores, -jnp.inf)
        probs = jax.nn.softmax(scores, axis=-1).reshape(b, DIFF_HEADS, 2, QUERY_BLOCK, kv_len)
        attn = (probs[:, :, 0] - lam_full * probs[:, :, 1]).astype(v.dtype)
        outs.append(jnp.einsum('bhqk,bkhd->bqhd', attn, v[:, :kv_len]))
    o = jnp.concatenate(outs, axis=1)
    o = rms_norm(o, subln) * (1.0 - lambda_init)
    return o.reshape(b, s, BRANCH_WIDTH)


def gla_chunked(q, k, v, gk):
    b, s, h, dk = q.shape
    dv = v.shape[-1]
    nc = s // CHUNK
    dtype = q.dtype

    def to_chunks(t):
        return t.reshape(b, nc, CHUNK, h, t.shape[-1]).transpose(1, 0, 3, 2, 4)

    causal = jnp.tril(jnp.ones((CHUNK, CHUNK), dtype=bool))

    def step(state, inp):
        q_c, k_c, v_c, g_c = inp
        cum = jnp.cumsum(g_c.astype(jnp.float32), axis=2)
        diff = cum[:, :, :, None, :] - cum[:, :, None, :, :]
        decay = jnp.exp(jnp.where(causal[:, :, None], diff, -jnp.inf)).astype(dtype)
        scores = jnp.einsum('bhtd,bhsd,bhtsd->bhts', q_c, k_c, decay)
        o = (jnp.einsum('bhts,bhse->bhte', scores, v_c)
             + jnp.einsum('bhtd,bhde->bhte', q_c * jnp.exp(cum).astype(dtype), state))
        k_dec = k_c * jnp.exp(cum[:, :, -1:] - cum).astype(dtype)
        state = (jnp.exp(cum[:, :, -1]).astype(dtype)[..., None] * state
                 + jnp.einsum('bhsd,bhse->bhde', k_dec, v_c))
        return state, o

    state0 = jnp.zeros((b, h, dk, dv), dtype)
    _, o = lax.scan(step, state0, (to_chunks(q), to_chunks(k), to_chunks(v), to_chunks(gk)))
    return o.transpose(1, 0, 3, 2, 4).reshape(b, s, h, dv)


def gla_branch(q, k, v, out_gate, gate_low, gk_w2, gk_b, norm_w):
    b, s, _ = q.shape
    gk = jax.nn.log_sigmoid((gate_low @ gk_w2 + gk_b).astype(jnp.float32)) / GLA_GATE_NORMALIZER
    o = gla_chunked(
        q.reshape(b, s, GLA_HEADS, GLA_DK) * (GLA_DK ** -0.5),
        k.reshape(b, s, GLA_HEADS, GLA_DK),
        v.reshape(b, s, GLA_HEADS, GLA_DV),
        gk.reshape(b, s, GLA_HEADS, GLA_DK))
    o = rms_norm(o, norm_w).reshape(b, s, BRANCH_WIDTH)
    return o * jax.nn.silu(out_gate)


def conformer_conv_branch(u, dw_w, dw_b, ln_w, ln_b):
    a, gate = jnp.split(u, 2, axis=-1)
    c = causal_dwconv(a * jax.nn.sigmoid(gate), dw_w, dw_b)
    return jax.nn.silu(layer_norm(c, ln_w, ln_b))


def ssd_chunked(x, dt, a, bm, cm):
    b, s, h, p = x.shape
    g, n = bm.shape[-2:]
    e = h // g
    nc = s // CHUNK
    dtype = x.dtype
    xc = x.reshape(b, nc, CHUNK, g, e, p)
    dtc = dt.reshape(b, nc, CHUNK, g, e)
    bc = bm.reshape(b, nc, CHUNK, g, n)
    cc = cm.reshape(b, nc, CHUNK, g, n)
    log_a = (dtc.astype(jnp.float32) * a.reshape(g, e)).transpose(0, 3, 4, 1, 2)
    cum = jnp.cumsum(log_a, axis=-1)
    causal = jnp.tril(jnp.ones((CHUNK, CHUNK), dtype=bool))
    seg = cum[..., :, None] - cum[..., None, :]
    decay_in = jnp.exp(jnp.where(causal, seg, -jnp.inf)).astype(dtype)
    xdt = xc * dtc[..., None].astype(dtype)
    cb = jnp.einsum('bclgn,bcsgn->bgcls', cc, bc)
    y_diag = jnp.einsum('bgcls,bgecls,bcsgep->bclgep', cb, decay_in, xdt)
    decay_to_end = jnp.exp(cum[..., -1:] - cum).astype(dtype)
    chunk_states = jnp.einsum('bcsgn,bgecs,bcsgep->cbgepn', bc, decay_to_end, xdt)
    chunk_decay = jnp.exp(cum[..., -1]).astype(dtype).transpose(3, 0, 1, 2)

    def step(state, inp):
        st, dec = inp
        return dec[..., None, None] * state + st, state

    _, h_prev = lax.scan(step, jnp.zeros((b, g, e, p, n), dtype), (chunk_states, chunk_decay))
    y_off = jnp.einsum('bclgn,bgecl,cbgepn->bclgep', cc, jnp.exp(cum).astype(dtype), h_prev)
    return (y_diag + y_off).reshape(b, s, h, p)


def ssd_branch(z, xbc, dt_raw, conv_w, conv_b, dt_bias, a_log, d_skip, norm_w):
    b, s, _ = z.shape
    xbc = jax.nn.silu(causal_dwconv(xbc, conv_w, conv_b))
    xs, bm, cm = split_cols(xbc, (BRANCH_WIDTH, SSD_GROUPS * SSD_STATE, SSD_GROUPS * SSD_STATE))
    xs = xs.reshape(b, s, SSD_HEADS, SSD_HEAD_DIM)
    dt = jax.nn.softplus(dt_raw + dt_bias)
    a = -jnp.exp(a_log.astype(jnp.float32))
    y = ssd_chunked(xs, dt, a, bm.reshape(b, s, SSD_GROUPS, SSD_STATE), cm.reshape(b, s, SSD_GROUPS, SSD_STATE))
    y = y + d_skip[:, None] * xs
    y = (y.reshape(b, s, BRANCH_WIDTH) * jax.nn.silu(z)).reshape(b, s, SSD_GROUPS, BRANCH_WIDTH // SSD_GROUPS)
    return rms_norm(y, norm_w.reshape(SSD_GROUPS, BRANCH_WIDTH // SSD_GROUPS)).reshape(b, s, BRANCH_WIDTH)


def setup_inputs(seed: int = 0) -> dict:
    key = jax.random.key(seed)
    ks = jax.random.split(key, 32)

    def nrm(i, shape, scale):
        return scale * jax.random.normal(ks[i], shape, jnp.float32)

    def gain(i, shape):
        return 1.0 + nrm(i, shape, 0.02)

    L = DEPTH
    dt0 = jnp.exp(jax.random.uniform(ks[15], (L, SSD_HEADS), jnp.float32, math.log(1e-3), math.log(1e-1)))
    return {
        'x': nrm(0, (BATCH, SEQ, D_MODEL), 1.0),
        'mix_norm': gain(1, (L, D_MODEL)),
        'w_in': nrm(2, (L, D_MODEL, IN_COLS), D_MODEL ** -0.5),
        'diff_qk_norm': gain(3, (L, 2, DIFF_QK_DIM)),
        'diff_lambda': nrm(4, (L, 4, DIFF_QK_DIM), 0.1),
        'diff_subln': gain(5, (L, DIFF_V_DIM)),
        'gla_gk_w2': nrm(6, (L, GLA_GATE_RANK, GLA_KEY_WIDTH), GLA_GATE_RANK ** -0.5),
        'gla_gk_b': nrm(7, (L, GLA_KEY_WIDTH), 0.1),
        'gla_norm': gain(8, (L, GLA_DV)),
        'conv_dw_w': nrm(9, (L, CONV_WIDTH, CONV_CHANNELS), CONV_WIDTH ** -0.5),
        'conv_dw_b': nrm(10, (L, CONV_CHANNELS), 0.02),
        'conv_ln_w': gain(11, (L, CONV_CHANNELS)),
        'conv_ln_b': nrm(12, (L, CONV_CHANNELS), 0.02),
        'ssd_conv_w': nrm(13, (L, SSD_CONV, SSD_XBC), SSD_CONV ** -0.5),
        'ssd_conv_b': nrm(14, (L, SSD_XBC), 0.02),
        'ssd_dt_bias': dt0 + jnp.log(-jnp.expm1(-dt0)),
        'ssd_a_log': jnp.log(jax.random.uniform(ks[16], (L, SSD_HEADS), jnp.float32, 1.0, 16.0)),
        'ssd_d': 1.0 + nrm(17, (L, SSD_HEADS), 0.1),
        'ssd_norm': gain(18, (L, BRANCH_WIDTH)),
        'w_branch': nrm(19, (L, N_BRANCH, BRANCH_WIDTH, D_MODEL), BRANCH_WIDTH ** -0.5),
        'w_gate': nrm(20, (L, N_BRANCH, D_MODEL, D_MODEL), D_MODEL ** -0.5),
        'b_gate': nrm(21, (L, N_BRANCH, D_MODEL), 0.1),
        'w_out': nrm(22, (L, D_MODEL, D_MODEL), D_MODEL ** -0.5),
        'ffn_norm': gain(23, (L, D_MODEL)),
        'ffn_w_up': nrm(24, (L, D_MODEL, 2 * D_FF), D_MODEL ** -0.5),
        'ffn_conv_w': nrm(25, (L, FFN_CONV, 2 * D_FF), FFN_CONV ** -0.5),
        'ffn_conv_b': nrm(26, (L, 2 * D_FF), 0.02),
        'ffn_w_down': nrm(27, (L, D_FF, D_MODEL), D_FF ** -0.5),
    }


def reference(x, mix_norm, w_in, diff_qk_norm, diff_lambda, diff_subln, gla_gk_w2, gla_gk_b, gla_norm,
              conv_dw_w, conv_dw_b, conv_ln_w, conv_ln_b, ssd_conv_w, ssd_conv_b, ssd_dt_bias, ssd_a_log,
              ssd_d, ssd_norm, w_branch, w_gate, b_gate, w_out, ffn_norm, ffn_w_up, ffn_conv_w, ffn_conv_b,
              ffn_w_down):
    for l in range(DEPTH):
        lambda_init = 0.8 - 0.6 * math.exp(-0.3 * l)
        h = rms_norm(x, mix_norm[l])
        (d_q, d_k, d_v, g_q, g_k, g_v, g_out, g_low, c_in, s_z, s_xbc, s_dt) = split_cols(h @ w_in[l], IN_SPLITS)
        branches = (
            diff_attention_branch(d_q, d_k, d_v, diff_qk_norm[l], diff_lambda[l], diff_subln[l], lambda_init),
            gla_branch(g_q, g_k, g_v, g_out, g_low, gla_gk_w2[l], gla_gk_b[l], gla_norm[l]),
            conformer_conv_branch(c_in, conv_dw_w[l], conv_dw_b[l], conv_ln_w[l], conv_ln_b[l]),
            ssd_branch(s_z, s_xbc, s_dt, ssd_conv_w[l], ssd_conv_b[l], ssd_dt_bias[l], ssd_a_log[l],
                       ssd_d[l], ssd_norm[l]),
        )
        merged = None
        for i, y in enumerate(branches):
            term = jax.nn.sigmoid(h @ w_gate[l, i] + b_gate[l, i]) * (y @ w_branch[l, i])
            merged = term if merged is None else merged + term
        x = x + merged @ w_out[l]
        h = rms_norm(x, ffn_norm[l])
        u = causal_dwconv(h @ ffn_w_up[l], ffn_conv_w[l], ffn_conv_b[l])
        gate, val = jnp.split(u, 2, axis=-1)
        x = x + (jax.nn.silu(gate) * val) @ ffn_w_down[l]
    return x
```

```python
import math
import numpy as np
from contextlib import ExitStack
import concourse.bass as bass
import concourse.mybir as mybir
from concourse.bass_utils import run_bass_kernel_spmd

F32 = mybir.dt.float32
BF16 = mybir.dt.bfloat16
AF = mybir.ActivationFunctionType
ALU = mybir.AluOpType
AX = mybir.AxisListType

T = 2048
D = 2048
KC = 16
L = 4
EPS = 1e-6
DFF = 5632
NPAIR = 44
NEG = -30000.0

OFF = dict(aq=0, ak=512, av=1024, bq=1536, bk=1792, bv=2048, bo=2560, bl=3072,
           ca=3088, cg=3600, dz=4112, dx=4624, dt=5648)


def _win_groups():
    r = lambda a, n: list(range(a, a + n))
    g = []
    g.append(r(OFF['aq'], 512))
    g.append(r(OFF['ak'], 512))
    g.append(r(OFF['bq'], 256) + r(OFF['bk'], 256))
    g.append(r(OFF['ca'], 128) + r(OFF['cg'], 128) + r(OFF['ca'] + 128, 128) + r(OFF['cg'] + 128, 128))
    g.append(r(OFF['ca'] + 256, 128) + r(OFF['cg'] + 256, 128) + r(OFF['ca'] + 384, 128) + r(OFF['cg'] + 384, 128))
    g.append(r(OFF['dx'], 512))
    g.append(r(OFF['dx'] + 512, 512))
    g.append(r(OFF['bl'], 16))
    g.append(r(OFF['av'], 512))
    g.append(r(OFF['bv'], 512))
    g.append(r(OFF['bo'], 512))
    g.append(r(OFF['dz'], 512))
    g.append(r(OFF['bk'], 256) + r(OFF['dt'], 8))
    return g


WIN_GROUPS = _win_groups()
WIN_NCOLS = [len(g) for g in WIN_GROUPS]


class KB:
    def __init__(self, nc):
        self.nc = nc
        self.es = ExitStack()
        self.E = dict(pe=nc.tensor, act=nc.scalar, dve=nc.vector, pool=nc.gpsimd, sp=nc.sync)
        self.psem = {}
        self.pcnt = {}
        for e in self.E:
            self.psem[e] = self.es.enter_context(nc.semaphore("p_" + e))
            self.pcnt[e] = 0
        self.waited = {}
        self.nsem = 0
        self.nname = 0

    def newsem(self, name):
        if getattr(self, 'freelist', None):
            key = self.freelist.pop()
            self.live.append(key)
            return key
        if not hasattr(self, 'live'):
            self.live = []
            self.freelist = []
        self.nsem += 1
        key = f"d:{name}_{self.nsem}"
        self.psem[key] = self.es.enter_context(self.nc.semaphore(f"{name}_{self.nsem}"))
        self.pcnt[key] = 0
        self.live.append(key)
        return key

    def sb(self, name, shape, dt, es=None):
        self.nname += 1
        return (es or self.es).enter_context(self.nc.sbuf_tensor(f"{name}_{self.nname}", shape, dt))

    def ps(self, name, shape, dt, es=None):
        self.nname += 1
        return (es or self.es).enter_context(self.nc.psum_tensor(f"{name}_{self.nname}", shape, dt))

    def mark(self, e, ins):
        ins.then_inc(self.psem[e], 1)
        self.pcnt[e] += 1
        return (e, self.pcnt[e])

    def wait(self, e, *evs):
        for ev in evs:
            if ev is None:
                continue
            if isinstance(ev, list):
                self.wait(e, *ev)
                continue
            key, val = ev
            if self.waited.get((e, key), 0) >= val:
                continue
            self.E[e].wait_ge(self.psem[key], val)
            self.waited[(e, key)] = val

    def dma(self, q, out, in_, key, **kw):
        ins = self.E[q].dma_start(out=out, in_=in_, **kw)
        ins.then_inc(self.psem[key], 16)
        self.pcnt[key] += 16
        return (key, self.pcnt[key])

    def barrier(self, extra=()):
        engines = ('pe', 'act', 'dve', 'sp')
        evs = [(e, self.pcnt[e]) for e in engines if self.pcnt[e] > 0] + [e for e in extra if e is not None]
        for e in engines:
            self.wait(e, *evs)
        self.last_barrier = evs
        if hasattr(self, 'live'):
            self.freelist.extend(self.live)
            self.live = []


class Ring:
    def __init__(self, bufs):
        self.bufs = bufs
        self.free = [None] * len(bufs)
        self.i = 0

    def get(self):
        i = self.i
        self.i = (self.i + 1) % len(self.bufs)
        return i, self.bufs[i], self.free[i]

    def release(self, i, ev):
        self.free[i] = ev


class WStream:
    def __init__(self, kb, slots, loads):
        self.kb = kb
        self.slots = slots
        self.n = len(slots)
        self.loads = loads
        self.keys = [kb.newsem("w") for _ in slots]
        self.free = [None] * self.n
        self.ev = {}
        kb.wait('pool', getattr(kb, 'last_barrier', None))
        for i in range(min(self.n, len(loads))):
            self._issue(i)

    def _issue(self, i):
        s = i % self.n
        self.kb.wait('pool', self.free[s])
        dram, view = self.loads[i]
        self.ev[i] = self.kb.dma('pool', view(self.slots[s]), dram, self.keys[s])

    def get(self, i):
        return self.slots[i % self.n], self.ev[i]

    def release(self, i, ev):
        self.free[i % self.n] = ev
        if i + self.n < len(self.loads):
            self._issue(i + self.n)


class Stager:
    def __init__(self, kb, bufs, q='sp'):
        self.kb = kb
        self.ring = Ring(bufs)
        self.keys = [kb.newsem("st") for _ in bufs]
        self.q = q
        self.last = []

    def get(self):
        return self.ring.get()

    def store(self, i, dst, src, ev):
        self.kb.wait(self.q, ev)
        e = self.kb.dma(self.q, dst, src, self.keys[i])
        self.ring.release(i, e)
        self.last.append(e)
        self.last = self.last[-len(self.keys):]
        return e


class Loader:
    def __init__(self, kb, bufs, q='sp'):
        self.kb = kb
        self.ring = Ring(bufs)
        self.keys = [kb.newsem("ld") for _ in bufs]
        self.q = q

    def load(self, fn):
        i, buf, free = self.ring.get()
        self.kb.wait(self.q, free)
        ev = None
        for o, s in fn(buf):
            ev = self.kb.dma(self.q, o, s, self.keys[i])
        return i, buf, ev

    def release(self, i, ev):
        self.ring.release(i, ev)


def phase_norm(kb, x_src, wb_dram, hT, ident, cst_eps):
    nc = kb.nc
    with ExitStack() as es:
        wb = kb.sb("nw", [128, D], F32, es)
        kw = kb.newsem("nw")
        ev_w = kb.dma('sp', wb[:], wb_dram, kw)
        ld = Loader(kb, [kb.sb("nx", [128, D], F32, es) for _ in range(2)])
        xn = Ring([kb.sb("nxn", [128, D], BF16, es) for _ in range(2)])
        junk = kb.sb("njunk", [128, D], BF16, es)
        ss = kb.sb("nss", [128, 16], F32, es)
        rstd = kb.sb("nrstd", [128, 16], F32, es)
        pst = Ring([kb.ps("npt", [128, 16, 128], BF16, es) for _ in range(2)])
        kb.wait('dve', ev_w)
        for tt in range(T // 128):
            li, xt, ev_x = ld.load(lambda b: [(b[:], x_src[tt * 128:(tt + 1) * 128, :])])
            kb.wait('act', ev_x)
            ev_sq = kb.mark('act', nc.scalar.activation(out=junk[:], in_=xt[:], func=AF.Square,
                                                         accum_out=ss[:, tt:tt + 1]))
            kb.wait('act', ev_sq)
            ev_sq = kb.mark('act', nc.scalar.activation(out=rstd[:, tt:tt + 1], in_=ss[:, tt:tt + 1], func=AF.Sqrt,
                                                         scale=1.0 / D, bias=cst_eps[:]))
            kb.wait('dve', ev_sq, ev_x)
            ev_rc = kb.mark('dve', nc.vector.reciprocal(out=rstd[:, tt:tt + 1], in_=rstd[:, tt:tt + 1]))
            kb.wait('dve', ev_rc)
            xi, xnb, xfree = xn.get()
            kb.wait('dve', xfree)
            ev_xn = kb.mark('dve', nc.vector.scalar_tensor_tensor(out=xnb[:], in0=xt[:], scalar=rstd[:, tt:tt + 1],
                                                                  in1=wb[:], op0=ALU.mult, op1=ALU.mult))
            ld.release(li, ev_xn)
            pi, pt, pfree = pst.get()
            kb.wait('pe', ev_xn, pfree)
            for c in range(KC):
                ins = nc.tensor.transpose(out=pt[:, c, :], in_=xnb[:, c * 128:(c + 1) * 128], identity=ident[:])
            ev_t = kb.mark('pe', ins)
            xn.release(xi, ev_t)
            e = 'act' if tt % 2 == 0 else 'dve'
            kb.wait(e, ev_t)
            if e == 'act':
                ins = nc.scalar.copy(out=hT[:, :, tt * 128:(tt + 1) * 128], in_=pt[:])
            else:
                ins = nc.vector.tensor_copy(out=hT[:, :, tt * 128:(tt + 1) * 128], in_=pt[:])
            pst.release(pi, kb.mark(e, ins))
        kb.barrier()


def phase_inproj(kb, l, hT, dr, cst):
    nc = kb.nc
    with ExitStack() as es:
        wsl = [kb.sb("w", [128, 16, 512], BF16, es) for _ in range(2)]
        loads = [(dr['win'][l, g, :, :, 0:WIN_NCOLS[g]], (lambda s, n=WIN_NCOLS[g]: s[:, :, 0:n])) for g in range(13)]
        ws = WStream(kb, wsl, loads)
        banks = Ring([kb.ps("b", [128, 512], F32, es) for _ in range(6)])
        sbank = Ring([kb.ps("sb", [128, 512], F32, es) for _ in range(2)])
        stf = Stager(kb, [kb.sb("stf", [128, 512], F32, es) for _ in range(3)])
        stb = Stager(kb, [kb.sb("stb", [128, 512], BF16, es) for _ in range(3)])
        sqr = Ring([kb.sb("sq", [128, 512], BF16, es) for _ in range(3)])
        rr = Ring([kb.sb("rr", [128, 512], F32, es) for _ in range(2)])
        sig = Ring([kb.sb("sig", [128, 512], F32, es) for _ in range(2)])
        qkg = kb.sb("qkg", [128, 2], F32, es)
        kq = kb.newsem("qkg")
        ev_g = kb.dma('sp', qkg[:], dr['qkg'][l], kq)
        kb.wait('dve', ev_g)
        ev_qkg = kb.mark('dve', nc.vector.tensor_scalar(out=qkg[:, 0:1], in0=qkg[:, 0:1], scalar1=0.125, scalar2=None, op0=ALU.mult))
        kb.wait('dve', ev_qkg)
        cnt = [0]

        def cp_eng():
            cnt[0] += 1
            return 'act' if cnt[0] % 2 == 0 else 'dve'

        def copy_out(e, out, in_):
            if e == 'act':
                return nc.scalar.copy(out=out, in_=in_)
            return nc.vector.tensor_copy(out=out, in_=in_)

        def mm_fm(wt, cc, tt, bank, m=128):
            for k in range(KC):
                ins = nc.tensor.matmul(bank[0:m, :], lhsT=wt[:, k, cc * 128:cc * 128 + m],
                                       rhs=hT[:, k, tt * 512:(tt + 1) * 512], start=(k == 0), stop=(k == KC - 1))
            return kb.mark('pe', ins)

        pending = []

        def flush_pending():
            while pending:
                pending.pop(0)()

        for g in (0, 1):
            wt, ev = ws.get(g)
            kb.wait('pe', ev)
            dst = dr['qT'] if g == 0 else dr['kT']
            for cc in range(4):
                for tt in range(4):
                    bi, bank, bfree = banks.get()
                    kb.wait('pe', bfree)
                    ev_mm = mm_fm(wt, cc, tt, bank)
                    flush_pending()
                    si, sq, sfree = sqr.get()
                    kb.wait('act', ev_mm, sfree)
                    ev_sq = kb.mark('act', nc.scalar.activation(out=sq[:], in_=bank[:], func=AF.Square))

                    def stats(bi=bi, bank=bank, si=si, sq=sq, ev_sq=ev_sq, cc=cc, tt=tt, g=g, dst=dst):
                        pi, pb, pfree = sbank.get()
                        kb.wait('pe', ev_sq, pfree)
                        ev_st = kb.mark('pe', nc.tensor.matmul(pb[:], lhsT=cst['onesblk'][:], rhs=sq[:], start=True, stop=True))
                        sqr.release(si, ev_st)
                        ri, r, rfree = rr.get()
                        kb.wait('dve', ev_st, rfree)
                        kb.wait('act', ev_st, rfree)
                        ev_r = kb.mark('act', nc.scalar.activation(out=r[:], in_=pb[:], func=AF.Sqrt, scale=1.0 / 64, bias=cst['eps'][:]))
                        kb.wait('dve', ev_r)
                        nc.vector.reciprocal(out=r[:], in_=r[:])
                        oi, ob, ofree = stb.get()
                        kb.wait('dve', ofree)
                        ev_o = kb.mark('dve', nc.vector.scalar_tensor_tensor(out=ob[:], in0=bank[:], scalar=qkg[:, g:g + 1],
                                                                           in1=r[:], op0=ALU.mult, op1=ALU.mult))
                        sbank.release(pi, ev_o)
                        rr.release(ri, ev_o)
                        banks.release(bi, ev_o)
                        stb.store(oi, dst[cc, :, tt * 512:(tt + 1) * 512], ob[:], ev_o)
                    pending.append(stats)
            flush_pending()
            ws.release(g, ev_mm)

        wt, ev = ws.get(2)
        kb.wait('pe', ev)
        for cc in range(4):
            for tt in range(4):
                bi, bank, bfree = banks.get()
                kb.wait('pe', bfree)
                ev_mm = mm_fm(wt, cc, tt, bank)
                e = cp_eng()
                oi, ob, ofree = stf.get()
                kb.wait(e, ev_mm, ofree)
                ev_o = kb.mark(e, copy_out(e, ob[:], bank[:]))
                banks.release(bi, ev_o)
                stf.store(oi, dr['gqkT'][cc, :, tt * 512:(tt + 1) * 512], ob[:], ev_o)
        ws.release(2, ev_mm)

        for g in (3, 4):
            wt, ev = ws.get(g)
            kb.wait('pe', ev)
            for pr in range(2):
                for tt in range(4):
                    bia, banka, bfree = banks.get()
                    kb.wait('pe', bfree)
                    ev_a = mm_fm(wt, 2 * pr, tt, banka)
                    big, bankg, bfree = banks.get()
                    kb.wait('pe', bfree)
                    ev_gm = mm_fm(wt, 2 * pr + 1, tt, bankg)
                    gi, sg, gfree = sig.get()
                    kb.wait('act', ev_gm, gfree)
                    ev_s = kb.mark('act', nc.scalar.activation(out=sg[:], in_=bankg[:], func=AF.Sigmoid))
                    banks.release(big, ev_s)
                    oi, ob, ofree = stf.get()
                    kb.wait('dve', ev_s, ev_a, ofree)
                    ev_o = kb.mark('dve', nc.vector.tensor_tensor(out=ob[:], in0=banka[:], in1=sg[:], op=ALU.mult))
                    banks.release(bia, ev_o)
                    sig.release(gi, ev_o)
                    c = (g - 3) * 2 + pr
                    stf.store(oi, dr['gluT'][c, :, tt * 512:(tt + 1) * 512], ob[:], ev_o)
            ws.release(g, ev_gm)

        for g in (5, 6):
            wt, ev = ws.get(g)
            kb.wait('pe', ev)
            for cc in range(4):
                for tt in range(4):
                    bi, bank, bfree = banks.get()
                    kb.wait('pe', bfree)
                    ev_mm = mm_fm(wt, cc, tt, bank)
                    e = cp_eng()
                    oi, ob, ofree = stf.get()
                    kb.wait(e, ev_mm, ofree)
                    ev_o = kb.mark(e, copy_out(e, ob[:], bank[:]))
                    banks.release(bi, ev_o)
                    stf.store(oi, dr['xbcT'][(g - 5) * 4 + cc, :, tt * 512:(tt + 1) * 512], ob[:], ev_o)
            ws.release(g, ev_mm)

        wt, ev = ws.get(7)
        kb.wait('pe', ev)
        for tt in range(4):
            bi, bank, bfree = banks.get()
            kb.wait('pe', bfree)
            ev_mm = mm_fm(wt, 0, tt, bank, m=16)
            e = cp_eng()
            oi, ob, ofree = stf.get()
            kb.wait(e, ev_mm, ofree)
            ev_o = kb.mark(e, copy_out(e, ob[0:16, :], bank[0:16, :]))
            banks.release(bi, ev_o)
            stf.store(oi, dr['glowT'][:, tt * 512:(tt + 1) * 512], ob[0:16, :], ev_o)
        ws.release(7, ev_mm)

        tm_dst = {8: ('v', BF16), 9: ('gv', BF16), 10: ('gout', F32), 11: ('z', F32)}
        for g in range(8, 13):
            wt, ev = ws.get(g)
            kb.wait('pe', ev)
            n = WIN_NCOLS[g]
            for tt in range(T // 128):
                bi, bank, bfree = banks.get()
                kb.wait('pe', bfree)
                for k in range(KC):
                    ins = nc.tensor.matmul(bank[:, 0:n], lhsT=hT[:, k, tt * 128:(tt + 1) * 128], rhs=wt[:, k, 0:n],
                                           start=(k == 0), stop=(k == KC - 1))
                ev_mm = kb.mark('pe', ins)
                e = cp_eng()
                if g == 12:
                    oi, ob, ofree = stf.get()
                    kb.wait(e, ev_mm, ofree)
                    ev_o = kb.mark(e, copy_out(e, ob[:, 0:n], bank[:, 0:n]))
                    banks.release(bi, ev_o)
                    kb.wait('sp', ev_o)
                    kb.dma('sp', dr['gk'][tt * 128:(tt + 1) * 128, :], ob[:, 0:256], stf.keys[oi])
                    stf.store(oi, dr['dt'][tt * 128:(tt + 1) * 128, :], ob[:, 256:264], ev_o)
                else:
                    name, dt_ = tm_dst[g]
                    st = stb if dt_ == BF16 else stf
                    oi, ob, ofree = st.get()
                    kb.wait(e, ev_mm, ofree)
                    ev_o = kb.mark(e, copy_out(e, ob[:], bank[:]))
                    banks.release(bi, ev_o)
                    st.store(oi, dr[name][tt * 128:(tt + 1) * 128, :], ob[:], ev_o)
            ws.release(g, ev_mm)
        kb.barrier(extra=stf.last + stb.last)


def mixer_conv(kb, l, dr, cst):
    nc = kb.nc
    with ExitStack() as es:
        cw = kb.sb("cw", [128, 4, 31], F32, es)
        cb = kb.sb("cb", [128, 4], F32, es)
        lw = kb.sb("lw", [128, 4], F32, es)
        lb = kb.sb("lb", [128, 4], F32, es)
        kp = kb.newsem("cp")
        kb.dma('sp', cw[:], dr['cdw'][l], kp)
        kb.dma('sp', cb[:], dr['cdb'][l], kp)
        kb.dma('sp', lw[:], dr['clnw'][l], kp)
        ev_p = kb.dma('sp', lb[:], dr['clnb'][l], kp)
        glu = [kb.sb("glu", [128, 30 + T], F32, es) for _ in range(4)]
        acc = [kb.sb("acc", [128, T], F32, es) for _ in range(4)]
        kg = kb.newsem("cg")
        ev_acc = []
        for c in range(4):
            e = 'dve'
            eng = kb.E[e]
            ev_z = kb.mark(e, eng.memset(glu[c][:, 0:30], 0.0))
            ev_l = kb.dma('sp', glu[c][:, 30:30 + T], dr['gluT'][c], kg)
            kb.wait(e, ev_l, ev_p, ev_z)
            eng.tensor_scalar(out=acc[c][:], in0=glu[c][:, 0:T], scalar1=cw[:, c, 0:1], scalar2=cb[:, c:c + 1],
                              op0=ALU.mult, op1=ALU.add)
            for j in range(1, 31):
                ins = eng.scalar_tensor_tensor(out=acc[c][:], in0=glu[c][:, j:j + T], scalar=cw[:, c, j:j + 1],
                                               in1=acc[c][:], op0=ALU.mult, op1=ALU.add)
            ev_acc.append(kb.mark(e, ins))
        ps1 = kb.ps("cs1", [128, 512], F32, es)
        ps2 = kb.ps("cs2", [128, 512], F32, es)
        sq = Ring([kb.sb("csq", [128, 512], F32, es) for _ in range(2)])
        mean = kb.sb("cmean", [128, 512], F32, es)
        msq = kb.sb("cmsq", [128, 512], F32, es)
        rstd = kb.sb("crstd", [128, 512], F32, es)
        tmp = Ring([kb.sb("ctmp", [128, 512], F32, es) for _ in range(2)])
        st = Stager(kb, [kb.sb("cst", [128, 512], BF16, es) for _ in range(2)])
        ev_prev = None
        for tt in range(4):
            sl = slice(tt * 512, (tt + 1) * 512)
            kb.wait('pe', ev_prev)
            for c in range(4):
                kb.wait('pe', ev_acc[c])
                nc.tensor.matmul(ps1[:], lhsT=cst['ones_f'][:], rhs=acc[c][:, sl], start=(c == 0), stop=(c == 3))
            ev_s1 = None
            for c in range(4):
                si, sb_, sfree = sq.get()
                kb.wait('act', ev_acc[c], sfree)
                ev_q = kb.mark('act', nc.scalar.activation(out=sb_[:], in_=acc[c][:, sl], func=AF.Square))
                kb.wait('pe', ev_q)
                ev_m = kb.mark('pe', nc.tensor.matmul(ps2[:], lhsT=cst['ones_f'][:], rhs=sb_[:], start=(c == 0), stop=(c == 3)))
                sq.release(si, ev_m)
            kb.wait('dve', ev_m)
            nc.vector.tensor_scalar(out=mean[:], in0=ps1[:], scalar1=1.0 / 512, scalar2=None, op0=ALU.mult)
            nc.vector.tensor_tensor(out=msq[:], in0=mean[:], in1=mean[:], op=ALU.mult)
            ev_v = kb.mark('dve', nc.vector.scalar_tensor_tensor(out=rstd[:], in0=ps2[:], scalar=1.0 / 512, in1=msq[:],
                                                                op0=ALU.mult, op1=ALU.subtract))
            kb.wait('act', ev_v)
            ev_sd = kb.mark('act', nc.scalar.activation(out=rstd[:], in_=rstd[:], func=AF.Sqrt, bias=cst['eps'][:]))
            kb.wait('dve', ev_sd)
            nc.vector.reciprocal(out=rstd[:], in_=rstd[:])
            for c in range(4):
                ti, tb, tfree = tmp.get()
                kb.wait('dve', tfree)
                nc.vector.tensor_tensor(out=tb[:], in0=acc[c][:, sl], in1=mean[:], op=ALU.subtract)
                ev_t = kb.mark('dve', nc.vector.tensor_tensor(out=tb[:], in0=tb[:], in1=rstd[:], op=ALU.mult))
                oi, ob, ofree = st.get()
                kb.wait('act', ev_t, ofree, ev_p)
                ev_y = kb.mark('act', nc.scalar.activation(out=ob[:], in_=tb[:], func=AF.Silu, scale=lw[:, c:c + 1], bias=lb[:, c:c + 1]))
                tmp.release(ti, ev_y)
                st.store(oi, dr['yT'][8 + c, :, sl], ob[:], ev_y)
            ev_prev = ev_t
        kb.barrier(extra=st.last + [('pool', kb.pcnt['pool'])])


def phase_gates(kb, l, hT, dr, cst):
    nc = kb.nc
    with ExitStack() as es:
        yT = kb.sb("yT", [128, 16, T], BF16, es)
        ky = kb.newsem("yT")
        for c in range(16):
            ev_y = kb.dma('sp', yT[:, c, :], dr['yT'][c], ky)
        bg = kb.sb("bg", [128, 4, 16], F32, es)
        ev_b = kb.dma('sp', bg[:], dr['bgate'][l], ky)
        wsl = [kb.sb("wg", [128, 20, 128], BF16, es) for _ in range(2)]
        loads = [(dr['wgb'][l, cc, i], (lambda s: s[:])) for cc in range(16) for i in range(4)]
        ws = WStream(kb, wsl, loads)
        gb = Ring([kb.ps("gb", [128, 512], F32, es) for _ in range(4)])
        bb = Ring([kb.ps("bb", [128, 512], F32, es) for _ in range(4)])
        sig = Ring([kb.sb("gsig", [128, 512], F32, es) for _ in range(2)])
        macc = Ring([kb.sb("macc", [128, T], F32, es) for _ in range(2)])
        tmp = kb.sb("gtmp", [128, 512], F32, es)
        st = Stager(kb, [kb.sb("gst", [128, T], BF16, es) for _ in range(2)])
        kb.wait('pe', ev_y)
        kb.wait('act', ev_b)
        n = 0
        for cc in range(16):
            mi, mb, mfree = macc.get()
            for i in range(4):
                wt, ev = ws.get(n)
                kb.wait('pe', ev)
                for tt in range(4):
                    sl = slice(tt * 512, (tt + 1) * 512)
                    gi, gbank, gfree = gb.get()
                    kb.wait('pe', gfree)
                    for k in range(KC):
                        ins = nc.tensor.matmul(gbank[:], lhsT=wt[:, k, :], rhs=hT[:, k, sl], start=(k == 0), stop=(k == KC - 1))
                    ev_g = kb.mark('pe', ins)
                    bi, bbank, bfree = bb.get()
                    kb.wait('pe', bfree)
                    for k in range(4):
                        ins = nc.tensor.matmul(bbank[:], lhsT=wt[:, 16 + k, :], rhs=yT[:, 4 * i + k, sl], start=(k == 0), stop=(k == 3))
                    ev_br = kb.mark('pe', ins)
                    si, sg, sfree = sig.get()
                    kb.wait('act', ev_g, sfree)
                    ev_s = kb.mark('act', nc.scalar.activation(out=sg[:], in_=gbank[:], func=AF.Sigmoid, bias=bg[:, i, cc:cc + 1]))
                    gb.release(gi, ev_s)
                    kb.wait('dve', ev_s, ev_br)
                    if i == 0:
                        kb.wait('dve', mfree)
                        ev_m = kb.mark('dve', nc.vector.tensor_tensor(out=mb[:, sl], in0=bbank[:], in1=sg[:], op=ALU.mult))
                    else:
                        nc.vector.tensor_tensor(out=tmp[:], in0=bbank[:], in1=sg[:], op=ALU.mult)
                        ev_m = kb.mark('dve', nc.vector.tensor_tensor(out=mb[:, sl], in0=mb[:, sl], in1=tmp[:], op=ALU.add))
                    bb.release(bi, ev_m)
                    sig.release(si, ev_m)
                ws.release(n, ev_br)
                n += 1
            oi, ob, ofree = st.get()
            kb.wait('act', ev_m, ofree)
            ev_c = kb.mark('act', nc.scalar.copy(out=ob[:], in_=mb[:]))
            macc.release(mi, ev_c)
            st.store(oi, dr['mT'][cc], ob[:], ev_c)
        kb.barrier(extra=st.last)


def phase_outproj(kb, l, x_src, x_dst, dr, cst):
    nc = kb.nc
    with ExitStack() as es:
        mT = kb.sb("mT", [128, 16, T], BF16, es)
        km = kb.newsem("mT")
        for c in range(16):
            ev_m = kb.dma('sp', mT[:, c, :], dr['mT'][c], km)
        wsl = [kb.sb("wo", [128, 16, 512], BF16, es) for _ in range(2)]
        loads = [(dr['wout'][l, g], (lambda s: s[:])) for g in range(4)]
        ws = WStream(kb, wsl, loads)
        banks = Ring([kb.ps("ob", [128, 512], F32, es) for _ in range(4)])
        ld = Loader(kb, [kb.sb("ox", [128, 512], F32, es) for _ in range(3)])
        st = Stager(kb, [kb.sb("oo", [128, 512], F32, es) for _ in range(3)])
        kb.wait('pe', ev_m)
        for g in range(4):
            wt, ev = ws.get(g)
            kb.wait('pe', ev)
            for tt in range(16):
                rows = slice(tt * 128, (tt + 1) * 128)
                cols = slice(g * 512, (g + 1) * 512)
                li, xb, ev_x = ld.load(lambda b: [(b[:], x_src[rows, cols])])
                bi, bank, bfree = banks.get()
                kb.wait('pe', bfree)
                for k in range(KC):
                    ins = nc.tensor.matmul(bank[:], lhsT=mT[:, k, rows], rhs=wt[:, k, :], start=(k == 0), stop=(k == KC - 1))
                ev_mm = kb.mark('pe', ins)
                oi, ob, ofree = st.get()
                kb.wait('dve', ev_mm, ev_x, ofree)
                ev_o = kb.mark('dve', nc.vector.tensor_tensor(out=ob[:], in0=bank[:], in1=xb[:], op=ALU.add))
                banks.release(bi, ev_o)
                ld.release(li, ev_o)
                st.store(oi, x_dst[rows, cols], ob[:], ev_o)
            ws.release(g, ev_mm)
        kb.barrier(extra=st.last)


def phase_ffn_up(kb, l, hT, dr, cst):
    nc = kb.nc
    with ExitStack() as es:
        fw = kb.sb("fw", [128, 88, 3], F32, es)
        fb = kb.sb("fb", [128, 88], F32, es)
        kf = kb.newsem("fp")
        kb.dma('sp', fw[:], dr['fcw'][l], kf)
        ev_p = kb.dma('sp', fb[:], dr['fcb'][l], kf)
        wsl = [kb.sb("wu", [128, 16, 512], BF16, es) for _ in range(2)]
        loads = [(dr['wup'][l, g], (lambda s: s[:])) for g in range(22)]
        ws = WStream(kb, wsl, loads)
        banks = Ring([kb.ps("ub", [128, 512], F32, es) for _ in range(8)])
        ubuf = Ring([kb.sb("uu", [128, 2 + T], F32, es) for _ in range(4)])
        accg = kb.sb("accg", [128, T], F32, es)
        accv = kb.sb("accv", [128, T], F32, es)
        st = Stager(kb, [kb.sb("ast", [128, 2, T], BF16, es) for _ in range(2)])
        for i in range(4):
            ev_z = kb.mark('dve', nc.vector.memset(ubuf.bufs[i][:, 0:2], 0.0))
        kb.wait('act', ev_z)
        kb.wait('dve', ev_p)
        sti = None
        for g in range(22):
            wt, ev = ws.get(g)
            kb.wait('pe', ev)
            for pr in range(2):
                j = 2 * g + pr
                us = []
                for half in range(2):
                    cc = 2 * pr + half
                    ui, ub, ufree = ubuf.get()
                    evs = []
                    for tt in range(4):
                        bi, bank, bfree = banks.get()
                        kb.wait('pe', bfree)
                        for k in range(KC):
                            ins = nc.tensor.matmul(bank[:], lhsT=wt[:, k, cc * 128:(cc + 1) * 128], rhs=hT[:, k, tt * 512:(tt + 1) * 512],
                                                   start=(k == 0), stop=(k == KC - 1))
                        ev_mm = kb.mark('pe', ins)
                        kb.wait('act', ev_mm, ufree)
                        ev_c = kb.mark('act', nc.scalar.copy(out=ub[:, 2 + tt * 512:2 + (tt + 1) * 512], in_=bank[:]))
                        banks.release(bi, ev_c)
                    us.append((ui, ub, ev_c))
                outs = []
                for half, accb in ((0, accg), (1, accv)):
                    ui, ub, ev_c = us[half]
                    q = 2 * j + half
                    kb.wait('dve', ev_c)
                    nc.vector.tensor_scalar(out=accb[:], in0=ub[:, 0:T], scalar1=fw[:, q, 0:1], scalar2=fb[:, q:q + 1],
                                            op0=ALU.mult, op1=ALU.add)
                    nc.vector.scalar_tensor_tensor(out=accb[:], in0=ub[:, 1:1 + T], scalar=fw[:, q, 1:2], in1=accb[:],
                                                   op0=ALU.mult, op1=ALU.add)
                    ev_a = kb.mark('dve', nc.vector.scalar_tensor_tensor(out=accb[:], in0=ub[:, 2:2 + T], scalar=fw[:, q, 2:3],
                                                                        in1=accb[:], op0=ALU.mult, op1=ALU.add))
                    outs.append(ev_a)
                ui_g, ub_g, _ = us[0]
                kb.wait('act', outs[0])
                ev_s = kb.mark('act', nc.scalar.activation(out=ub_g[:, 2:2 + T], in_=accg[:], func=AF.Silu))
                if pr == 0:
                    sti, sob, sofree = st.get()
                kb.wait('dve', ev_s, sofree)
                ev_o = kb.mark('dve', nc.vector.tensor_tensor(out=sob[:, pr, :], in0=ub_g[:, 2:2 + T], in1=accv[:], op=ALU.mult))
                ubuf.release(us[0][0], ev_o)
                ubuf.release(us[1][0], outs[1])
            kb.wait('sp', ev_o)
            j0 = 2 * g
            for t8 in range(8):
                e_st = kb.dma('sp', dr['aT'][t8, :, j0:j0 + 2, :], sob[:, :, t8 * 256:(t8 + 1) * 256], st.keys[sti])
            st.ring.release(sti, e_st)
            st.last.append(e_st)
            ws.release(g, ev_mm)
        kb.barrier(extra=st.last[-2:])


def phase_ffn_down(kb, l, x_src, x_dst, dr, cst):
    nc = kb.nc
    with ExitStack() as es:
        wsl = [kb.sb("wd", [128, 44, 512], BF16, es) for _ in range(1)]
        loads = [(dr['wdn'][l, g], (lambda s: s[:])) for g in range(4)]
        ws = WStream(kb, wsl, loads)
        banks = Ring([kb.ps("db", [128, 512], F32, es) for _ in range(4)])
        la = Loader(kb, [kb.sb("da", [128, 44, 256], BF16, es) for _ in range(2)])
        ld = Loader(kb, [kb.sb("dx", [128, 512], F32, es) for _ in range(3)])
        st = Stager(kb, [kb.sb("do", [128, 512], F32, es) for _ in range(3)])
        for g in range(4):
            wt, ev = ws.get(g)
            kb.wait('pe', ev)
            cols = slice(g * 512, (g + 1) * 512)
            for t8 in range(8):
                ai, ab, ev_a = la.load(lambda b: [(b[:], dr['aT'][t8])])
                kb.wait('pe', ev_a)
                for h2 in range(2):
                    rows = slice(t8 * 256 + h2 * 128, t8 * 256 + (h2 + 1) * 128)
                    li, xb, ev_x = ld.load(lambda b: [(b[:], x_src[rows, cols])])
                    bi, bank, bfree = banks.get()
                    kb.wait('pe', bfree)
                    for k in range(NPAIR):
                        ins = nc.tensor.matmul(bank[:], lhsT=ab[:, k, h2 * 128:(h2 + 1) * 128], rhs=wt[:, k, :],
                                               start=(k == 0), stop=(k == NPAIR - 1))
                    ev_mm = kb.mark('pe', ins)
                    oi, ob, ofree = st.get()
                    kb.wait('dve', ev_mm, ev_x, ofree)
                    ev_o = kb.mark('dve', nc.vector.tensor_tensor(out=ob[:], in0=bank[:], in1=xb[:], op=ALU.add))
                    banks.release(bi, ev_o)
                    ld.release(li, ev_o)
                    st.store(oi, x_dst[rows, cols], ob[:], ev_o)
                la.release(ai, ev_mm)
            ws.release(g, ev_mm)
        kb.barrier(extra=st.last)


def mixer_attn(kb, l, dr, cst, lam_init):
    nc = kb.nc
    with ExitStack() as es:
        lam = kb.sb("lam", [128, 256], F32, es)
        sub = kb.sb("subln", [128, 128], F32, es)
        kp = kb.newsem("ap")
        kb.dma('sp', lam[:], dr['lam'][l], kp)
        ev_p = kb.dma('sp', sub[:], dr['subln'][l], kp)
        prod = kb.sb("aprod", [128, 2, 64], F32, es)
        s2 = kb.sb("as2", [128, 2], F32, es)
        nl = kb.sb("anl", [128, 1], F32, es)
        kb.wait('dve', ev_p)
        nc.vector.tensor_tensor(out=prod[:, 0, :], in0=lam[:, 0:64], in1=lam[:, 64:128], op=ALU.mult)
        nc.vector.tensor_tensor(out=prod[:, 1, :], in0=lam[:, 128:192], in1=lam[:, 192:256], op=ALU.mult)
        ev = kb.mark('dve', nc.vector.reduce_sum(out=s2[:], in_=prod[:], axis=AX.X))
        kb.wait('act', ev)
        ev = kb.mark('act', nc.scalar.activation(out=s2[:], in_=s2[:], func=AF.Exp))
        kb.wait('dve', ev)
        ev = kb.mark('dve', nc.vector.tensor_tensor(out=nl[:], in0=s2[:, 1:2], in1=s2[:, 0:1], op=ALU.subtract))
        kb.wait('dve', ev)
        nc.vector.tensor_scalar(out=nl[:], in0=nl[:], scalar1=-lam_init, scalar2=None, op0=ALU.add)
        ev_nl = kb.mark('dve', nc.vector.tensor_scalar(out=sub[:], in0=sub[:], scalar1=1.0 - lam_init, scalar2=None, op0=ALU.mult))
        kb.wait('dve', ev_nl)

        qk = Loader(kb, [kb.sb("aqk", [128, 2, T], BF16, es) for _ in range(2)])
        va = Loader(kb, [kb.sb("ava", [128, 16, 129], BF16, es) for _ in range(2)])
        for b in va.ring.bufs:
            ev_one = kb.mark('dve', nc.vector.memset(b[:, :, 128:129], 1.0))
        kb.wait('pe', ev_one)
        sbank = Ring([kb.ps("asb", [128, 512], F32, es) for _ in range(3)])
        obank = Ring([kb.ps("aob", [128, 2, 129], F32, es) for _ in range(2)])
        tbank = Ring([kb.ps("atb", [128, 128], BF16, es) for _ in range(2)])
        pT = Ring([kb.sb("apT", [128, 512], BF16, es) for _ in range(3)])
        rc = Ring([kb.sb("arc", [128, 4], F32, es) for _ in range(2)])
        t1 = kb.sb("at1", [128, 128], F32, es)
        ob = Ring([kb.sb("ao", [128, 128], F32, es) for _ in range(2)])
        junk = kb.sb("ajunk", [128, 128], F32, es)
        yb = Ring([kb.sb("ay", [128, 128], BF16, es) for _ in range(2)])
        st = Stager(kb, [kb.sb("ayT", [128, T], BF16, es) for _ in range(2)])
        vview = dr['v'].rearrange("(j p) c -> p j c", p=128)
        for h in range(4):
            qi, qkb, ev_q = qk.load(lambda b: [(b[:, 0, :], dr['qT'][h]), (b[:, 1, :], dr['kT'][h])])
            vi, vb, ev_v = va.load(lambda b: [(b[:, :, 0:128], vview[:, :, h * 128:(h + 1) * 128])])
            kb.wait('pe', ev_q, ev_v)
            sti, yTh, yfree = st.get()
            for i in range(16):
                oi, obk, ofree = obank.get()
                kb.wait('pe', ofree)
                for m in range(2):
                    rows = slice(m * 64, (m + 1) * 64)
                    for jg in range(0, i + 1, 4):
                        je = min(jg + 4, i + 1)
                        w = (je - jg) * 128
                        si, sbk, sfree = sbank.get()
                        kb.wait('pe', sfree)
                        for jj in range(jg, je):
                            ins = nc.tensor.matmul(sbk[:, (jj - jg) * 128:(jj - jg + 1) * 128], lhsT=qkb[rows, 1, jj * 128:(jj + 1) * 128],
                                                   rhs=qkb[rows, 0, i * 128:(i + 1) * 128], start=True, stop=True)
                        ev_s = kb.mark('pe', ins)
                        pi, pb, pfree = pT.get()
                        kb.wait('act', ev_s, pfree)
                        ev_e = kb.mark('act', nc.scalar.activation(out=pb[:, 0:w], in_=sbk[:, 0:w], func=AF.Exp))
                        sbank.release(si, ev_e)
                        if je == i + 1:
                            off = (i - jg) * 128
                            kb.wait('dve', ev_e)
                            ev_e = kb.mark('dve', nc.vector.memset(pb[64:128, off:off + 64], 0.0))
                        kb.wait('pe', ev_e)
                        for jj in range(jg, je):
                            ins = nc.tensor.matmul(obk[:, m, :], lhsT=pb[:, (jj - jg) * 128:(jj - jg + 1) * 128], rhs=vb[:, jj, :],
                                                   start=(jj == 0), stop=(jj == i))
                        ev_pv = kb.mark('pe', ins)
                        pT.release(pi, ev_pv)
                ri, rcb, rfree = rc.get()
                kb.wait('dve', ev_pv, rfree)
                ev = kb.mark('dve', nc.vector.reciprocal(out=rcb[:, 0:2], in_=obk[:, :, 128]))
                kb.wait('dve', ev)
                ev = kb.mark('dve', nc.vector.tensor_tensor(out=rcb[:, 1:2], in0=rcb[:, 1:2], in1=nl[:], op=ALU.mult))
                kb.wait('dve', ev)
                nc.vector.tensor_scalar(out=t1[:], in0=obk[:, 0, 0:128], scalar1=rcb[:, 0:1], scalar2=None, op0=ALU.mult)
                bi, obuf, bfree = ob.get()
                kb.wait('dve', bfree)
                ev_o = kb.mark('dve', nc.vector.scalar_tensor_tensor(out=obuf[:], in0=obk[:, 1, 0:128], scalar=rcb[:, 1:2], in1=t1[:],
                                                                    op0=ALU.mult, op1=ALU.add))
                obank.release(oi, ev_o)
                kb.wait('act', ev_o)
                ev = kb.mark('act', nc.scalar.activation(out=junk[:], in_=obuf[:], func=AF.Square, accum_out=rcb[:, 2:3]))
                kb.wait('act', ev)
                ev = kb.mark('act', nc.scalar.activation(out=rcb[:, 3:4], in_=rcb[:, 2:3], func=AF.Sqrt, scale=1.0 / 128, bias=cst['eps'][:]))
                kb.wait('dve', ev)
                ev = kb.mark('dve', nc.vector.reciprocal(out=rcb[:, 3:4], in_=rcb[:, 3:4]))
                kb.wait('dve', ev)
                yi, ybuf, yfree2 = yb.get()
                kb.wait('dve', yfree2)
                ev_y = kb.mark('dve', nc.vector.scalar_tensor_tensor(out=ybuf[:], in0=obuf[:], scalar=rcb[:, 3:4], in1=sub[:],
                                                                    op0=ALU.mult, op1=ALU.mult))
                ob.release(bi, ev_y)
                rc.release(ri, ev_y)
                ti, tb, tfree = tbank.get()
                kb.wait('pe', ev_y, tfree)
                ev_t = kb.mark('pe', nc.tensor.transpose(out=tb[:], in_=ybuf[:], identity=cst['ident'][:]))
                yb.release(yi, ev_t)
                kb.wait('act', ev_t, yfree)
                ev_c = kb.mark('act', nc.scalar.copy(out=yTh[:, i * 128:(i + 1) * 128], in_=tb[:]))
                tbank.release(ti, ev_c)
            qk.release(qi, ev_pv)
            va.release(vi, ev_pv)
            st.store(sti, dr['yT'][h], yTh[:], ev_c)
        kb.barrier(extra=st.last)


def mixer_gla(kb, l, dr, cst):
    nc = kb.nc
    with ExitStack() as es:
        kp = kb.newsem("bp")
        w2f = kb.sb("w2f", [32, 256], F32, es)
        w2a = kb.sb("w2a", [32, 256], BF16, es)
        gn = kb.sb("gn", [64, 512], F32, es)
        glf = kb.sb("glf", [32, T], F32, es)
        glb = kb.sb("glb", [32, T], BF16, es)
        ev_m = kb.mark('dve', nc.vector.memset(glf[:], 1.0))
        kb.wait('sp', ev_m)
        kb.dma('sp', w2f[:], dr['w2aug'][l], kp)
        kb.dma('sp', gn[:], dr['glan'][l, 0:64, :], kp)
        ev_p = kb.dma('sp', glf[0:16, :], dr['glowT'][:, :], kp)
        kb.wait('dve', ev_p)
        nc.vector.tensor_copy(out=w2a[:], in_=w2f[:])
        ev_gl = kb.mark('dve', nc.vector.tensor_copy(out=glb[:], in_=glf[:]))
        g_all = kb.sb("g_all", [64, 32, 256], BF16, es)
        qtT = [kb.sb("qtT", [128, T], BF16, es) for _ in range(2)]
        ktT = [kb.sb("ktT", [128, T], BF16, es) for _ in range(2)]
        elast = kb.sb("elast", [128, 2, 32], F32, es)
        S = [kb.sb("S", [128, 256], F32, es) for _ in range(2)]
        Sbf = [kb.sb("Sbf", [128, 33, 128], BF16, es) for _ in range(2)]
        yTst = kb.sb("byT", [128, 4, T], BF16, es)
        for hp in range(2):
            nc.vector.memset(S[hp][:], 0.0)
            ev_z = kb.mark('dve', nc.vector.memset(Sbf[hp][:, 0, :], 0.0))
        banks = Ring([kb.ps("bb", [128, 512], F32, es) for _ in range(6)])
        tbank = Ring([kb.ps("btb", [128, 4, 64], BF16, es) for _ in range(2)])
        f1 = Ring([kb.sb("bf1", [128, 512], F32, es) for _ in range(3)])
        f2 = Ring([kb.sb("bf2", [128, 512], F32, es) for _ in range(3)])
        kb.wait('pe', ev_gl)
        for c in range(32):
            bi, bank, bfree = banks.get()
            kb.wait('pe', bfree)
            ev_mm = kb.mark('pe', nc.tensor.matmul(bank[0:64, 0:256], lhsT=glb[:, c * 64:(c + 1) * 64], rhs=w2a[:], start=True, stop=True))
            fi, fb, ffree = f1.get()
            kb.wait('act', ev_mm, ffree)
            nc.scalar.activation(out=fb[0:64, 0:256], in_=bank[0:64, 0:256], func=AF.Exp, scale=-1.0)
            ev_a = kb.mark('act', nc.scalar.activation(out=fb[0:64, 0:256], in_=fb[0:64, 0:256], func=AF.Ln, bias=cst['one'][0:64, :]))
            banks.release(bi, ev_a)
            kb.wait('dve', ev_a)
            ev_g = kb.mark('dve', nc.vector.tensor_scalar(out=g_all[:, c, :], in0=fb[0:64, 0:256], scalar1=-1.0 / 16, scalar2=None, op0=ALU.mult))
            f1.release(fi, ev_g)
        import os
        if os.environ.get('BSTAGE') == '1':
            kb.barrier()
            return
        ldq = Loader(kb, [kb.sb("bldq", [128, 2, 512], F32, es) for _ in range(2)])
        kb.wait('pe', ev_g)
        for t4 in range(4):
            sl = slice(t4 * 512, (t4 + 1) * 512)
            for hp in range(2):
                li, qb, ev_q = ldq.load(lambda b: [(b[:, 0, :], dr['gqkT'][hp, :, sl]), (b[:, 1, :], dr['gqkT'][2 + hp, :, sl])])
                bi, bank, bfree = banks.get()
                kb.wait('pe', bfree)
                for cj in range(8):
                    c = t4 * 8 + cj
                    ins = nc.tensor.matmul(bank[:, cj * 64:(cj + 1) * 64], lhsT=g_all[:, c, hp * 128:(hp + 1) * 128], rhs=cst['tri_b'][:],
                                           start=True, stop=True)
                ev_mm = kb.mark('pe', ins)
                ai, eq, afree = f1.get()
                ci, ek, cfree = f2.get()
                kb.wait('act', ev_mm, afree, cfree)
                nc.scalar.activation(out=eq[:], in_=bank[:], func=AF.Exp)
                ev_e = kb.mark('act', nc.scalar.activation(out=ek[:], in_=bank[:], func=AF.Exp, scale=-1.0))
                banks.release(bi, ev_e)
                kb.wait('dve', ev_e, ev_q)
                nc.vector.scalar_tensor_tensor(out=qtT[hp][:, sl], in0=qb[:, 0, :], scalar=0.125, in1=eq[:], op0=ALU.mult, op1=ALU.mult)
                nc.vector.tensor_tensor(out=ktT[hp][:, sl], in0=qb[:, 1, :], in1=ek[:], op=ALU.mult)
                ev_d = kb.mark('dve', nc.vector.tensor_copy(out=elast[:, hp, t4 * 8:(t4 + 1) * 8], in_=eq[:, 63:512:64]))
                f1.release(ai, ev_d)
                f2.release(ci, ev_d)
                ldq.release(li, ev_d)
        kb.wait('pe', ev_d)
        kb.wait('dve', ev_d)
        if os.environ.get('BSTAGE') == '2':
            kb.barrier()
            return
        ldk = Loader(kb, [kb.sb("bldk", [64, 256], F32, es) for _ in range(2)])
        ldv = Loader(kb, [kb.sb("bldv", [64, 512], BF16, es) for _ in range(2)])
        ldo = Loader(kb, [kb.sb("bldo", [64, 512], F32, es) for _ in range(2)])
        khat = Ring([kb.sb("khat", [64, 256], BF16, es) for _ in range(2)])
        am = Ring([kb.sb("bam", [64, 4, 64], BF16, es) for _ in range(2)])
        sm = Ring([kb.sb("bsm", [64, 8], F32, es) for _ in range(2)])
        yb = Ring([kb.sb("bby", [64, 512], BF16, es) for _ in range(2)])
        bjunk = kb.sb("bjunk", [64, 128], F32, es)
        ev_state = [ev_z, ev_z]
        for c in range(32):
            rows = slice(c * 64, (c + 1) * 64)
            ki, kbuf, ev_k = ldk.load(lambda b: [(b[:], dr['gk'][rows, :])])
            vi, vbuf, ev_v = ldv.load(lambda b: [(b[:], dr['gv'][rows, :])])
            oi, obuf, ev_o = ldo.load(lambda b: [(b[:], dr['gout'][rows, :])])
            bi, bank, bfree = banks.get()
            kb.wait('pe', bfree)
            ev_mm = kb.mark('pe', nc.tensor.matmul(bank[0:64, 0:256], lhsT=cst['slow_b'][:], rhs=g_all[:, c, :], start=True, stop=True))
            fi, fb, ffree = f1.get()
            kb.wait('act', ev_mm, ffree)
            ev_e = kb.mark('act', nc.scalar.activation(out=fb[0:64, 0:256], in_=bank[0:64, 0:256], func=AF.Exp))
            banks.release(bi, ev_e)
            hi, kh, hfree = khat.get()
            kb.wait('dve', ev_e, ev_k, hfree)
            ev_kh = kb.mark('dve', nc.vector.tensor_tensor(out=kh[:], in0=kbuf[:], in1=fb[0:64, 0:256], op=ALU.mult))
            f1.release(fi, ev_kh)
            ldk.release(ki, ev_kh)
            sb_ = []
            for par in range(2):
                bi, sbk, bfree = banks.get()
                kb.wait('pe', bfree)
                r_ = slice(par * 64, (par + 1) * 64)
                for hh in range(2):
                    ins = nc.tensor.matmul(sbk[0:64, hh * 64:(hh + 1) * 64], lhsT=ktT[hh][r_, rows], rhs=qtT[hh][r_, rows], start=True, stop=True)
                sb_.append((bi, sbk))
            ev_s = kb.mark('pe', ins)
            ai, amb, afree = am.get()
            kb.wait('dve', ev_s, afree)
            for par in range(2):
                ev_am = kb.mark('dve', nc.vector.tensor_tensor(out=amb[:, par::2, :], in0=sb_[par][1][0:64, 0:128].rearrange("p (h t) -> p h t", h=2),
                                                                in1=cst['tri4'][:, 0:2, :], op=ALU.mult))
            banks.release(sb_[0][0], ev_am)
            banks.release(sb_[1][0], ev_am)
            bo, obk, bfree = banks.get()
            kb.wait('pe', bfree, ev_am, ev_v, ev_state[0], ev_state[1])
            for h in range(4):
                ins = nc.tensor.matmul(obk[0:64, h * 128:(h + 1) * 128], lhsT=amb[:, h, :], rhs=vbuf[:, h * 128:(h + 1) * 128], start=True, stop=True)
            o2 = []
            for par in range(2):
                b2, ob2, bfree = banks.get()
                kb.wait('pe', bfree)
                r_ = slice(par * 64, (par + 1) * 64)
                for hh in range(2):
                    ins = nc.tensor.matmul(ob2[0:64, hh * 128:(hh + 1) * 128], lhsT=qtT[hh][r_, rows], rhs=Sbf[hh][r_, c, :], start=True, stop=True)
                o2.append((b2, ob2))
            ev_ob = kb.mark('pe', ins)
            am.release(ai, ev_ob)
            kb.wait('pe', ev_kh)
            for hp in range(2):
                bk, kvb, bfree = banks.get()
                kb.wait('pe', bfree)
                ev_kv = kb.mark('pe', nc.tensor.matmul(kvb[:, 0:256], lhsT=kh[:, hp * 128:(hp + 1) * 128], rhs=vbuf[:, hp * 256:(hp + 1) * 256],
                                                       start=True, stop=True))
                kb.wait('dve', ev_kv)
                ev_S = kb.mark('dve', nc.vector.scalar_tensor_tensor(out=S[hp][:], in0=S[hp][:], scalar=elast[:, hp, c:c + 1], in1=kvb[:, 0:256],
                                                                    op0=ALU.mult, op1=ALU.add))
                banks.release(bk, ev_S)
                kb.wait('act', ev_S)
                nc.scalar.copy(out=Sbf[hp][0:64, c + 1, :], in_=S[hp][0:64, 0:128])
                ev_state[hp] = kb.mark('act', nc.scalar.copy(out=Sbf[hp][64:128, c + 1, :], in_=S[hp][64:128, 128:256]))
                kb.wait('dve', ev_state[hp])
            khat.release(hi, ev_kv)
            ldv.release(vi, ev_kv)
            fi, sq, ffree = f1.get()
            kb.wait('act', ev_ob, ffree)
            for par in range(2):
                ev_q = kb.mark('act', nc.scalar.copy(out=sq[0:64, :].rearrange("p (h d) -> p h d", h=4)[:, par::2, :],
                                                      in_=o2[par][1][0:64, 0:256].rearrange("p (h d) -> p h d", h=2)))
            banks.release(o2[0][0], ev_q)
            banks.release(o2[1][0], ev_q)
            kb.wait('dve', ev_q)
            ev_os = kb.mark('dve', nc.vector.tensor_tensor(out=sq[0:64, :], in0=obk[0:64, :], in1=sq[0:64, :], op=ALU.add))
            banks.release(bo, ev_os)
            si, smb, sfree = sm.get()
            kb.wait('act', ev_os, sfree)
            for h in range(4):
                ev = kb.mark('act', nc.scalar.activation(out=bjunk[:], in_=sq[0:64, h * 128:(h + 1) * 128], func=AF.Square, accum_out=smb[:, h:h + 1]))
            kb.wait('act', ev)
            ev = kb.mark('act', nc.scalar.activation(out=smb[:, 4:8], in_=smb[:, 0:4], func=AF.Sqrt, scale=1.0 / 128, bias=cst['eps'][0:64, :]))
            gi, sg, gfree = f2.get()
            kb.wait('act', ev_o, gfree)
            ev_sg = kb.mark('act', nc.scalar.activation(out=sg[0:64, :], in_=obuf[:], func=AF.Silu))
            ldo.release(oi, ev_sg)
            kb.wait('dve', ev)
            ev = kb.mark('dve', nc.vector.reciprocal(out=smb[:, 4:8], in_=smb[:, 4:8]))
            kb.wait('dve', ev)
            for h in range(4):
                nc.vector.tensor_scalar(out=sq[0:64, h * 128:(h + 1) * 128], in0=sq[0:64, h * 128:(h + 1) * 128], scalar1=smb[:, 4 + h:5 + h],
                                        scalar2=None, op0=ALU.mult)
            kb.wait('dve', ev_sg)
            nc.vector.tensor_tensor(out=sq[0:64, :], in0=sq[0:64, :], in1=sg[0:64, :], op=ALU.mult)
            yi, ybuf, yfree = yb.get()
            kb.wait('dve', yfree)
            ev_y = kb.mark('dve', nc.vector.tensor_tensor(out=ybuf[:], in0=sq[0:64, :], in1=gn[:], op=ALU.mult))
            f1.release(fi, ev_y)
            f2.release(gi, ev_y)
            sm.release(si, ev_y)
            ti, tb, tfree = tbank.get()
            kb.wait('pe', ev_y, tfree)
            for h in range(4):
                ins = nc.tensor.transpose(out=tb[:, h, :], in_=ybuf[:, h * 128:(h + 1) * 128], identity=cst['ident'][0:64, 0:64])
            ev_t = kb.mark('pe', ins)
            yb.release(yi, ev_t)
            kb.wait('act', ev_t)
            ev_c = kb.mark('act', nc.scalar.copy(out=yTst[:, :, rows], in_=tb[:]))
            tbank.release(ti, ev_c)
        kb.wait('sp', ev_c)
        ks = kb.newsem("bst")
        evs = [kb.dma('sp', dr['yT'][4 + h], yTst[:, h, :], ks) for h in range(4)]
        kb.barrier(extra=evs[-1:])


def mixer_ssd(kb, l, dr, cst):
    nc = kb.nc
    with ExitStack() as es:
        kp = kb.newsem("dp")
        scw = kb.sb("scw", [128, 8, 4], F32, es)
        scb = kb.sb("scb", [128, 8], F32, es)
        dtb = kb.sb("dtb", [64, 8], F32, es)
        aneg = kb.sb("aneg", [64, 8], F32, es)
        dsk = kb.sb("dsk", [64, 512], F32, es)
        nw = kb.sb("dnw", [64, 512], F32, es)
        kb.dma('sp', scw[:], dr['scw'][l], kp)
        kb.dma('sp', scb[:], dr['scb'][l], kp)
        kb.dma('sp', dtb[:], dr['dtb'][l, 0:64, :], kp)
        kb.dma('sp', aneg[:], dr['alog'][l, 0:64, :], kp)
        kb.dma('sp', dsk[:], dr['ssdd'][l, 0:64, :], kp)
        ev_p = kb.dma('sp', nw[:], dr['ssdn'][l, 0:64, :], kp)
        kb.wait('act', ev_p)
        ev = kb.mark('act', nc.scalar.activation(out=aneg[:], in_=aneg[:], func=AF.Exp))
        kb.wait('dve', ev, ev_p)
        ev_an = kb.mark('dve', nc.vector.tensor_scalar(out=aneg[:], in0=aneg[:], scalar1=-1.0, scalar2=None, op0=ALU.mult))
        kb.wait('pool', ev_p)
        xc = [kb.sb("xc", [128, T], BF16, es) for _ in range(8)]
        cbuf = [kb.sb("dcb", [128, 3 + T], F32, es) for _ in range(2)]
        cacc = [kb.sb("dca", [128, T], F32, es) for _ in range(2)]
        kc = [kb.newsem("dc"), kb.newsem("dc")]
        cfree = [None, None]
        afree = [None, None]
        ev_x = None
        for ch in range(8):
            p = ch % 2
            e = 'dve'
            eng = kb.E[e]
            kb.wait(e, cfree[p])
            ev_z = kb.mark(e, eng.memset(cbuf[p][:, 0:3], 0.0))
            kb.wait('sp', cfree[p])
            ev_l = kb.dma('sp', cbuf[p][:, 3:3 + T], dr['xbcT'][ch], kc[p])
            kb.wait(e, ev_l, ev_z, afree[p])
            eng.tensor_scalar(out=cacc[p][:], in0=cbuf[p][:, 0:T], scalar1=scw[:, ch, 0:1], scalar2=scb[:, ch:ch + 1], op0=ALU.mult, op1=ALU.add)
            for j in range(1, 4):
                ins = eng.scalar_tensor_tensor(out=cacc[p][:], in0=cbuf[p][:, j:j + T], scalar=scw[:, ch, j:j + 1], in1=cacc[p][:],
                                               op0=ALU.mult, op1=ALU.add)
            ev_a = kb.mark(e, ins)
            cfree[p] = ev_a
            kb.wait('act', ev_a)
            ev_x = kb.mark('act', nc.scalar.activation(out=xc[ch][:], in_=cacc[p][:], func=AF.Silu))
            afree[p] = ev_x
        kb.wait('pe', ev_x)
        H = [kb.sb("H", [128, 256], F32, es) for _ in range(2)]
        Hbf = [kb.sb("Hbf", [128, 256], BF16, es) for _ in range(2)]
        yTst = kb.sb("dyT", [128, 4, T], BF16, es)
        for g in range(2):
            nc.vector.memset(H[g][:], 0.0)
            ev_h0 = kb.mark('dve', nc.vector.memset(Hbf[g][:], 0.0))
        ev_hbf = [ev_h0, ev_h0]
        banks = Ring([kb.ps("db", [128, 512], F32, es) for _ in range(6)])
        tbank = Ring([kb.ps("dtb", [128, 1024], BF16, es) for _ in range(2)])
        ldt = Loader(kb, [kb.sb("dldt", [64, 8], F32, es) for _ in range(2)])
        ldz = Loader(kb, [kb.sb("dldz", [64, 512], F32, es) for _ in range(2)])
        xB = Ring([kb.sb("dxB", [64, 768], BF16, es) for _ in range(2)])
        sm = Ring([kb.sb("dsm", [128, 48], F32, es) for _ in range(2)])
        lab = Ring([kb.sb("dlab", [64, 8], BF16, es) for _ in range(2)])
        rseg = Ring([kb.sb("drs", [64, 8, 64], BF16, es) for _ in range(2)])
        Bw = Ring([kb.sb("dBw", [64, 8, 128], BF16, es) for _ in range(2)])
        Eb = Ring([kb.sb("dE", [64, 512], F32, es) for _ in range(2)])
        cbs = Ring([kb.sb("dcbs", [64, 128], F32, es) for _ in range(2)])
        MT = Ring([kb.sb("dMT", [64, 8, 64], BF16, es) for _ in range(2)])
        ydsb = Ring([kb.sb("dyd", [64, 512], F32, es) for _ in range(2)])
        ybuf = Ring([kb.sb("dy", [64, 512], F32, es) for _ in range(2)])
        szb = Ring([kb.sb("dsz", [64, 512], F32, es) for _ in range(2)])
        yo16 = Ring([kb.sb("dyo", [64, 512], BF16, es) for _ in range(2)])
        junk = kb.sb("djunk", [64, 256], F32, es)
        kb.wait('dve', ev_an)
        for c in range(32):
            rows = slice(c * 64, (c + 1) * 64)
            di, dtr, ev_d = ldt.load(lambda b: [(b[:], dr['dt'][rows, :])])
            zi, zb, ev_zl = ldz.load(lambda b: [(b[:], dr['z'][rows, :])])
            ti, tb, tfree = tbank.get()
            kb.wait('pe', tfree)
            for k in range(6):
                ins = nc.tensor.transpose(out=tb[0:64, k * 128:(k + 1) * 128], in_=xc[k][:, rows], identity=cst['ident'][:])
            ev_t = kb.mark('pe', ins)
            xi, xb, xfree = xB.get()
            kb.wait('act', ev_t, xfree)
            ev_xb = kb.mark('act', nc.scalar.copy(out=xb[:], in_=tb[0:64, 0:768]))
            tbank.release(ti, ev_xb)
            si, s_, sfree = sm.get()
            kb.wait('dve', ev_d, sfree)
            ev = kb.mark('dve', nc.vector.tensor_tensor(out=s_[0:64, 0:8], in0=dtr[:], in1=dtb[:], op=ALU.add))
            ldt.release(di, ev)
            kb.wait('act', ev)
            ev = kb.mark('act', nc.scalar.activation(out=s_[0:64, 0:8], in_=s_[0:64, 0:8], func=AF.Exp))
            kb.wait('act', ev)
            ev = kb.mark('act', nc.scalar.activation(out=s_[0:64, 0:8], in_=s_[0:64, 0:8], func=AF.Ln, bias=cst['one'][0:64, :]))
            kb.wait('dve', ev)
            ev = kb.mark('dve', nc.vector.tensor_tensor(out=s_[0:64, 8:16], in0=s_[0:64, 0:8], in1=aneg[:], op=ALU.mult))
            kb.wait('dve', ev)
            li, lb_, lfree = lab.get()
            kb.wait('dve', lfree)
            ev_la = kb.mark('dve', nc.vector.tensor_copy(out=lb_[:], in_=s_[0:64, 8:16]))
            ri, rs_, rfree = rseg.get()
            kb.wait('pool', ev_la, rfree)
            for e8 in range(8):
                ins = nc.gpsimd.tensor_scalar(out=rs_[:, e8, :], in0=cst['tri_f'][:], scalar1=s_[0:64, 8 + e8:9 + e8], scalar2=None, op0=ALU.mult)
            ev_rs = kb.mark('pool', ins)
            bi, cb_, bfree = banks.get()
            kb.wait('pe', ev_la, bfree)
            nc.tensor.matmul(cb_[0:64, 0:8], lhsT=cst['tri_b'][:], rhs=lb_[:], start=True, stop=True)
            nc.tensor.matmul(cb_[0:64, 8:16], lhsT=cst['slow_b'][:], rhs=lb_[:], start=True, stop=True)
            ev_cm = kb.mark('pe', nc.tensor.matmul(cb_[:, 16:24], lhsT=cst['ones_b'][0:64, :], rhs=lb_[:], start=True, stop=True))
            lab.release(li, ev_cm)
            kb.wait('act', ev_cm)
            nc.scalar.activation(out=s_[0:64, 16:32], in_=cb_[0:64, 0:16], func=AF.Exp)
            ev_ex = kb.mark('act', nc.scalar.activation(out=s_[:, 32:40], in_=cb_[:, 16:24], func=AF.Exp))
            banks.release(bi, ev_ex)
            kb.wait('dve', ev_ex)
            ev_w = kb.mark('dve', nc.vector.tensor_tensor(out=s_[0:64, 40:48], in0=s_[0:64, 24:32], in1=s_[0:64, 0:8], op=ALU.mult))
            wi, bw, wfree = Bw.get()
            kb.wait('pool', ev_w, ev_xb, wfree)
            for e8 in range(8):
                g = e8 // 4
                ins = nc.gpsimd.tensor_scalar(out=bw[:, e8, :], in0=xb[:, 512 + g * 128:512 + (g + 1) * 128], scalar1=s_[0:64, 40 + e8:41 + e8],
                                              scalar2=None, op0=ALU.mult)
            ev_bw = kb.mark('pool', ins)
            b1, segb, bfree = banks.get()
            kb.wait('pe', ev_rs, bfree)
            for g in range(2):
                nc.tensor.matmul(segb[0:64, g * 256:(g + 1) * 256], lhsT=cst['slow_b'][:], rhs=rs_[:, g * 4:(g + 1) * 4, :], start=True, stop=False)
                ins = nc.tensor.matmul(segb[0:64, g * 256:(g + 1) * 256], lhsT=cst['ident'][0:64, 0:64], rhs=cst['negm4'][:], start=False, stop=True)
            ev_sg = kb.mark('pe', ins)
            rseg.release(ri, ev_sg)
            b2, cbb, bfree = banks.get()
            kb.wait('pe', bfree)
            for g in range(2):
                ins = nc.tensor.matmul(cbb[0:64, g * 64:(g + 1) * 64], lhsT=xc[4 + g][:, rows], rhs=xc[6 + g][:, rows], start=True, stop=True)
            ev_cb = kb.mark('pe', ins)
            ei, E, efree = Eb.get()
            ci, cs_, cfree2 = cbs.get()
            kb.wait('act', ev_sg, ev_cb, efree, cfree2)
            nc.scalar.activation(out=E[:], in_=segb[0:64, :], func=AF.Exp)
            ev_E = kb.mark('act', nc.scalar.copy(out=cs_[:], in_=cbb[0:64, 0:128]))
            banks.release(b1, ev_E)
            banks.release(b2, ev_E)
            mi, mt, mfree = MT.get()
            kb.wait('dve', ev_E, mfree)
            for e8 in range(8):
                g = e8 // 4
                ins = nc.vector.scalar_tensor_tensor(out=mt[:, e8, :], in0=E[:, e8 * 64:(e8 + 1) * 64], scalar=s_[0:64, e8:e8 + 1],
                                                     in1=cs_[:, g * 64:(g + 1) * 64], op0=ALU.mult, op1=ALU.mult)
            ev_mt = kb.mark('dve', ins)
            Eb.release(ei, ev_mt)
            cbs.release(ci, ev_mt)
            b3, ydb, bfree = banks.get()
            kb.wait('pe', ev_mt, ev_xb, bfree)
            for e8 in range(8):
                ins = nc.tensor.matmul(ydb[0:64, e8 * 64:(e8 + 1) * 64], lhsT=mt[:, e8, :], rhs=xb[:, e8 * 64:(e8 + 1) * 64], start=True, stop=True)
            ev_yd = kb.mark('pe', ins)
            MT.release(mi, ev_yd)
            b4, yob, bfree = banks.get()
            kb.wait('pe', bfree, ev_hbf[0], ev_hbf[1])
            for g in range(2):
                ins = nc.tensor.matmul(yob[0:64, g * 256:(g + 1) * 256], lhsT=xc[6 + g][:, rows], rhs=Hbf[g][:], start=True, stop=True)
            ev_yo = kb.mark('pe', ins)
            b5, csb, bfree = banks.get()
            kb.wait('pe', ev_bw, bfree)
            for e8 in range(8):
                ins = nc.tensor.matmul(csb[:, e8 * 64:(e8 + 1) * 64], lhsT=bw[:, e8, :], rhs=xb[:, e8 * 64:(e8 + 1) * 64], start=True, stop=True)
            ev_cs = kb.mark('pe', ins)
            Bw.release(wi, ev_cs)
            kb.wait('dve', ev_cs)
            for e8 in range(8):
                g = e8 // 4
                cs2 = slice((e8 % 4) * 64, (e8 % 4 + 1) * 64)
                ins = nc.vector.scalar_tensor_tensor(out=H[g][:, cs2], in0=H[g][:, cs2], scalar=s_[:, 32 + e8:33 + e8],
                                                     in1=csb[:, e8 * 64:(e8 + 1) * 64], op0=ALU.mult, op1=ALU.add)
            ev_H = kb.mark('dve', ins)
            banks.release(b5, ev_H)
            kb.wait('act', ev_H, ev_yo)
            nc.scalar.copy(out=Hbf[0][:], in_=H[0][:])
            ev_hb = kb.mark('act', nc.scalar.copy(out=Hbf[1][:], in_=H[1][:]))
            ev_hbf = [ev_hb, ev_hb]
            kb.wait('dve', ev_hb)
            yi_, ydv, yfree = ydsb.get()
            kb.wait('act', ev_yd, yfree)
            ev_ydc = kb.mark('act', nc.scalar.copy(out=ydv[:], in_=ydb[0:64, :]))
            banks.release(b3, ev_ydc)
            yb_i, y, ybfree = ybuf.get()
            kb.wait('dve', ev_ydc, ev_yo, ybfree)
            for e8 in range(8):
                cs2 = slice(e8 * 64, (e8 + 1) * 64)
                ins = nc.vector.scalar_tensor_tensor(out=y[:, cs2], in0=yob[0:64, cs2], scalar=s_[0:64, 16 + e8:17 + e8], in1=ydv[:, cs2],
                                                     op0=ALU.mult, op1=ALU.add)
            ev_c1 = kb.mark('dve', ins)
            banks.release(b4, ev_c1)
            ydsb.release(yi_, ev_c1)
            zi2, sz, zfree = szb.get()
            kb.wait('act', ev_zl, zfree)
            ev_sz = kb.mark('act', nc.scalar.activation(out=sz[:], in_=zb[:], func=AF.Silu))
            ldz.release(zi, ev_sz)
            tmpd = ydv
            nc.vector.tensor_tensor(out=tmpd[:], in0=xb[:, 0:512], in1=dsk[:], op=ALU.mult)
            nc.vector.tensor_tensor(out=y[:], in0=y[:], in1=tmpd[:], op=ALU.add)
            kb.wait('dve', ev_sz)
            ev_y3 = kb.mark('dve', nc.vector.tensor_tensor(out=y[:], in0=y[:], in1=sz[:], op=ALU.mult))
            ydsb.release(yi_, ev_y3)
            xB.release(xi, ev_y3)
            szb.release(zi2, ev_y3)
            kb.wait('act', ev_y3)
            for g in range(2):
                ev = kb.mark('act', nc.scalar.activation(out=junk[:], in_=y[:, g * 256:(g + 1) * 256], func=AF.Square, accum_out=s_[0:64, 24 + g:25 + g]))
            kb.wait('act', ev)
            ev = kb.mark('act', nc.scalar.activation(out=s_[0:64, 26:28], in_=s_[0:64, 24:26], func=AF.Sqrt, scale=1.0 / 256, bias=cst['eps'][0:64, :]))
            kb.wait('dve', ev)
            ev = kb.mark('dve', nc.vector.reciprocal(out=s_[0:64, 26:28], in_=s_[0:64, 26:28]))
            kb.wait('dve', ev)
            oi, yo_, ofree = yo16.get()
            kb.wait('dve', ofree)
            for g in range(2):
                ins = nc.vector.scalar_tensor_tensor(out=yo_[:, g * 256:(g + 1) * 256], in0=y[:, g * 256:(g + 1) * 256], scalar=s_[0:64, 26 + g:27 + g],
                                                     in1=nw[:, g * 256:(g + 1) * 256], op0=ALU.mult, op1=ALU.mult)
            ev_yf = kb.mark('dve', ins)
            ybuf.release(yb_i, ev_yf)
            sm.release(si, ev_yf)
            ti, tb, tfree = tbank.get()
            kb.wait('pe', ev_yf, tfree)
            for h in range(4):
                ins = nc.tensor.transpose(out=tb[:, h * 64:(h + 1) * 64], in_=yo_[:, h * 128:(h + 1) * 128], identity=cst['ident'][0:64, 0:64])
            ev_t2 = kb.mark('pe', ins)
            yo16.release(oi, ev_t2)
            kb.wait('act', ev_t2)
            ev_c = kb.mark('act', nc.scalar.copy(out=yTst[:, :, rows], in_=tb[:, 0:256].rearrange("p (h t) -> p h t", h=4)))
            tbank.release(ti, ev_c)
        kb.wait('sp', ev_c)
        ks = kb.newsem("dst")
        evs = [kb.dma('sp', dr['yT'][12 + h], yTst[:, h, :], ks) for h in range(4)]
        kb.barrier(extra=evs[-1:] + [('pool', kb.pcnt['pool'])])

def make_consts(kb):
    nc = kb.nc
    c = {}
    g = nc.gpsimd
    ones_f = kb.sb("onesf", [128, 128], F32)
    g.memset(ones_f[:], 1.0)
    zeros_f = kb.sb("zerosf", [128, 256], F32)
    g.memset(zeros_f[:], 0.0)
    ident_f = kb.sb("identf", [128, 128], F32)
    g.affine_select(out=ident_f[:], in_=ones_f[:], pattern=[[-1, 128]], compare_op=ALU.is_equal, fill=0.0, base=0, channel_multiplier=1)
    ident = kb.sb("ident", [128, 128], BF16)
    g.tensor_copy(out=ident[:], in_=ident_f[:])
    ones_b = kb.sb("onesb", [128, 128], BF16)
    g.tensor_copy(out=ones_b[:], in_=ones_f[:])
    ob = kb.sb("onesblk", [128, 128], BF16)
    g.memset(ob[:], 0.0)
    g.memset(ob[0:64, 0:64], 1.0)
    g.memset(ob[64:128, 64:128], 1.0)
    eps = kb.sb("epsc", [128, 1], F32)
    g.memset(eps[:], EPS)
    one = kb.sb("onec", [128, 1], F32)
    g.memset(one[:], 1.0)
    tri_f = kb.sb("trif", [64, 64], F32)
    g.affine_select(out=tri_f[:], in_=ones_f[0:64, 0:64], pattern=[[1, 64]], compare_op=ALU.is_ge, fill=0.0, base=0, channel_multiplier=-1)
    tri_b = kb.sb("trib", [64, 64], BF16)
    g.tensor_copy(out=tri_b[:], in_=tri_f[:])
    slow_f = kb.sb("slowf", [64, 64], F32)
    g.affine_select(out=slow_f[:], in_=ones_f[0:64, 0:64], pattern=[[-1, 64]], compare_op=ALU.is_gt, fill=0.0, base=0, channel_multiplier=1)
    slow_b = kb.sb("slowb", [64, 64], BF16)
    g.tensor_copy(out=slow_b[:], in_=slow_f[:])
    tri4 = kb.sb("tri4", [64, 4, 64], F32)
    for a in range(4):
        g.tensor_copy(out=tri4[:, a, :], in_=tri_f[:])
    negf = kb.sb("negf", [64, 64], F32)
    g.affine_select(out=negf[:], in_=zeros_f[0:64, 0:64], pattern=[[1, 64]], compare_op=ALU.is_ge, fill=NEG, base=0, channel_multiplier=-1)
    negm4 = kb.sb("negm4", [64, 256], BF16)
    for a in range(4):
        ins = g.tensor_copy(out=negm4[:, a * 64:(a + 1) * 64], in_=negf[:])
    ev = kb.mark('pool', ins)
    c.update(ident=ident, ident_f=ident_f, ones_f=ones_f, ones_b=ones_b, onesblk=ob, eps=eps, one=one, tri_f=tri_f, tri_b=tri_b,
             slow_b=slow_b, tri4=tri4, negm4=negm4, ev=ev)
    return c


def prep_host(inputs):
    f = np.float32
    P = {}
    rep = lambda a, n=128: np.ascontiguousarray(np.broadcast_to(a[:, None, :], (a.shape[0], n, a.shape[-1])))
    w_in = inputs['w_in']
    win = np.zeros((L, 13, 128, 16, 512), f)
    for g, cols in enumerate(WIN_GROUPS):
        sub = w_in[:, :, cols]
        win[:, g, :, :, 0:len(cols)] = sub.reshape(L, 16, 128, len(cols)).transpose(0, 2, 1, 3)
    P['win'] = win
    q = inputs['diff_qk_norm']
    P['qkg'] = np.ascontiguousarray(np.concatenate([q, q], axis=2).transpose(0, 2, 1))
    P['mixn'] = rep(inputs['mix_norm'])
    P['ffnn'] = rep(inputs['ffn_norm'])
    P['lam'] = rep(inputs['diff_lambda'].reshape(L, 256))
    P['subln'] = rep(inputs['diff_subln'])
    w2a = np.zeros((L, 32, 256), f)
    w2a[:, 0:16] = inputs['gla_gk_w2']
    w2a[:, 16] = inputs['gla_gk_b']
    P['w2aug'] = w2a
    P['glan'] = rep(np.tile(inputs['gla_norm'], (1, 4)))
    P['cdw'] = np.ascontiguousarray(inputs['conv_dw_w'].reshape(L, 31, 4, 128).transpose(0, 3, 2, 1))
    fm = lambda a, n: np.ascontiguousarray(a.reshape(L, n, 128).transpose(0, 2, 1))
    P['cdb'] = fm(inputs['conv_dw_b'], 4)
    P['clnw'] = fm(inputs['conv_ln_w'], 4)
    P['clnb'] = fm(inputs['conv_ln_b'], 4)
    P['scw'] = np.ascontiguousarray(inputs['ssd_conv_w'].reshape(L, 4, 8, 128).transpose(0, 3, 2, 1))
    P['scb'] = fm(inputs['ssd_conv_b'], 8)
    P['dtb'] = rep(inputs['ssd_dt_bias'])
    P['alog'] = rep(inputs['ssd_a_log'])
    P['ssdd'] = rep(np.repeat(inputs['ssd_d'], 64, axis=1))
    P['ssdn'] = rep(inputs['ssd_norm'])
    wg = inputs['w_gate'].reshape(L, 4, 16, 128, 16, 128)
    wb = inputs['w_branch'].reshape(L, 4, 4, 128, 16, 128)
    wgb = np.empty((L, 16, 4, 128, 20, 128), f)
    wgb[:, :, :, :, 0:16, :] = wg.transpose(0, 4, 1, 3, 2, 5)
    wgb[:, :, :, :, 16:20, :] = wb.transpose(0, 4, 1, 3, 2, 5)
    P['wgb'] = wgb
    P['bgate'] = np.ascontiguousarray(inputs['b_gate'].reshape(L, 4, 16, 128).transpose(0, 3, 1, 2))
    P['wout'] = np.ascontiguousarray(inputs['w_out'].reshape(L, 16, 128, 4, 512).transpose(0, 3, 2, 1, 4))
    up = inputs['ffn_w_up'].reshape(L, 16, 128, 2, NPAIR, 128)
    wup = np.empty((L, 22, 128, 16, 4, 128), f)
    upr = up.transpose(0, 4, 2, 1, 3, 5)
    upr = upr.reshape(L, 22, 2, 128, 16, 2, 128)
    wup[:] = upr.transpose(0, 1, 3, 4, 2, 5, 6).reshape(L, 22, 128, 16, 4, 128)
    P['wup'] = wup.reshape(L, 22, 128, 16, 512)
    fcw = inputs['ffn_conv_w'].reshape(L, 3, 2, NPAIR, 128)
    P['fcw'] = np.ascontiguousarray(fcw.transpose(0, 4, 3, 2, 1).reshape(L, 128, 88, 3))
    fcb = inputs['ffn_conv_b'].reshape(L, 2, NPAIR, 128)
    P['fcb'] = np.ascontiguousarray(fcb.transpose(0, 3, 2, 1).reshape(L, 128, 88))
    P['wdn'] = np.ascontiguousarray(inputs['ffn_w_down'].reshape(L, NPAIR, 128, 4, 512).transpose(0, 3, 2, 1, 4))
    return P


SCRATCH = dict(
    qT=([4, 128, T], BF16), kT=([4, 128, T], BF16), v=([T, 512], BF16),
    gqkT=([4, 128, T], F32), gk=([T, 256], F32), gv=([T, 512], BF16), gout=([T, 512], F32), glowT=([16, T], F32),
    gluT=([4, 128, T], F32), xbcT=([8, 128, T], F32), z=([T, 512], F32), dt=([T, 8], F32),
    yT=([16, 128, T], BF16), mT=([16, 128, T], BF16), aT=([8, 128, NPAIR, 256], BF16),
    xm=([T, D], F32), xc=([T, D], F32),
)

PARAM_SHAPES = dict(
    win=[L, 13, 128, 16, 512], qkg=[L, 128, 2], mixn=[L, 128, D], ffnn=[L, 128, D], lam=[L, 128, 256], subln=[L, 128, 128],
    w2aug=[L, 32, 256], glan=[L, 128, 512], cdw=[L, 128, 4, 31], cdb=[L, 128, 4], clnw=[L, 128, 4], clnb=[L, 128, 4],
    scw=[L, 128, 8, 4], scb=[L, 128, 8], dtb=[L, 128, 8], alog=[L, 128, 8], ssdd=[L, 128, 512], ssdn=[L, 128, 512],
    wgb=[L, 16, 4, 128, 20, 128], bgate=[L, 128, 4, 16], wout=[L, 4, 128, 16, 512], wup=[L, 22, 128, 16, 512],
    fcw=[L, 128, 88, 3], fcb=[L, 128, 88], wdn=[L, 4, 128, NPAIR, 512],
)


def build(dbg_outs=(), nlayers=L, stop_after=None, skip=()):
    nc = bass.Bass("TRN2", target_bir_lowering=False)
    dr = {}
    x_in = nc.dram_tensor("x", [T, D], F32, kind="ExternalInput").ap()
    for name, shape in PARAM_SHAPES.items():
        dr[name] = nc.dram_tensor(name, shape, F32, kind="ExternalInput").ap()
    y_out = nc.dram_tensor("y", [T, D], F32, kind="ExternalOutput").ap()
    for name, (shape, dt_) in SCRATCH.items():
        kind = "ExternalOutput" if name in dbg_outs else "Internal"
        dr[name] = nc.dram_tensor("s_" + name, shape, dt_, kind=kind).ap()
    kb = KB(nc)
    cst = make_consts(kb)
    for e in ('pe', 'act', 'dve', 'sp'):
        kb.wait(e, cst['ev'])
    hT = kb.sb("hT", [128, KC, T], BF16)
    for l in range(nlayers):
        last = (l == nlayers - 1)
        lam_init = 0.8 - 0.6 * math.exp(-0.3 * l)
        x_src = x_in if l == 0 else dr['xc']
        x_dst = y_out if last else dr['xc']
        phase_norm(kb, x_src, dr['mixn'][l], hT, cst['ident'], cst['eps'])
        if stop_after == 'norm':
            break
        phase_inproj(kb, l, hT, dr, cst)
        if stop_after == 'inproj':
            break
        if 'A' not in skip:
            mixer_attn(kb, l, dr, cst, lam_init)
        if 'B' not in skip:
            mixer_gla(kb, l, dr, cst)
        if 'C' not in skip:
            mixer_conv(kb, l, dr, cst)
        if 'D' not in skip:
            mixer_ssd(kb, l, dr, cst)
        if stop_after == 'mix':
            break
        phase_gates(kb, l, hT, dr, cst)
        phase_outproj(kb, l, x_src, dr['xm'], dr, cst)
        if stop_after == 'out':
            break
        phase_norm(kb, dr['xm'], dr['ffnn'][l], hT, cst['ident'], cst['eps'])
        phase_ffn_up(kb, l, hT, dr, cst)
        phase_ffn_down(kb, l, dr['xm'], x_dst, dr, cst)
    if 'hT' in dbg_outs:
        hT_d = nc.dram_tensor("s_hT", [128, KC, T], BF16, kind="ExternalOutput").ap()
        k = kb.newsem("dbg")
        ev = kb.dma('sp', hT_d[:, :, :], hT[:], k)
        kb.wait('sp', ev)
    kb.es.close()
    return nc


_NC_CACHE = {}


def kernel(**inputs):
    inputs = {k: np.asarray(v) for k, v in inputs.items()}
    P = prep_host(inputs)
    if 'nc' not in _NC_CACHE:
        _NC_CACHE['nc'] = build()
    nc = _NC_CACHE['nc']
    x = np.ascontiguousarray(inputs['x'], dtype=np.float32)
    n_cores = 4
    in_maps = []
    for c in range(n_cores):
        m = dict(P)
        m['x'] = x[c]
        in_maps.append(m)
    res = run_bass_kernel_spmd(nc, in_maps, core_ids=list(range(n_cores)))
    out = np.stack([np.asarray(res.results[c]['y'], dtype=np.float32) for c in range(n_cores)], axis=0)
    return out
```

```python
import math
import numpy as np
from contextlib import ExitStack
import concourse.bass as bass
import concourse.mybir as mybir
from concourse.bass_utils import run_bass_kernel_spmd

F32 = mybir.dt.float32
BF16 = mybir.dt.bfloat16
AF = mybir.ActivationFunctionType
ALU = mybir.AluOpType
AX = mybir.AxisListType

T = 2048
D = 2048
KC = 16
L = 4
EPS = 1e-6
DFF = 5632
NPAIR = 44
NEG = -30000.0

OFF = dict(aq=0, ak=512, av=1024, bq=1536, bk=1792, bv=2048, bo=2560, bl=3072,
           ca=3088, cg=3600, dz=4112, dx=4624, dt=5648)


def _win_groups():
    r = lambda a, n: list(range(a, a + n))
    g = []
    g.append(r(OFF['aq'], 512))
    g.append(r(OFF['ak'], 512))
    g.append(r(OFF['bq'], 256) + r(OFF['bk'], 256))
    g.append(r(OFF['ca'], 128) + r(OFF['cg'], 128) + r(OFF['ca'] + 128, 128) + r(OFF['cg'] + 128, 128))
    g.append(r(OFF['ca'] + 256, 128) + r(OFF['cg'] + 256, 128) + r(OFF['ca'] + 384, 128) + r(OFF['cg'] + 384, 128))
    g.append(r(OFF['dx'], 512))
    g.append(r(OFF['dx'] + 512, 512))
    g.append(r(OFF['bl'], 16))
    g.append(r(OFF['av'], 512))
    g.append(r(OFF['bv'], 512))
    g.append(r(OFF['bo'], 512))
    g.append(r(OFF['dz'], 512))
    g.append(r(OFF['bk'], 256) + r(OFF['dt'], 8))
    return g


WIN_GROUPS = _win_groups()
WIN_NCOLS = [len(g) for g in WIN_GROUPS]


class KB:
    def __init__(self, nc):
        self.nc = nc
        self.es = ExitStack()
        self.E = dict(pe=nc.tensor, act=nc.scalar, dve=nc.vector, pool=nc.gpsimd, sp=nc.sync)
        self.psem = {}
        self.pcnt = {}
        for e in self.E:
            self.psem[e] = self.es.enter_context(nc.semaphore("p_" + e))
            self.pcnt[e] = 0
        self.waited = {}
        self.nsem = 0
        self.nname = 0

    def newsem(self, name):
        if getattr(self, 'freelist', None):
            key = self.freelist.pop()
            self.live.append(key)
            return key
        if not hasattr(self, 'live'):
            self.live = []
            self.freelist = []
        self.nsem += 1
        key = f"d:{name}_{self.nsem}"
        self.psem[key] = self.es.enter_context(self.nc.semaphore(f"{name}_{self.nsem}"))
        self.pcnt[key] = 0
        self.live.append(key)
        return key

    def sb(self, name, shape, dt, es=None):
        self.nname += 1
        return (es or self.es).enter_context(self.nc.sbuf_tensor(f"{name}_{self.nname}", shape, dt))

    def ps(self, name, shape, dt, es=None):
        self.nname += 1
        return (es or self.es).enter_context(self.nc.psum_tensor(f"{name}_{self.nname}", shape, dt))

    def mark(self, e, ins):
        ins.then_inc(self.psem[e], 1)
        self.pcnt[e] += 1
        return (e, self.pcnt[e])

    def wait(self, e, *evs):
        for ev in evs:
            if ev is None:
                continue
            if isinstance(ev, list):
                self.wait(e, *ev)
                continue
            key, val = ev
            if self.waited.get((e, key), 0) >= val:
                continue
            self.E[e].wait_ge(self.psem[key], val)
            self.waited[(e, key)] = val

    def dma(self, q, out, in_, key, **kw):
        ins = self.E[q].dma_start(out=out, in_=in_, **kw)
        ins.then_inc(self.psem[key], 16)
        self.pcnt[key] += 16
        return (key, self.pcnt[key])

    def barrier(self, extra=()):
        engines = ('pe', 'act', 'dve', 'sp')
        evs = [(e, self.pcnt[e]) for e in engines if self.pcnt[e] > 0] + [e for e in extra if e is not None]
        for e in engines:
            self.wait(e, *evs)
        self.last_barrier = evs
        if hasattr(self, 'live'):
            self.freelist.extend(self.live)
            self.live = []


class Ring:
    def __init__(self, bufs):
        self.bufs = bufs
        self.free = [None] * len(bufs)
        self.i = 0

    def get(self):
        i = self.i
        self.i = (self.i + 1) % len(self.bufs)
        return i, self.bufs[i], self.free[i]

    def release(self, i, ev):
        self.free[i] = ev


class WStream:
    def __init__(self, kb, slots, loads):
        self.kb = kb
        self.slots = slots
        self.n = len(slots)
        self.loads = loads
        self.keys = [kb.newsem("w") for _ in slots]
        self.free = [None] * self.n
        self.ev = {}
        kb.wait('pool', getattr(kb, 'last_barrier', None))
        for i in range(min(self.n, len(loads))):
            self._issue(i)

    def _issue(self, i):
        s = i % self.n
        self.kb.wait('pool', self.free[s])
        dram, view = self.loads[i]
        self.ev[i] = self.kb.dma('pool', view(self.slots[s]), dram, self.keys[s])

    def get(self, i):
        return self.slots[i % self.n], self.ev[i]

    def release(self, i, ev):
        self.free[i % self.n] = ev
        if i + self.n < len(self.loads):
            self._issue(i + self.n)


class Stager:
    def __init__(self, kb, bufs, q='sp'):
        self.kb = kb
        self.ring = Ring(bufs)
        self.keys = [kb.newsem("st") for _ in bufs]
        self.q = q
        self.last = []

    def get(self):
        return self.ring.get()

    def store(self, i, dst, src, ev):
        self.kb.wait(self.q, ev)
        e = self.kb.dma(self.q, dst, src, self.keys[i])
        self.ring.release(i, e)
        self.last.append(e)
        self.last = self.last[-len(self.keys):]
        return e


class Loader:
    def __init__(self, kb, bufs, q='sp'):
        self.kb = kb
        self.ring = Ring(bufs)
        self.keys = [kb.newsem("ld") for _ in bufs]
        self.q = q

    def load(self, fn):
        i, buf, free = self.ring.get()
        self.kb.wait(self.q, free)
        ev = None
        for o, s in fn(buf):
            ev = self.kb.dma(self.q, o, s, self.keys[i])
        return i, buf, ev

    def release(self, i, ev):
        self.ring.release(i, ev)


def phase_norm(kb, x_src, wb_dram, hT, ident, cst_eps):
    nc = kb.nc
    with ExitStack() as es:
        wb = kb.sb("nw", [128, D], F32, es)
        kw = kb.newsem("nw")
        ev_w = kb.dma('sp', wb[:], wb_dram, kw)
        ld = Loader(kb, [kb.sb("nx", [128, D], F32, es) for _ in range(2)])
        xn = Ring([kb.sb("nxn", [128, D], BF16, es) for _ in range(2)])
        junk = kb.sb("njunk", [128, D], BF16, es)
        ss = kb.sb("nss", [128, 16], F32, es)
        rstd = kb.sb("nrstd", [128, 16], F32, es)
        pst = Ring([kb.ps("npt", [128, 16, 128], BF16, es) for _ in range(2)])
        kb.wait('dve', ev_w)
        for tt in range(T // 128):
            li, xt, ev_x = ld.load(lambda b: [(b[:], x_src[tt * 128:(tt + 1) * 128, :])])
            kb.wait('act', ev_x)
            ev_sq = kb.mark('act', nc.scalar.activation(out=junk[:], in_=xt[:], func=AF.Square,
                                                         accum_out=ss[:, tt:tt + 1]))
            kb.wait('act', ev_sq)
            ev_sq = kb.mark('act', nc.scalar.activation(out=rstd[:, tt:tt + 1], in_=ss[:, tt:tt + 1], func=AF.Sqrt,
                                                         scale=1.0 / D, bias=cst_eps[:]))
            kb.wait('dve', ev_sq, ev_x)
            ev_rc = kb.mark('dve', nc.vector.reciprocal(out=rstd[:, tt:tt + 1], in_=rstd[:, tt:tt + 1]))
            kb.wait('dve', ev_rc)
            xi, xnb, xfree = xn.get()
            kb.wait('dve', xfree)
            ev_xn = kb.mark('dve', nc.vector.scalar_tensor_tensor(out=xnb[:], in0=xt[:], scalar=rstd[:, tt:tt + 1],
                                                                  in1=wb[:], op0=ALU.mult, op1=ALU.mult))
            ld.release(li, ev_xn)
            pi, pt, pfree = pst.get()
            kb.wait('pe', ev_xn, pfree)
            for c in range(KC):
                ins = nc.tensor.transpose(out=pt[:, c, :], in_=xnb[:, c * 128:(c + 1) * 128], identity=ident[:])
            ev_t = kb.mark('pe', ins)
            xn.release(xi, ev_t)
            e = 'act' if tt % 2 == 0 else 'dve'
            kb.wait(e, ev_t)
            if e == 'act':
                ins = nc.scalar.copy(out=hT[:, :, tt * 128:(tt + 1) * 128], in_=pt[:])
            else:
                ins = nc.vector.tensor_copy(out=hT[:, :, tt * 128:(tt + 1) * 128], in_=pt[:])
            pst.release(pi, kb.mark(e, ins))
        kb.barrier()


def phase_inproj(kb, l, hT, dr, cst):
    nc = kb.nc
    with ExitStack() as es:
        wsl = [kb.sb("w", [128, 16, 512], BF16, es) for _ in range(2)]
        loads = [(dr['win'][l, g, :, :, 0:WIN_NCOLS[g]], (lambda s, n=WIN_NCOLS[g]: s[:, :, 0:n])) for g in range(13)]
        ws = WStream(kb, wsl, loads)
        banks = Ring([kb.ps("b", [128, 512], F32, es) for _ in range(6)])
        sbank = Ring([kb.ps("sb", [128, 512], F32, es) for _ in range(2)])
        stf = Stager(kb, [kb.sb("stf", [128, 512], F32, es) for _ in range(3)])
        stb = Stager(kb, [kb.sb("stb", [128, 512], BF16, es) for _ in range(3)])
        sqr = Ring([kb.sb("sq", [128, 512], BF16, es) for _ in range(3)])
        rr = Ring([kb.sb("rr", [128, 512], F32, es) for _ in range(2)])
        sig = Ring([kb.sb("sig", [128, 512], F32, es) for _ in range(2)])
        qkg = kb.sb("qkg", [128, 2], F32, es)
        kq = kb.newsem("qkg")
        ev_g = kb.dma('sp', qkg[:], dr['qkg'][l], kq)
        kb.wait('dve', ev_g)
        ev_qkg = kb.mark('dve', nc.vector.tensor_scalar(out=qkg[:, 0:1], in0=qkg[:, 0:1], scalar1=0.125, scalar2=None, op0=ALU.mult))
        kb.wait('dve', ev_qkg)
        cnt = [0]

        def cp_eng():
            cnt[0] += 1
            return 'act' if cnt[0] % 2 == 0 else 'dve'

        def copy_out(e, out, in_):
            if e == 'act':
                return nc.scalar.copy(out=out, in_=in_)
            return nc.vector.tensor_copy(out=out, in_=in_)

        def mm_fm(wt, cc, tt, bank, m=128):
            for k in range(KC):
                ins = nc.tensor.matmul(bank[0:m, :], lhsT=wt[:, k, cc * 128:cc * 128 + m],
                                       rhs=hT[:, k, tt * 512:(tt + 1) * 512], start=(k == 0), stop=(k == KC - 1))
            return kb.mark('pe', ins)

        pending = []

        def flush_pending():
            while pending:
                pending.pop(0)()

        for g in (0, 1):
            wt, ev = ws.get(g)
            kb.wait('pe', ev)
            dst = dr['qT'] if g == 0 else dr['kT']
            for cc in range(4):
                for tt in range(4):
                    bi, bank, bfree = banks.get()
                    kb.wait('pe', bfree)
                    ev_mm = mm_fm(wt, cc, tt, bank)
                    flush_pending()
                    si, sq, sfree = sqr.get()
                    kb.wait('act', ev_mm, sfree)
                    ev_sq = kb.mark('act', nc.scalar.activation(out=sq[:], in_=bank[:], func=AF.Square))

                    def stats(bi=bi, bank=bank, si=si, sq=sq, ev_sq=ev_sq, cc=cc, tt=tt, g=g, dst=dst):
                        pi, pb, pfree = sbank.get()
                        kb.wait('pe', ev_sq, pfree)
                        ev_st = kb.mark('pe', nc.tensor.matmul(pb[:], lhsT=cst['onesblk'][:], rhs=sq[:], start=True, stop=True))
                        sqr.release(si, ev_st)
                        ri, r, rfree = rr.get()
                        kb.wait('dve', ev_st, rfree)
                        kb.wait('act', ev_st, rfree)
                        ev_r = kb.mark('act', nc.scalar.activation(out=r[:], in_=pb[:], func=AF.Sqrt, scale=1.0 / 64, bias=cst['eps'][:]))
                        kb.wait('dve', ev_r)
                        nc.vector.reciprocal(out=r[:], in_=r[:])
                        oi, ob, ofree = stb.get()
                        kb.wait('dve', ofree)
                        ev_o = kb.mark('dve', nc.vector.scalar_tensor_tensor(out=ob[:], in0=bank[:], scalar=qkg[:, g:g + 1],
                                                                           in1=r[:], op0=ALU.mult, op1=ALU.mult))
                        sbank.release(pi, ev_o)
                        rr.release(ri, ev_o)
                        banks.release(bi, ev_o)
                        stb.store(oi, dst[cc, :, tt * 512:(tt + 1) * 512], ob[:], ev_o)
                    pending.append(stats)
            flush_pending()
            ws.release(g, ev_mm)

        wt, ev = ws.get(2)
        kb.wait('pe', ev)
        for cc in range(4):
            for tt in range(4):
                bi, bank, bfree = banks.get()
                kb.wait('pe', bfree)
                ev_mm = mm_fm(wt, cc, tt, bank)
                e = cp_eng()
                oi, ob, ofree = stf.get()
                kb.wait(e, ev_mm, ofree)
                ev_o = kb.mark(e, copy_out(e, ob[:], bank[:]))
                banks.release(bi, ev_o)
                stf.store(oi, dr['gqkT'][cc, :, tt * 512:(tt + 1) * 512], ob[:], ev_o)
        ws.release(2, ev_mm)

        for g in (3, 4):
            wt, ev = ws.get(g)
            kb.wait('pe', ev)
            for pr in range(2):
                for tt in range(4):
                    bia, banka, bfree = banks.get()
                    kb.wait('pe', bfree)
                    ev_a = mm_fm(wt, 2 * pr, tt, banka)
                    big, bankg, bfree = banks.get()
                    kb.wait('pe', bfree)
                    ev_gm = mm_fm(wt, 2 * pr + 1, tt, bankg)
                    gi, sg, gfree = sig.get()
                    kb.wait('act', ev_gm, gfree)
                    ev_s = kb.mark('act', nc.scalar.activation(out=sg[:], in_=bankg[:], func=AF.Sigmoid))
                    banks.release(big, ev_s)
                    oi, ob, ofree = stf.get()
                    kb.wait('dve', ev_s, ev_a, ofree)
                    ev_o = kb.mark('dve', nc.vector.tensor_tensor(out=ob[:], in0=banka[:], in1=sg[:], op=ALU.mult))
                    banks.release(bia, ev_o)
                    sig.release(gi, ev_o)
                    c = (g - 3) * 2 + pr
                    stf.store(oi, dr['gluT'][c, :, tt * 512:(tt + 1) * 512], ob[:], ev_o)
            ws.release(g, ev_gm)

        for g in (5, 6):
            wt, ev = ws.get(g)
            kb.wait('pe', ev)
            for cc in range(4):
                for tt in range(4):
                    bi, bank, bfree = banks.get()
                    kb.wait('pe', bfree)
                    ev_mm = mm_fm(wt, cc, tt, bank)
                    e = cp_eng()
                    oi, ob, ofree = stf.get()
                    kb.wait(e, ev_mm, ofree)
                    ev_o = kb.mark(e, copy_out(e, ob[:], bank[:]))
                    banks.release(bi, ev_o)
                    stf.store(oi, dr['xbcT'][(g - 5) * 4 + cc, :, tt * 512:(tt + 1) * 512], ob[:], ev_o)
            ws.release(g, ev_mm)

        wt, ev = ws.get(7)
        kb.wait('pe', ev)
        for tt in range(4):
            bi, bank, bfree = banks.get()
            kb.wait('pe', bfree)
            ev_mm = mm_fm(wt, 0, tt, bank, m=16)
            e = cp_eng()
            oi, ob, ofree = stf.get()
            kb.wait(e, ev_mm, ofree)
            ev_o = kb.mark(e, copy_out(e, ob[0:16, :], bank[0:16, :]))
            banks.release(bi, ev_o)
            stf.store(oi, dr['glowT'][:, tt * 512:(tt + 1) * 512], ob[0:16, :], ev_o)
        ws.release(7, ev_mm)

        tm_dst = {8: ('v', BF16), 9: ('gv', BF16), 10: ('gout', F32), 11: ('z', F32)}
        for g in range(8, 13):
            wt, ev = ws.get(g)
            kb.wait('pe', ev)
            n = WIN_NCOLS[g]
            for tt in range(T // 128):
                bi, bank, bfree = banks.get()
                kb.wait('pe', bfree)
                for k in range(KC):
                    ins = nc.tensor.matmul(bank[:, 0:n], lhsT=hT[:, k, tt * 128:(tt + 1) * 128], rhs=wt[:, k, 0:n],
                                           start=(k == 0), stop=(k == KC - 1))
                ev_mm = kb.mark('pe', ins)
                e = cp_eng()
                if g == 12:
                    oi, ob, ofree = stf.get()
                    kb.wait(e, ev_mm, ofree)
                    ev_o = kb.mark(e, copy_out(e, ob[:, 0:n], bank[:, 0:n]))
                    banks.release(bi, ev_o)
                    kb.wait('sp', ev_o)
                    kb.dma('sp', dr['gk'][tt * 128:(tt + 1) * 128, :], ob[:, 0:256], stf.keys[oi])
                    stf.store(oi, dr['dt'][tt * 128:(tt + 1) * 128, :], ob[:, 256:264], ev_o)
                else:
                    name, dt_ = tm_dst[g]
                    st = stb if dt_ == BF16 else stf
                    oi, ob, ofree = st.get()
                    kb.wait(e, ev_mm, ofree)
                    ev_o = kb.mark(e, copy_out(e, ob[:], bank[:]))
                    banks.release(bi, ev_o)
                    st.store(oi, dr[name][tt * 128:(tt + 1) * 128, :], ob[:], ev_o)
            ws.release(g, ev_mm)
        kb.barrier(extra=stf.last + stb.last)


def mixer_conv(kb, l, dr, cst):
    nc = kb.nc
    with ExitStack() as es:
        cw = kb.sb("cw", [128, 4, 31], F32, es)
        cb = kb.sb("cb", [128, 4], F32, es)
        lw = kb.sb("lw", [128, 4], F32, es)
        lb = kb.sb("lb", [128, 4], F32, es)
        kp = kb.newsem("cp")
        kb.dma('sp', cw[:], dr['cdw'][l], kp)
        kb.dma('sp', cb[:], dr['cdb'][l], kp)
        kb.dma('sp', lw[:], dr['clnw'][l], kp)
        ev_p = kb.dma('sp', lb[:], dr['clnb'][l], kp)
        glu = [kb.sb("glu", [128, 30 + T], F32, es) for _ in range(4)]
        acc = [kb.sb("acc", [128, T], F32, es) for _ in range(4)]
        kg = kb.newsem("cg")
        ev_acc = []
        for c in range(4):
            e = 'dve'
            eng = kb.E[e]
            ev_z = kb.mark(e, eng.memset(glu[c][:, 0:30], 0.0))
            ev_l = kb.dma('sp', glu[c][:, 30:30 + T], dr['gluT'][c], kg)
            kb.wait(e, ev_l, ev_p, ev_z)
            eng.tensor_scalar(out=acc[c][:], in0=glu[c][:, 0:T], scalar1=cw[:, c, 0:1], scalar2=cb[:, c:c + 1],
                              op0=ALU.mult, op1=ALU.add)
            for j in range(1, 31):
                ins = eng.scalar_tensor_tensor(out=acc[c][:], in0=glu[c][:, j:j + T], scalar=cw[:, c, j:j + 1],
                                               in1=acc[c][:], op0=ALU.mult, op1=ALU.add)
            ev_acc.append(kb.mark(e, ins))
        ps1 = kb.ps("cs1", [128, 512], F32, es)
        ps2 = kb.ps("cs2", [128, 512], F32, es)
        sq = Ring([kb.sb("csq", [128, 512], F32, es) for _ in range(2)])
        mean = kb.sb("cmean", [128, 512], F32, es)
        msq = kb.sb("cmsq", [128, 512], F32, es)
        rstd = kb.sb("crstd", [128, 512], F32, es)
        tmp = Ring([kb.sb("ctmp", [128, 512], F32, es) for _ in range(2)])
        st = Stager(kb, [kb.sb("cst", [128, 512], BF16, es) for _ in range(2)])
        ev_prev = None
        for tt in range(4):
            sl = slice(tt * 512, (tt + 1) * 512)
            kb.wait('pe', ev_prev)
            for c in range(4):
                kb.wait('pe', ev_acc[c])
                nc.tensor.matmul(ps1[:], lhsT=cst['ones_f'][:], rhs=acc[c][:, sl], start=(c == 0), stop=(c == 3))
            ev_s1 = None
            for c in range(4):
                si, sb_, sfree = sq.get()
                kb.wait('act', ev_acc[c], sfree)
                ev_q = kb.mark('act', nc.scalar.activation(out=sb_[:], in_=acc[c][:, sl], func=AF.Square))
                kb.wait('pe', ev_q)
                ev_m = kb.mark('pe', nc.tensor.matmul(ps2[:], lhsT=cst['ones_f'][:], rhs=sb_[:], start=(c == 0), stop=(c == 3)))
                sq.release(si, ev_m)
            kb.wait('dve', ev_m)
            nc.vector.tensor_scalar(out=mean[:], in0=ps1[:], scalar1=1.0 / 512, scalar2=None, op0=ALU.mult)
            nc.vector.tensor_tensor(out=msq[:], in0=mean[:], in1=mean[:], op=ALU.mult)
            ev_v = kb.mark('dve', nc.vector.scalar_tensor_tensor(out=rstd[:], in0=ps2[:], scalar=1.0 / 512, in1=msq[:],
                                                                op0=ALU.mult, op1=ALU.subtract))
            kb.wait('act', ev_v)
            ev_sd = kb.mark('act', nc.scalar.activation(out=rstd[:], in_=rstd[:], func=AF.Sqrt, bias=cst['eps'][:]))
            kb.wait('dve', ev_sd)
            nc.vector.reciprocal(out=rstd[:], in_=rstd[:])
            for c in range(4):
                ti, tb, tfree = tmp.get()
                kb.wait('dve', tfree)
                nc.vector.tensor_tensor(out=tb[:], in0=acc[c][:, sl], in1=mean[:], op=ALU.subtract)
                ev_t = kb.mark('dve', nc.vector.tensor_tensor(out=tb[:], in0=tb[:], in1=rstd[:], op=ALU.mult))
                oi, ob, ofree = st.get()
                kb.wait('act', ev_t, ofree, ev_p)
                ev_y = kb.mark('act', nc.scalar.activation(out=ob[:], in_=tb[:], func=AF.Silu, scale=lw[:, c:c + 1], bias=lb[:, c:c + 1]))
                tmp.release(ti, ev_y)
                st.store(oi, dr['yT'][8 + c, :, sl], ob[:], ev_y)
            ev_prev = ev_t
        kb.barrier(extra=st.last + [('pool', kb.pcnt['pool'])])


def phase_gates(kb, l, hT, dr, cst):
    nc = kb.nc
    with ExitStack() as es:
        yT = kb.sb("yT", [128, 16, T], BF16, es)
        ky = kb.newsem("yT")
        for c in range(16):
            ev_y = kb.dma('sp', yT[:, c, :], dr['yT'][c], ky)
        bg = kb.sb("bg", [128, 4, 16], F32, es)
        ev_b = kb.dma('sp', bg[:], dr['bgate'][l], ky)
        wsl = [kb.sb("wg", [128, 20, 128], BF16, es) for _ in range(2)]
        loads = [(dr['wgb'][l, cc, i], (lambda s: s[:])) for cc in range(16) for i in range(4)]
        ws = WStream(kb, wsl, loads)
        gb = Ring([kb.ps("gb", [128, 512], F32, es) for _ in range(4)])
        bb = Ring([kb.ps("bb", [128, 512], F32, es) for _ in range(4)])
        sig = Ring([kb.sb("gsig", [128, 512], F32, es) for _ in range(2)])
        macc = Ring([kb.sb("macc", [128, T], F32, es) for _ in range(2)])
        tmp = kb.sb("gtmp", [128, 512], F32, es)
        st = Stager(kb, [kb.sb("gst", [128, T], BF16, es) for _ in range(2)])
        kb.wait('pe', ev_y)
        kb.wait('act', ev_b)
        n = 0
        for cc in range(16):
            mi, mb, mfree = macc.get()
            for i in range(4):
                wt, ev = ws.get(n)
                kb.wait('pe', ev)
                for tt in range(4):
                    sl = slice(tt * 512, (tt + 1) * 512)
                    gi, gbank, gfree = gb.get()
                    kb.wait('pe', gfree)
                    for k in range(KC):
                        ins = nc.tensor.matmul(gbank[:], lhsT=wt[:, k, :], rhs=hT[:, k, sl], start=(k == 0), stop=(k == KC - 1))
                    ev_g = kb.mark('pe', ins)
                    bi, bbank, bfree = bb.get()
                    kb.wait('pe', bfree)
                    for k in range(4):
                        ins = nc.tensor.matmul(bbank[:], lhsT=wt[:, 16 + k, :], rhs=yT[:, 4 * i + k, sl], start=(k == 0), stop=(k == 3))
                    ev_br = kb.mark('pe', ins)
                    si, sg, sfree = sig.get()
                    kb.wait('act', ev_g, sfree)
                    ev_s = kb.mark('act', nc.scalar.activation(out=sg[:], in_=gbank[:], func=AF.Sigmoid, bias=bg[:, i, cc:cc + 1]))
                    gb.release(gi, ev_s)
                    kb.wait('dve', ev_s, ev_br)
                    if i == 0:
                        kb.wait('dve', mfree)
                        ev_m = kb.mark('dve', nc.vector.tensor_tensor(out=mb[:, sl], in0=bbank[:], in1=sg[:], op=ALU.mult))
                    else:
                        nc.vector.tensor_tensor(out=tmp[:], in0=bbank[:], in1=sg[:], op=ALU.mult)
                        ev_m = kb.mark('dve', nc.vector.tensor_tensor(out=mb[:, sl], in0=mb[:, sl], in1=tmp[:], op=ALU.add))
                    bb.release(bi, ev_m)
                    sig.release(si, ev_m)
                ws.release(n, ev_br)
                n += 1
            oi, ob, ofree = st.get()
            kb.wait('act', ev_m, ofree)
            ev_c = kb.mark('act', nc.scalar.copy(out=ob[:], in_=mb[:]))
            macc.release(mi, ev_c)
            st.store(oi, dr['mT'][cc], ob[:], ev_c)
        kb.barrier(extra=st.last)


def phase_outproj(kb, l, x_src, x_dst, dr, cst):
    nc = kb.nc
    with ExitStack() as es:
        mT = kb.sb("mT", [128, 16, T], BF16, es)
        km = kb.newsem("mT")
        for c in range(16):
            ev_m = kb.dma('sp', mT[:, c, :], dr['mT'][c], km)
        wsl = [kb.sb("wo", [128, 16, 512], BF16, es) for _ in range(2)]
        loads = [(dr['wout'][l, g], (lambda s: s[:])) for g in range(4)]
        ws = WStream(kb, wsl, loads)
        banks = Ring([kb.ps("ob", [128, 512], F32, es) for _ in range(4)])
        ld = Loader(kb, [kb.sb("ox", [128, 512], F32, es) for _ in range(3)])
        st = Stager(kb, [kb.sb("oo", [128, 512], F32, es) for _ in range(3)])
        kb.wait('pe', ev_m)
        for g in range(4):
            wt, ev = ws.get(g)
            kb.wait('pe', ev)
            for tt in range(16):
                rows = slice(tt * 128, (tt + 1) * 128)
                cols = slice(g * 512, (g + 1) * 512)
                li, xb, ev_x = ld.load(lambda b: [(b[:], x_src[rows, cols])])
                bi, bank, bfree = banks.get()
                kb.wait('pe', bfree)
                for k in range(KC):
                    ins = nc.tensor.matmul(bank[:], lhsT=mT[:, k, rows], rhs=wt[:, k, :], start=(k == 0), stop=(k == KC - 1))
                ev_mm = kb.mark('pe', ins)
                oi, ob, ofree = st.get()
                kb.wait('dve', ev_mm, ev_x, ofree)
                ev_o = kb.mark('dve', nc.vector.tensor_tensor(out=ob[:], in0=bank[:], in1=xb[:], op=ALU.add))
                banks.release(bi, ev_o)
                ld.release(li, ev_o)
                st.store(oi, x_dst[rows, cols], ob[:], ev_o)
            ws.release(g, ev_mm)
        kb.barrier(extra=st.last)


def phase_ffn_up(kb, l, hT, dr, cst):
    nc = kb.nc
    with ExitStack() as es:
        fw = kb.sb("fw", [128, 88, 3], F32, es)
        fb = kb.sb("fb", [128, 88], F32, es)
        kf = kb.newsem("fp")
        kb.dma('sp', fw[:], dr['fcw'][l], kf)
        ev_p = kb.dma('sp', fb[:], dr['fcb'][l], kf)
        wsl = [kb.sb("wu", [128, 16, 512], BF16, es) for _ in range(2)]
        loads = [(dr['wup'][l, g], (lambda s: s[:])) for g in range(22)]
        ws = WStream(kb, wsl, loads)
        banks = Ring([kb.ps("ub", [128, 512], F32, es) for _ in range(8)])
        ubuf = Ring([kb.sb("uu", [128, 2 + T], F32, es) for _ in range(4)])
        accg = kb.sb("accg", [128, T], F32, es)
        accv = kb.sb("accv", [128, T], F32, es)
        st = Stager(kb, [kb.sb("ast", [128, 2, T], BF16, es) for _ in range(2)])
        for i in range(4):
            ev_z = kb.mark('dve', nc.vector.memset(ubuf.bufs[i][:, 0:2], 0.0))
        kb.wait('act', ev_z)
        kb.wait('dve', ev_p)
        sti = None
        for g in range(22):
            wt, ev = ws.get(g)
            kb.wait('pe', ev)
            for pr in range(2):
                j = 2 * g + pr
                us = []
                for half in range(2):
                    cc = 2 * pr + half
                    ui, ub, ufree = ubuf.get()
                    evs = []
                    for tt in range(4):
                        bi, bank, bfree = banks.get()
                        kb.wait('pe', bfree)
                        for k in range(KC):
                            ins = nc.tensor.matmul(bank[:], lhsT=wt[:, k, cc * 128:(cc + 1) * 128], rhs=hT[:, k, tt * 512:(tt + 1) * 512],
                                                   start=(k == 0), stop=(k == KC - 1))
                        ev_mm = kb.mark('pe', ins)
                        kb.wait('act', ev_mm, ufree)
                        ev_c = kb.mark('act', nc.scalar.copy(out=ub[:, 2 + tt * 512:2 + (tt + 1) * 512], in_=bank[:]))
                        banks.release(bi, ev_c)
                    us.append((ui, ub, ev_c))
                outs = []
                for half, accb in ((0, accg), (1, accv)):
                    ui, ub, ev_c = us[half]
                    q = 2 * j + half
                    kb.wait('dve', ev_c)
                    nc.vector.tensor_scalar(out=accb[:], in0=ub[:, 0:T], scalar1=fw[:, q, 0:1], scalar2=fb[:, q:q + 1],
                                            op0=ALU.mult, op1=ALU.add)
                    nc.vector.scalar_tensor_tensor(out=accb[:], in0=ub[:, 1:1 + T], scalar=fw[:, q, 1:2], in1=accb[:],
                                                   op0=ALU.mult, op1=ALU.add)
                    ev_a = kb.mark('dve', nc.vector.scalar_tensor_tensor(out=accb[:], in0=ub[:, 2:2 + T], scalar=fw[:, q, 2:3],
                                                                        in1=accb[:], op0=ALU.mult, op1=ALU.add))
                    outs.append(ev_a)
                ui_g, ub_g, _ = us[0]
                kb.wait('act', outs[0])
                ev_s = kb.mark('act', nc.scalar.activation(out=ub_g[:, 2:2 + T], in_=accg[:], func=AF.Silu))
                if pr == 0:
                    sti, sob, sofree = st.get()
                kb.wait('dve', ev_s, sofree)
                ev_o = kb.mark('dve', nc.vector.tensor_tensor(out=sob[:, pr, :], in0=ub_g[:, 2:2 + T], in1=accv[:], op=ALU.mult))
                ubuf.release(us[0][0], ev_o)
                ubuf.release(us[1][0], outs[1])
            kb.wait('sp', ev_o)
            j0 = 2 * g
            for t8 in range(8):
                e_st = kb.dma('sp', dr['aT'][t8, :, j0:j0 + 2, :], sob[:, :, t8 * 256:(t8 + 1) * 256], st.keys[sti])
            st.ring.release(sti, e_st)
            st.last.append(e_st)
            ws.release(g, ev_mm)
        kb.barrier(extra=st.last[-2:])


def phase_ffn_down(kb, l, x_src, x_dst, dr, cst):
    nc = kb.nc
    with ExitStack() as es:
        wsl = [kb.sb("wd", [128, 44, 512], BF16, es) for _ in range(2)]
        loads = [(dr['wdn'][l, g], (lambda s: s[:])) for g in range(4)]
        ws = WStream(kb, wsl, loads)
        banks = Ring([kb.ps("db", [128, 512], F32, es) for _ in range(4)])
        la = Loader(kb, [kb.sb("da", [128, 44, 256], BF16, es) for _ in range(2)])
        ld = Loader(kb, [kb.sb("dx", [128, 512], F32, es) for _ in range(3)])
        st = Stager(kb, [kb.sb("do", [128, 512], F32, es) for _ in range(3)])
        for g in range(4):
            wt, ev = ws.get(g)
            kb.wait('pe', ev)
            cols = slice(g * 512, (g + 1) * 512)
            for t8 in range(8):
                ai, ab, ev_a = la.load(lambda b: [(b[:], dr['aT'][t8])])
                kb.wait('pe', ev_a)
                for h2 in range(2):
                    rows = slice(t8 * 256 + h2 * 128, t8 * 256 + (h2 + 1) * 128)
                    li, xb, ev_x = ld.load(lambda b: [(b[:], x_src[rows, cols])])
                    bi, bank, bfree = banks.get()
                    kb.wait('pe', bfree)
                    for k in range(NPAIR):
                        ins = nc.tensor.matmul(bank[:], lhsT=ab[:, k, h2 * 128:(h2 + 1) * 128], rhs=wt[:, k, :],
                                               start=(k == 0), stop=(k == NPAIR - 1))
                    ev_mm = kb.mark('pe', ins)
                    oi, ob, ofree = st.get()
                    kb.wait('dve', ev_mm, ev_x, ofree)
                    ev_o = kb.mark('dve', nc.vector.tensor_tensor(out=ob[:], in0=bank[:], in1=xb[:], op=ALU.add))
                    banks.release(bi, ev_o)
                    ld.release(li, ev_o)
                    st.store(oi, x_dst[rows, cols], ob[:], ev_o)
                la.release(ai, ev_mm)
            ws.release(g, ev_mm)
        kb.barrier(extra=st.last)


def mixer_attn(kb, l, dr, cst, lam_init):
    nc = kb.nc
    with ExitStack() as es:
        lam = kb.sb("lam", [128, 256], F32, es)
        sub = kb.sb("subln", [128, 128], F32, es)
        kp = kb.newsem("ap")
        kb.dma('sp', lam[:], dr['lam'][l], kp)
        ev_p = kb.dma('sp', sub[:], dr['subln'][l], kp)
        prod = kb.sb("aprod", [128, 2, 64], F32, es)
        s2 = kb.sb("as2", [128, 2], F32, es)
        nl = kb.sb("anl", [128, 1], F32, es)
        kb.wait('dve', ev_p)
        nc.vector.tensor_tensor(out=prod[:, 0, :], in0=lam[:, 0:64], in1=lam[:, 64:128], op=ALU.mult)
        nc.vector.tensor_tensor(out=prod[:, 1, :], in0=lam[:, 128:192], in1=lam[:, 192:256], op=ALU.mult)
        ev = kb.mark('dve', nc.vector.reduce_sum(out=s2[:], in_=prod[:], axis=AX.X))
        kb.wait('act', ev)
        ev = kb.mark('act', nc.scalar.activation(out=s2[:], in_=s2[:], func=AF.Exp))
        kb.wait('dve', ev)
        ev = kb.mark('dve', nc.vector.tensor_tensor(out=nl[:], in0=s2[:, 1:2], in1=s2[:, 0:1], op=ALU.subtract))
        kb.wait('dve', ev)
        nc.vector.tensor_scalar(out=nl[:], in0=nl[:], scalar1=-lam_init, scalar2=None, op0=ALU.add)
        ev_nl = kb.mark('dve', nc.vector.tensor_scalar(out=sub[:], in0=sub[:], scalar1=1.0 - lam_init, scalar2=None, op0=ALU.mult))
        kb.wait('dve', ev_nl)

        qk = Loader(kb, [kb.sb("aqk", [128, 2, T], BF16, es) for _ in range(2)])
        va = Loader(kb, [kb.sb("ava", [128, 16, 129], BF16, es) for _ in range(2)])
        for b in va.ring.bufs:
            ev_one = kb.mark('dve', nc.vector.memset(b[:, :, 128:129], 1.0))
        kb.wait('pe', ev_one)
        sbank = Ring([kb.ps("asb", [128, 512], F32, es) for _ in range(3)])
        obank = Ring([kb.ps("aob", [128, 2, 129], F32, es) for _ in range(2)])
        tbank = Ring([kb.ps("atb", [128, 128], BF16, es) for _ in range(2)])
        pT = Ring([kb.sb("apT", [128, 512], BF16, es) for _ in range(3)])
        rc = Ring([kb.sb("arc", [128, 4], F32, es) for _ in range(2)])
        t1 = kb.sb("at1", [128, 128], F32, es)
        ob = Ring([kb.sb("ao", [128, 128], F32, es) for _ in range(2)])
        junk = kb.sb("ajunk", [128, 128], F32, es)
        yb = Ring([kb.sb("ay", [128, 128], BF16, es) for _ in range(2)])
        st = Stager(kb, [kb.sb("ayT", [128, T], BF16, es) for _ in range(2)])
        vview = dr['v'].rearrange("(j p) c -> p j c", p=128)
        groups = [(i, m, jg) for i in range(16) for m in range(2) for jg in range(0, i + 1, 4)]
        for h in range(4):
            qi, qkb, ev_q = qk.load(lambda b: [(b[:, 0, :], dr['qT'][h]), (b[:, 1, :], dr['kT'][h])])
            vi, vb, ev_v = va.load(lambda b: [(b[:, :, 0:128], vview[:, :, h * 128:(h + 1) * 128])])
            kb.wait('pe', ev_q, ev_v)
            sti, yTh, yfree = st.get()
            cur_ob = {}
            last = {}

            def emit_scores(i, m, jg):
                rows = slice(m * 64, (m + 1) * 64)
                je = min(jg + 4, i + 1)
                w = (je - jg) * 128
                si, sbk, sfree = sbank.get()
                kb.wait('pe', sfree)
                for jj in range(jg, je):
                    ins = nc.tensor.matmul(sbk[:, (jj - jg) * 128:(jj - jg + 1) * 128], lhsT=qkb[rows, 1, jj * 128:(jj + 1) * 128],
                                           rhs=qkb[rows, 0, i * 128:(i + 1) * 128], start=True, stop=True)
                ev_s = kb.mark('pe', ins)
                pi, pb, pfree = pT.get()
                kb.wait('act', ev_s, pfree)
                ev_e = kb.mark('act', nc.scalar.activation(out=pb[:, 0:w], in_=sbk[:, 0:w], func=AF.Exp))
                sbank.release(si, ev_e)
                if je == i + 1:
                    off = (i - jg) * 128
                    kb.wait('dve', ev_e)
                    ev_e = kb.mark('dve', nc.vector.memset(pb[64:128, off:off + 64], 0.0))
                return (i, m, jg, je, pi, pb, ev_e)

            def emit_tail(i, oi, obk, ev_pv):
                ri, rcb, rfree = rc.get()
                kb.wait('dve', ev_pv, rfree)
                ev = kb.mark('dve', nc.vector.reciprocal(out=rcb[:, 0:2], in_=obk[:, :, 128]))
                kb.wait('dve', ev)
                ev = kb.mark('dve', nc.vector.tensor_tensor(out=rcb[:, 1:2], in0=rcb[:, 1:2], in1=nl[:], op=ALU.mult))
                kb.wait('dve', ev)
                nc.vector.tensor_scalar(out=t1[:], in0=obk[:, 0, 0:128], scalar1=rcb[:, 0:1], scalar2=None, op0=ALU.mult)
                bi, obuf, bfree = ob.get()
                kb.wait('dve', bfree)
                ev_o = kb.mark('dve', nc.vector.scalar_tensor_tensor(out=obuf[:], in0=obk[:, 1, 0:128], scalar=rcb[:, 1:2], in1=t1[:],
                                                                    op0=ALU.mult, op1=ALU.add))
                obank.release(oi, ev_o)
                kb.wait('act', ev_o)
                ev = kb.mark('act', nc.scalar.activation(out=junk[:], in_=obuf[:], func=AF.Square, accum_out=rcb[:, 2:3]))
                kb.wait('act', ev)
                ev = kb.mark('act', nc.scalar.activation(out=rcb[:, 3:4], in_=rcb[:, 2:3], func=AF.Sqrt, scale=1.0 / 128, bias=cst['eps'][:]))
                kb.wait('dve', ev)
                ev = kb.mark('dve', nc.vector.reciprocal(out=rcb[:, 3:4], in_=rcb[:, 3:4]))
                kb.wait('dve', ev)
                yi, ybuf, yfree2 = yb.get()
                kb.wait('dve', yfree2)
                ev_y = kb.mark('dve', nc.vector.scalar_tensor_tensor(out=ybuf[:], in0=obuf[:], scalar=rcb[:, 3:4], in1=sub[:],
                                                                    op0=ALU.mult, op1=ALU.mult))
                ob.release(bi, ev_y)
                rc.release(ri, ev_y)

                def tr():
                    ti, tb, tfree = tbank.get()
                    kb.wait('pe', ev_y, tfree)
                    ev_t = kb.mark('pe', nc.tensor.transpose(out=tb[:], in_=ybuf[:], identity=cst['ident'][:]))
                    yb.release(yi, ev_t)
                    kb.wait('act', ev_t, yfree)
                    ev_c = kb.mark('act', nc.scalar.copy(out=yTh[:, i * 128:(i + 1) * 128], in_=tb[:]))
                    tbank.release(ti, ev_c)
                    last['ev_c'] = ev_c
                return tr

            trs = []

            def emit_pv(stt):
                i, m, jg, je, pi, pb, ev_e = stt
                if m == 0 and jg == 0:
                    oi, obk, ofree = obank.get()
                    kb.wait('pe', ofree)
                    cur_ob['v'] = (oi, obk)
                oi, obk = cur_ob['v']
                kb.wait('pe', ev_e)
                for jj in range(jg, je):
                    ins = nc.tensor.matmul(obk[:, m, :], lhsT=pb[:, (jj - jg) * 128:(jj - jg + 1) * 128], rhs=vb[:, jj, :],
                                           start=(jj == 0), stop=(jj == i))
                ev_pv = kb.mark('pe', ins)
                pT.release(pi, ev_pv)
                last['ev_pv'] = ev_pv
                if m == 1 and je == i + 1:
                    trs.append(emit_tail(i, oi, obk, ev_pv))

            prev = None
            for gidx, (i, m, jg) in enumerate(groups):
                cur = emit_scores(i, m, jg)
                ready = trs[:]
                del trs[:]
                if prev is not None:
                    emit_pv(prev)
                for t_ in ready:
                    t_()
                prev = cur
            emit_pv(prev)
            for t_ in trs:
                t_()
            del trs[:]
            qk.release(qi, last['ev_pv'])
            va.release(vi, last['ev_pv'])
            st.store(sti, dr['yT'][h], yTh[:], last['ev_c'])
        kb.barrier(extra=st.last)


def mixer_gla(kb, l, dr, cst):
    nc = kb.nc
    with ExitStack() as es:
        kp = kb.newsem("bp")
        w2f = kb.sb("w2f", [32, 256], F32, es)
        w2a = kb.sb("w2a", [32, 256], BF16, es)
        gn = kb.sb("gn", [64, 512], F32, es)
        glf = kb.sb("glf", [32, T], F32, es)
        glb = kb.sb("glb", [32, T], BF16, es)
        ev_m = kb.mark('dve', nc.vector.memset(glf[:], 1.0))
        kb.wait('sp', ev_m)
        kb.dma('sp', w2f[:], dr['w2aug'][l], kp)
        kb.dma('sp', gn[:], dr['glan'][l, 0:64, :], kp)
        ev_p = kb.dma('sp', glf[0:16, :], dr['glowT'][:, :], kp)
        kb.wait('dve', ev_p)
        nc.vector.tensor_copy(out=w2a[:], in_=w2f[:])
        ev_gl = kb.mark('dve', nc.vector.tensor_copy(out=glb[:], in_=glf[:]))
        g_all = kb.sb("g_all", [64, 32, 256], BF16, es)
        qtT = [kb.sb("qtT", [128, T], BF16, es) for _ in range(2)]
        ktT = [kb.sb("ktT", [128, T], BF16, es) for _ in range(2)]
        elast = kb.sb("elast", [128, 2, 32], F32, es)
        S = [kb.sb("S", [128, 256], F32, es) for _ in range(2)]
        Sbf = [kb.sb("Sbf", [128, 33, 128], BF16, es) for _ in range(2)]
        yTst = kb.sb("byT", [128, 4, T], BF16, es)
        for hp in range(2):
            nc.vector.memset(S[hp][:], 0.0)
            ev_z = kb.mark('dve', nc.vector.memset(Sbf[hp][:, 0, :], 0.0))
        banks = Ring([kb.ps("bb", [128, 512], F32, es) for _ in range(6)])
        tbank = Ring([kb.ps("btb", [128, 4, 64], BF16, es) for _ in range(2)])
        f1 = Ring([kb.sb("bf1", [128, 512], F32, es) for _ in range(3)])
        f2 = Ring([kb.sb("bf2", [128, 512], F32, es) for _ in range(3)])
        kb.wait('pe', ev_gl)
        for c in range(32):
            bi, bank, bfree = banks.get()
            kb.wait('pe', bfree)
            ev_mm = kb.mark('pe', nc.tensor.matmul(bank[0:64, 0:256], lhsT=glb[:, c * 64:(c + 1) * 64], rhs=w2a[:], start=True, stop=True))
            fi, fb, ffree = f1.get()
            kb.wait('act', ev_mm, ffree)
            nc.scalar.activation(out=fb[0:64, 0:256], in_=bank[0:64, 0:256], func=AF.Exp, scale=-1.0)
            ev_a = kb.mark('act', nc.scalar.activation(out=fb[0:64, 0:256], in_=fb[0:64, 0:256], func=AF.Ln, bias=cst['one'][0:64, :]))
            banks.release(bi, ev_a)
            kb.wait('dve', ev_a)
            ev_g = kb.mark('dve', nc.vector.tensor_scalar(out=g_all[:, c, :], in0=fb[0:64, 0:256], scalar1=-1.0 / 16, scalar2=None, op0=ALU.mult))
            f1.release(fi, ev_g)
        import os
        if os.environ.get('BSTAGE') == '1':
            kb.barrier()
            return
        ldq = Loader(kb, [kb.sb("bldq", [128, 2, 512], F32, es) for _ in range(2)])
        kb.wait('pe', ev_g)
        for t4 in range(4):
            sl = slice(t4 * 512, (t4 + 1) * 512)
            for hp in range(2):
                li, qb, ev_q = ldq.load(lambda b: [(b[:, 0, :], dr['gqkT'][hp, :, sl]), (b[:, 1, :], dr['gqkT'][2 + hp, :, sl])])
                bi, bank, bfree = banks.get()
                kb.wait('pe', bfree)
                for cj in range(8):
                    c = t4 * 8 + cj
                    ins = nc.tensor.matmul(bank[:, cj * 64:(cj + 1) * 64], lhsT=g_all[:, c, hp * 128:(hp + 1) * 128], rhs=cst['tri_b'][:],
                                           start=True, stop=True)
                ev_mm = kb.mark('pe', ins)
                ai, eq, afree = f1.get()
                ci, ek, cfree = f2.get()
                kb.wait('act', ev_mm, afree, cfree)
                nc.scalar.activation(out=eq[:], in_=bank[:], func=AF.Exp)
                ev_e = kb.mark('act', nc.scalar.activation(out=ek[:], in_=bank[:], func=AF.Exp, scale=-1.0))
                banks.release(bi, ev_e)
                kb.wait('dve', ev_e, ev_q)
                nc.vector.scalar_tensor_tensor(out=qtT[hp][:, sl], in0=qb[:, 0, :], scalar=0.125, in1=eq[:], op0=ALU.mult, op1=ALU.mult)
                nc.vector.tensor_tensor(out=ktT[hp][:, sl], in0=qb[:, 1, :], in1=ek[:], op=ALU.mult)
                ev_d = kb.mark('dve', nc.vector.tensor_copy(out=elast[:, hp, t4 * 8:(t4 + 1) * 8], in_=eq[:, 63:512:64]))
                f1.release(ai, ev_d)
                f2.release(ci, ev_d)
                ldq.release(li, ev_d)
        kb.wait('pe', ev_d)
        kb.wait('dve', ev_d)
        if os.environ.get('BSTAGE') == '2':
            kb.barrier()
            return
        ldk = Loader(kb, [kb.sb("bldk", [64, 256], F32, es) for _ in range(2)])
        ldv = Loader(kb, [kb.sb("bldv", [64, 512], BF16, es) for _ in range(2)])
        ldo = Loader(kb, [kb.sb("bldo", [64, 512], F32, es) for _ in range(2)])
        khat = Ring([kb.sb("khat", [64, 256], BF16, es) for _ in range(2)])
        am = Ring([kb.sb("bam", [64, 4, 64], BF16, es) for _ in range(2)])
        sm = Ring([kb.sb("bsm", [64, 8], F32, es) for _ in range(2)])
        yb = Ring([kb.sb("bby", [64, 512], BF16, es) for _ in range(2)])
        bjunk = kb.sb("bjunk", [64, 128], F32, es)
        ev_state = [ev_z, ev_z]
        for c in range(32):
            rows = slice(c * 64, (c + 1) * 64)
            ki, kbuf, ev_k = ldk.load(lambda b: [(b[:], dr['gk'][rows, :])])
            vi, vbuf, ev_v = ldv.load(lambda b: [(b[:], dr['gv'][rows, :])])
            oi, obuf, ev_o = ldo.load(lambda b: [(b[:], dr['gout'][rows, :])])
            bi, bank, bfree = banks.get()
            kb.wait('pe', bfree)
            ev_mm = kb.mark('pe', nc.tensor.matmul(bank[0:64, 0:256], lhsT=cst['slow_b'][:], rhs=g_all[:, c, :], start=True, stop=True))
            fi, fb, ffree = f1.get()
            kb.wait('act', ev_mm, ffree)
            ev_e = kb.mark('act', nc.scalar.activation(out=fb[0:64, 0:256], in_=bank[0:64, 0:256], func=AF.Exp))
            banks.release(bi, ev_e)
            hi, kh, hfree = khat.get()
            kb.wait('dve', ev_e, ev_k, hfree)
            ev_kh = kb.mark('dve', nc.vector.tensor_tensor(out=kh[:], in0=kbuf[:], in1=fb[0:64, 0:256], op=ALU.mult))
            f1.release(fi, ev_kh)
            ldk.release(ki, ev_kh)
            sb_ = []
            for par in range(2):
                bi, sbk, bfree = banks.get()
                kb.wait('pe', bfree)
                r_ = slice(par * 64, (par + 1) * 64)
                for hh in range(2):
                    ins = nc.tensor.matmul(sbk[0:64, hh * 64:(hh + 1) * 64], lhsT=ktT[hh][r_, rows], rhs=qtT[hh][r_, rows], start=True, stop=True)
                sb_.append((bi, sbk))
            ev_s = kb.mark('pe', ins)
            ai, amb, afree = am.get()
            kb.wait('dve', ev_s, afree)
            for par in range(2):
                ev_am = kb.mark('dve', nc.vector.tensor_tensor(out=amb[:, par::2, :], in0=sb_[par][1][0:64, 0:128].rearrange("p (h t) -> p h t", h=2),
                                                                in1=cst['tri4'][:, 0:2, :], op=ALU.mult))
            banks.release(sb_[0][0], ev_am)
            banks.release(sb_[1][0], ev_am)
            bo, obk, bfree = banks.get()
            kb.wait('pe', bfree, ev_am, ev_v, ev_state[0], ev_state[1])
            for h in range(4):
                ins = nc.tensor.matmul(obk[0:64, h * 128:(h + 1) * 128], lhsT=amb[:, h, :], rhs=vbuf[:, h * 128:(h + 1) * 128], start=True, stop=True)
            o2 = []
            for par in range(2):
                b2, ob2, bfree = banks.get()
                kb.wait('pe', bfree)
                r_ = slice(par * 64, (par + 1) * 64)
                for hh in range(2):
                    ins = nc.tensor.matmul(ob2[0:64, hh * 128:(hh + 1) * 128], lhsT=qtT[hh][r_, rows], rhs=Sbf[hh][r_, c, :], start=True, stop=True)
                o2.append((b2, ob2))
            ev_ob = kb.mark('pe', ins)
            am.release(ai, ev_ob)
            kb.wait('pe', ev_kh)
            for hp in range(2):
                bk, kvb, bfree = banks.get()
                kb.wait('pe', bfree)
                ev_kv = kb.mark('pe', nc.tensor.matmul(kvb[:, 0:256], lhsT=kh[:, hp * 128:(hp + 1) * 128], rhs=vbuf[:, hp * 256:(hp + 1) * 256],
                                                       start=True, stop=True))
                kb.wait('dve', ev_kv)
                ev_S = kb.mark('dve', nc.vector.scalar_tensor_tensor(out=S[hp][:], in0=S[hp][:], scalar=elast[:, hp, c:c + 1], in1=kvb[:, 0:256],
                                                                    op0=ALU.mult, op1=ALU.add))
                banks.release(bk, ev_S)
                kb.wait('act', ev_S)
                nc.scalar.copy(out=Sbf[hp][0:64, c + 1, :], in_=S[hp][0:64, 0:128])
                ev_state[hp] = kb.mark('act', nc.scalar.copy(out=Sbf[hp][64:128, c + 1, :], in_=S[hp][64:128, 128:256]))
                kb.wait('dve', ev_state[hp])
            khat.release(hi, ev_kv)
            ldv.release(vi, ev_kv)
            fi, sq, ffree = f1.get()
            kb.wait('act', ev_ob, ffree)
            for par in range(2):
                ev_q = kb.mark('act', nc.scalar.copy(out=sq[0:64, :].rearrange("p (h d) -> p h d", h=4)[:, par::2, :],
                                                      in_=o2[par][1][0:64, 0:256].rearrange("p (h d) -> p h d", h=2)))
            banks.release(o2[0][0], ev_q)
            banks.release(o2[1][0], ev_q)
            kb.wait('dve', ev_q)
            ev_os = kb.mark('dve', nc.vector.tensor_tensor(out=sq[0:64, :], in0=obk[0:64, :], in1=sq[0:64, :], op=ALU.add))
            banks.release(bo, ev_os)
            si, smb, sfree = sm.get()
            kb.wait('act', ev_os, sfree)
            for h in range(4):
                ev = kb.mark('act', nc.scalar.activation(out=bjunk[:], in_=sq[0:64, h * 128:(h + 1) * 128], func=AF.Square, accum_out=smb[:, h:h + 1]))
            kb.wait('act', ev)
            ev = kb.mark('act', nc.scalar.activation(out=smb[:, 4:8], in_=smb[:, 0:4], func=AF.Sqrt, scale=1.0 / 128, bias=cst['eps'][0:64, :]))
            gi, sg, gfree = f2.get()
            kb.wait('act', ev_o, gfree)
            ev_sg = kb.mark('act', nc.scalar.activation(out=sg[0:64, :], in_=obuf[:], func=AF.Silu))
            ldo.release(oi, ev_sg)
            kb.wait('dve', ev)
            ev = kb.mark('dve', nc.vector.reciprocal(out=smb[:, 4:8], in_=smb[:, 4:8]))
            kb.wait('dve', ev)
            for h in range(4):
                nc.vector.tensor_scalar(out=sq[0:64, h * 128:(h + 1) * 128], in0=sq[0:64, h * 128:(h + 1) * 128], scalar1=smb[:, 4 + h:5 + h],
                                        scalar2=None, op0=ALU.mult)
            kb.wait('dve', ev_sg)
            nc.vector.tensor_tensor(out=sq[0:64, :], in0=sq[0:64, :], in1=sg[0:64, :], op=ALU.mult)
            yi, ybuf, yfree = yb.get()
            kb.wait('dve', yfree)
            ev_y = kb.mark('dve', nc.vector.tensor_tensor(out=ybuf[:], in0=sq[0:64, :], in1=gn[:], op=ALU.mult))
            f1.release(fi, ev_y)
            f2.release(gi, ev_y)
            sm.release(si, ev_y)
            ti, tb, tfree = tbank.get()
            kb.wait('pe', ev_y, tfree)
            for h in range(4):
                ins = nc.tensor.transpose(out=tb[:, h, :], in_=ybuf[:, h * 128:(h + 1) * 128], identity=cst['ident'][0:64, 0:64])
            ev_t = kb.mark('pe', ins)
            yb.release(yi, ev_t)
            kb.wait('act', ev_t)
            ev_c = kb.mark('act', nc.scalar.copy(out=yTst[:, :, rows], in_=tb[:]))
            tbank.release(ti, ev_c)
        kb.wait('sp', ev_c)
        ks = kb.newsem("bst")
        evs = [kb.dma('sp', dr['yT'][4 + h], yTst[:, h, :], ks) for h in range(4)]
        kb.barrier(extra=evs[-1:])


def mixer_ssd(kb, l, dr, cst):
    nc = kb.nc
    with ExitStack() as es:
        kp = kb.newsem("dp")
        scw = kb.sb("scw", [128, 8, 4], F32, es)
        scb = kb.sb("scb", [128, 8], F32, es)
        dtb = kb.sb("dtb", [64, 8], F32, es)
        aneg = kb.sb("aneg", [64, 8], F32, es)
        dsk = kb.sb("dsk", [64, 512], F32, es)
        nw = kb.sb("dnw", [64, 512], F32, es)
        kb.dma('sp', scw[:], dr['scw'][l], kp)
        kb.dma('sp', scb[:], dr['scb'][l], kp)
        kb.dma('sp', dtb[:], dr['dtb'][l, 0:64, :], kp)
        kb.dma('sp', aneg[:], dr['alog'][l, 0:64, :], kp)
        kb.dma('sp', dsk[:], dr['ssdd'][l, 0:64, :], kp)
        ev_p = kb.dma('sp', nw[:], dr['ssdn'][l, 0:64, :], kp)
        kb.wait('act', ev_p)
        ev = kb.mark('act', nc.scalar.activation(out=aneg[:], in_=aneg[:], func=AF.Exp))
        kb.wait('dve', ev, ev_p)
        ev_an = kb.mark('dve', nc.vector.tensor_scalar(out=aneg[:], in0=aneg[:], scalar1=-1.0, scalar2=None, op0=ALU.mult))
        kb.wait('pool', ev_p)
        xc = [kb.sb("xc", [128, T], BF16, es) for _ in range(8)]
        cbuf = [kb.sb("dcb", [128, 3 + T], F32, es) for _ in range(2)]
        cacc = [kb.sb("dca", [128, T], F32, es) for _ in range(2)]
        kc = [kb.newsem("dc"), kb.newsem("dc")]
        cfree = [None, None]
        afree = [None, None]
        ev_x = None
        for ch in range(8):
            p = ch % 2
            e = 'dve'
            eng = kb.E[e]
            kb.wait(e, cfree[p])
            ev_z = kb.mark(e, eng.memset(cbuf[p][:, 0:3], 0.0))
            kb.wait('sp', cfree[p])
            ev_l = kb.dma('sp', cbuf[p][:, 3:3 + T], dr['xbcT'][ch], kc[p])
            kb.wait(e, ev_l, ev_z, afree[p])
            eng.tensor_scalar(out=cacc[p][:], in0=cbuf[p][:, 0:T], scalar1=scw[:, ch, 0:1], scalar2=scb[:, ch:ch + 1], op0=ALU.mult, op1=ALU.add)
            for j in range(1, 4):
                ins = eng.scalar_tensor_tensor(out=cacc[p][:], in0=cbuf[p][:, j:j + T], scalar=scw[:, ch, j:j + 1], in1=cacc[p][:],
                                               op0=ALU.mult, op1=ALU.add)
            ev_a = kb.mark(e, ins)
            cfree[p] = ev_a
            kb.wait('act', ev_a)
            ev_x = kb.mark('act', nc.scalar.activation(out=xc[ch][:], in_=cacc[p][:], func=AF.Silu))
            afree[p] = ev_x
        kb.wait('pe', ev_x)
        H = [kb.sb("H", [128, 256], F32, es) for _ in range(2)]
        Hbf = [kb.sb("Hbf", [128, 256], BF16, es) for _ in range(2)]
        yTst = kb.sb("dyT", [128, 4, T], BF16, es)
        for g in range(2):
            nc.vector.memset(H[g][:], 0.0)
            ev_h0 = kb.mark('dve', nc.vector.memset(Hbf[g][:], 0.0))
        ev_hbf = [ev_h0, ev_h0]
        banks = Ring([kb.ps("db", [128, 512], F32, es) for _ in range(6)])
        tbank = Ring([kb.ps("dtb", [128, 1024], BF16, es) for _ in range(2)])
        ldt = Loader(kb, [kb.sb("dldt", [64, 8], F32, es) for _ in range(2)])
        ldz = Loader(kb, [kb.sb("dldz", [64, 512], F32, es) for _ in range(2)])
        xB = Ring([kb.sb("dxB", [64, 768], BF16, es) for _ in range(2)])
        sm = Ring([kb.sb("dsm", [128, 48], F32, es) for _ in range(2)])
        lab = Ring([kb.sb("dlab", [64, 8], BF16, es) for _ in range(2)])
        rseg = Ring([kb.sb("drs", [64, 8, 64], BF16, es) for _ in range(2)])
        Bw = Ring([kb.sb("dBw", [64, 8, 128], BF16, es) for _ in range(2)])
        Eb = Ring([kb.sb("dE", [64, 512], F32, es) for _ in range(2)])
        cbs = Ring([kb.sb("dcbs", [64, 128], F32, es) for _ in range(2)])
        MT = Ring([kb.sb("dMT", [64, 8, 64], BF16, es) for _ in range(2)])
        ydsb = Ring([kb.sb("dyd", [64, 512], F32, es) for _ in range(2)])
        ybuf = Ring([kb.sb("dy", [64, 512], F32, es) for _ in range(2)])
        szb = Ring([kb.sb("dsz", [64, 512], F32, es) for _ in range(2)])
        yo16 = Ring([kb.sb("dyo", [64, 512], BF16, es) for _ in range(2)])
        junk = kb.sb("djunk", [64, 256], F32, es)
        kb.wait('dve', ev_an)
        for c in range(32):
            rows = slice(c * 64, (c + 1) * 64)
            di, dtr, ev_d = ldt.load(lambda b: [(b[:], dr['dt'][rows, :])])
            zi, zb, ev_zl = ldz.load(lambda b: [(b[:], dr['z'][rows, :])])
            ti, tb, tfree = tbank.get()
            kb.wait('pe', tfree)
            for k in range(6):
                ins = nc.tensor.transpose(out=tb[0:64, k * 128:(k + 1) * 128], in_=xc[k][:, rows], identity=cst['ident'][:])
            ev_t = kb.mark('pe', ins)
            xi, xb, xfree = xB.get()
            kb.wait('act', ev_t, xfree)
            ev_xb = kb.mark('act', nc.scalar.copy(out=xb[:], in_=tb[0:64, 0:768]))
            tbank.release(ti, ev_xb)
            si, s_, sfree = sm.get()
            kb.wait('dve', ev_d, sfree)
            ev = kb.mark('dve', nc.vector.tensor_tensor(out=s_[0:64, 0:8], in0=dtr[:], in1=dtb[:], op=ALU.add))
            ldt.release(di, ev)
            kb.wait('act', ev)
            ev = kb.mark('act', nc.scalar.activation(out=s_[0:64, 0:8], in_=s_[0:64, 0:8], func=AF.Exp))
            kb.wait('act', ev)
            ev = kb.mark('act', nc.scalar.activation(out=s_[0:64, 0:8], in_=s_[0:64, 0:8], func=AF.Ln, bias=cst['one'][0:64, :]))
            kb.wait('dve', ev)
            ev = kb.mark('dve', nc.vector.tensor_tensor(out=s_[0:64, 8:16], in0=s_[0:64, 0:8], in1=aneg[:], op=ALU.mult))
            kb.wait('dve', ev)
            li, lb_, lfree = lab.get()
            kb.wait('dve', lfree)
            ev_la = kb.mark('dve', nc.vector.tensor_copy(out=lb_[:], in_=s_[0:64, 8:16]))
            ri, rs_, rfree = rseg.get()
            kb.wait('act', ev_la, rfree)
            for e8 in range(8):
                ins = nc.scalar.mul(out=rs_[:, e8, :], in_=cst['tri_f'][:], mul=s_[0:64, 8 + e8:9 + e8])
            ev_rs = kb.mark('act', ins)
            bi, cb_, bfree = banks.get()
            kb.wait('pe', ev_la, bfree)
            nc.tensor.matmul(cb_[0:64, 0:8], lhsT=cst['tri_b'][:], rhs=lb_[:], start=True, stop=True)
            nc.tensor.matmul(cb_[0:64, 8:16], lhsT=cst['slow_b'][:], rhs=lb_[:], start=True, stop=True)
            ev_cm = kb.mark('pe', nc.tensor.matmul(cb_[:, 16:24], lhsT=cst['ones_b'][0:64, :], rhs=lb_[:], start=True, stop=True))
            lab.release(li, ev_cm)
            kb.wait('act', ev_cm)
            nc.scalar.activation(out=s_[0:64, 16:32], in_=cb_[0:64, 0:16], func=AF.Exp)
            ev_ex = kb.mark('act', nc.scalar.activation(out=s_[:, 32:40], in_=cb_[:, 16:24], func=AF.Exp))
            banks.release(bi, ev_ex)
            kb.wait('dve', ev_ex)
            ev_w = kb.mark('dve', nc.vector.tensor_tensor(out=s_[0:64, 40:48], in0=s_[0:64, 24:32], in1=s_[0:64, 0:8], op=ALU.mult))
            wi, bw, wfree = Bw.get()
            kb.wait('act', ev_w, ev_xb, wfree)
            for e8 in range(8):
                g = e8 // 4
                ins = nc.scalar.mul(out=bw[:, e8, :], in_=xb[:, 512 + g * 128:512 + (g + 1) * 128], mul=s_[0:64, 40 + e8:41 + e8])
            ev_bw = kb.mark('act', ins)
            b1, segb, bfree = banks.get()
            kb.wait('pe', ev_rs, bfree)
            for g in range(2):
                nc.tensor.matmul(segb[0:64, g * 256:(g + 1) * 256], lhsT=cst['slow_b'][:], rhs=rs_[:, g * 4:(g + 1) * 4, :], start=True, stop=False)
                ins = nc.tensor.matmul(segb[0:64, g * 256:(g + 1) * 256], lhsT=cst['ident'][0:64, 0:64], rhs=cst['negm4'][:], start=False, stop=True)
            ev_sg = kb.mark('pe', ins)
            rseg.release(ri, ev_sg)
            b2, cbb, bfree = banks.get()
            kb.wait('pe', bfree)
            for g in range(2):
                ins = nc.tensor.matmul(cbb[0:64, g * 64:(g + 1) * 64], lhsT=xc[4 + g][:, rows], rhs=xc[6 + g][:, rows], start=True, stop=True)
            ev_cb = kb.mark('pe', ins)
            ei, E, efree = Eb.get()
            ci, cs_, cfree2 = cbs.get()
            kb.wait('act', ev_sg, ev_cb, efree, cfree2)
            nc.scalar.activation(out=E[:], in_=segb[0:64, :], func=AF.Exp)
            ev_E = kb.mark('act', nc.scalar.copy(out=cs_[:], in_=cbb[0:64, 0:128]))
            banks.release(b1, ev_E)
            banks.release(b2, ev_E)
            mi, mt, mfree = MT.get()
            kb.wait('dve', ev_E, mfree)
            for e8 in range(8):
                g = e8 // 4
                ins = nc.vector.scalar_tensor_tensor(out=mt[:, e8, :], in0=E[:, e8 * 64:(e8 + 1) * 64], scalar=s_[0:64, e8:e8 + 1],
                                                     in1=cs_[:, g * 64:(g + 1) * 64], op0=ALU.mult, op1=ALU.mult)
            ev_mt = kb.mark('dve', ins)
            Eb.release(ei, ev_mt)
            cbs.release(ci, ev_mt)
            b3, ydb, bfree = banks.get()
            kb.wait('pe', ev_mt, ev_xb, bfree)
            for e8 in range(8):
                ins = nc.tensor.matmul(ydb[0:64, e8 * 64:(e8 + 1) * 64], lhsT=mt[:, e8, :], rhs=xb[:, e8 * 64:(e8 + 1) * 64], start=True, stop=True)
            ev_yd = kb.mark('pe', ins)
            MT.release(mi, ev_yd)
            b4, yob, bfree = banks.get()
            kb.wait('pe', bfree, ev_hbf[0], ev_hbf[1])
            for g in range(2):
                ins = nc.tensor.matmul(yob[0:64, g * 256:(g + 1) * 256], lhsT=xc[6 + g][:, rows], rhs=Hbf[g][:], start=True, stop=True)
            ev_yo = kb.mark('pe', ins)
            b5, csb, bfree = banks.get()
            kb.wait('pe', ev_bw, bfree)
            for e8 in range(8):
                ins = nc.tensor.matmul(csb[:, e8 * 64:(e8 + 1) * 64], lhsT=bw[:, e8, :], rhs=xb[:, e8 * 64:(e8 + 1) * 64], start=True, stop=True)
            ev_cs = kb.mark('pe', ins)
            Bw.release(wi, ev_cs)
            kb.wait('dve', ev_cs)
            for e8 in range(8):
                g = e8 // 4
                cs2 = slice((e8 % 4) * 64, (e8 % 4 + 1) * 64)
                ins = nc.vector.scalar_tensor_tensor(out=H[g][:, cs2], in0=H[g][:, cs2], scalar=s_[:, 32 + e8:33 + e8],
                                                     in1=csb[:, e8 * 64:(e8 + 1) * 64], op0=ALU.mult, op1=ALU.add)
            ev_H = kb.mark('dve', ins)
            banks.release(b5, ev_H)
            kb.wait('act', ev_H, ev_yo)
            nc.scalar.copy(out=Hbf[0][:], in_=H[0][:])
            ev_hb = kb.mark('act', nc.scalar.copy(out=Hbf[1][:], in_=H[1][:]))
            ev_hbf = [ev_hb, ev_hb]
            kb.wait('dve', ev_hb)
            yi_, ydv, yfree = ydsb.get()
            kb.wait('act', ev_yd, yfree)
            ev_ydc = kb.mark('act', nc.scalar.copy(out=ydv[:], in_=ydb[0:64, :]))
            banks.release(b3, ev_ydc)
            yb_i, y, ybfree = ybuf.get()
            kb.wait('dve', ev_ydc, ev_yo, ybfree)
            for e8 in range(8):
                cs2 = slice(e8 * 64, (e8 + 1) * 64)
                ins = nc.vector.scalar_tensor_tensor(out=y[:, cs2], in0=yob[0:64, cs2], scalar=s_[0:64, 16 + e8:17 + e8], in1=ydv[:, cs2],
                                                     op0=ALU.mult, op1=ALU.add)
            ev_c1 = kb.mark('dve', ins)
            banks.release(b4, ev_c1)
            ydsb.release(yi_, ev_c1)
            zi2, sz, zfree = szb.get()
            kb.wait('act', ev_zl, zfree)
            ev_sz = kb.mark('act', nc.scalar.activation(out=sz[:], in_=zb[:], func=AF.Silu))
            ldz.release(zi, ev_sz)
            tmpd = ydv
            nc.vector.tensor_tensor(out=tmpd[:], in0=xb[:, 0:512], in1=dsk[:], op=ALU.mult)
            nc.vector.tensor_tensor(out=y[:], in0=y[:], in1=tmpd[:], op=ALU.add)
            kb.wait('dve', ev_sz)
            ev_y3 = kb.mark('dve', nc.vector.tensor_tensor(out=y[:], in0=y[:], in1=sz[:], op=ALU.mult))
            ydsb.release(yi_, ev_y3)
            xB.release(xi, ev_y3)
            szb.release(zi2, ev_y3)
            kb.wait('act', ev_y3)
            for g in range(2):
                ev = kb.mark('act', nc.scalar.activation(out=junk[:], in_=y[:, g * 256:(g + 1) * 256], func=AF.Square, accum_out=s_[0:64, 24 + g:25 + g]))
            kb.wait('act', ev)
            ev = kb.mark('act', nc.scalar.activation(out=s_[0:64, 26:28], in_=s_[0:64, 24:26], func=AF.Sqrt, scale=1.0 / 256, bias=cst['eps'][0:64, :]))
            kb.wait('dve', ev)
            ev = kb.mark('dve', nc.vector.reciprocal(out=s_[0:64, 26:28], in_=s_[0:64, 26:28]))
            kb.wait('dve', ev)
            oi, yo_, ofree = yo16.get()
            kb.wait('dve', ofree)
            for g in range(2):
                ins = nc.vector.scalar_tensor_tensor(out=yo_[:, g * 256:(g + 1) * 256], in0=y[:, g * 256:(g + 1) * 256], scalar=s_[0:64, 26 + g:27 + g],
                                                     in1=nw[:, g * 256:(g + 1) * 256], op0=ALU.mult, op1=ALU.mult)
            ev_yf = kb.mark('dve', ins)
            ybuf.release(yb_i, ev_yf)
            sm.release(si, ev_yf)
            ti, tb, tfree = tbank.get()
            kb.wait('pe', ev_yf, tfree)
            for h in range(4):
                ins = nc.tensor.transpose(out=tb[:, h * 64:(h + 1) * 64], in_=yo_[:, h * 128:(h + 1) * 128], identity=cst['ident'][0:64, 0:64])
            ev_t2 = kb.mark('pe', ins)
            yo16.release(oi, ev_t2)
            kb.wait('act', ev_t2)
            ev_c = kb.mark('act', nc.scalar.copy(out=yTst[:, :, rows], in_=tb[:, 0:256].rearrange("p (h t) -> p h t", h=4)))
            tbank.release(ti, ev_c)
        kb.wait('sp', ev_c)
        ks = kb.newsem("dst")
        evs = [kb.dma('sp', dr['yT'][12 + h], yTst[:, h, :], ks) for h in range(4)]
        kb.barrier(extra=evs[-1:] + [('pool', kb.pcnt['pool'])])

def make_consts(kb):
    nc = kb.nc
    c = {}
    g = nc.gpsimd
    ones_f = kb.sb("onesf", [128, 128], F32)
    g.memset(ones_f[:], 1.0)
    zeros_f = kb.sb("zerosf", [128, 256], F32)
    g.memset(zeros_f[:], 0.0)
    ident_f = kb.sb("identf", [128, 128], F32)
    g.affine_select(out=ident_f[:], in_=ones_f[:], pattern=[[-1, 128]], compare_op=ALU.is_equal, fill=0.0, base=0, channel_multiplier=1)
    ident = kb.sb("ident", [128, 128], BF16)
    g.tensor_copy(out=ident[:], in_=ident_f[:])
    ones_b = kb.sb("onesb", [128, 128], BF16)
    g.tensor_copy(out=ones_b[:], in_=ones_f[:])
    ob = kb.sb("onesblk", [128, 128], BF16)
    g.memset(ob[:], 0.0)
    g.memset(ob[0:64, 0:64], 1.0)
    g.memset(ob[64:128, 64:128], 1.0)
    eps = kb.sb("epsc", [128, 1], F32)
    g.memset(eps[:], EPS)
    one = kb.sb("onec", [128, 1], F32)
    g.memset(one[:], 1.0)
    tri_f = kb.sb("trif", [64, 64], F32)
    g.affine_select(out=tri_f[:], in_=ones_f[0:64, 0:64], pattern=[[1, 64]], compare_op=ALU.is_ge, fill=0.0, base=0, channel_multiplier=-1)
    tri_b = kb.sb("trib", [64, 64], BF16)
    g.tensor_copy(out=tri_b[:], in_=tri_f[:])
    slow_f = kb.sb("slowf", [64, 64], F32)
    g.affine_select(out=slow_f[:], in_=ones_f[0:64, 0:64], pattern=[[-1, 64]], compare_op=ALU.is_gt, fill=0.0, base=0, channel_multiplier=1)
    slow_b = kb.sb("slowb", [64, 64], BF16)
    g.tensor_copy(out=slow_b[:], in_=slow_f[:])
    tri4 = kb.sb("tri4", [64, 4, 64], F32)
    for a in range(4):
        g.tensor_copy(out=tri4[:, a, :], in_=tri_f[:])
    negf = kb.sb("negf", [64, 64], F32)
    g.affine_select(out=negf[:], in_=zeros_f[0:64, 0:64], pattern=[[1, 64]], compare_op=ALU.is_ge, fill=NEG, base=0, channel_multiplier=-1)
    negm4 = kb.sb("negm4", [64, 256], BF16)
    for a in range(4):
        ins = g.tensor_copy(out=negm4[:, a * 64:(a + 1) * 64], in_=negf[:])
    ev = kb.mark('pool', ins)
    c.update(ident=ident, ident_f=ident_f, ones_f=ones_f, ones_b=ones_b, onesblk=ob, eps=eps, one=one, tri_f=tri_f, tri_b=tri_b,
             slow_b=slow_b, tri4=tri4, negm4=negm4, ev=ev)
    return c


def prep_host(inputs):
    f = np.float32
    P = {}
    rep = lambda a, n=128: np.ascontiguousarray(np.broadcast_to(a[:, None, :], (a.shape[0], n, a.shape[-1])))
    w_in = inputs['w_in']
    win = np.zeros((L, 13, 128, 16, 512), f)
    for g, cols in enumerate(WIN_GROUPS):
        sub = w_in[:, :, cols]
        win[:, g, :, :, 0:len(cols)] = sub.reshape(L, 16, 128, len(cols)).transpose(0, 2, 1, 3)
    P['win'] = win
    q = inputs['diff_qk_norm']
    P['qkg'] = np.ascontiguousarray(np.concatenate([q, q], axis=2).transpose(0, 2, 1))
    P['mixn'] = rep(inputs['mix_norm'])
    P['ffnn'] = rep(inputs['ffn_norm'])
    P['lam'] = rep(inputs['diff_lambda'].reshape(L, 256))
    P['subln'] = rep(inputs['diff_subln'])
    w2a = np.zeros((L, 32, 256), f)
    w2a[:, 0:16] = inputs['gla_gk_w2']
    w2a[:, 16] = inputs['gla_gk_b']
    P['w2aug'] = w2a
    P['glan'] = rep(np.tile(inputs['gla_norm'], (1, 4)))
    P['cdw'] = np.ascontiguousarray(inputs['conv_dw_w'].reshape(L, 31, 4, 128).transpose(0, 3, 2, 1))
    fm = lambda a, n: np.ascontiguousarray(a.reshape(L, n, 128).transpose(0, 2, 1))
    P['cdb'] = fm(inputs['conv_dw_b'], 4)
    P['clnw'] = fm(inputs['conv_ln_w'], 4)
    P['clnb'] = fm(inputs['conv_ln_b'], 4)
    P['scw'] = np.ascontiguousarray(inputs['ssd_conv_w'].reshape(L, 4, 8, 128).transpose(0, 3, 2, 1))
    P['scb'] = fm(inputs['ssd_conv_b'], 8)
    P['dtb'] = rep(inputs['ssd_dt_bias'])
    P['alog'] = rep(inputs['ssd_a_log'])
    P['ssdd'] = rep(np.repeat(inputs['ssd_d'], 64, axis=1))
    P['ssdn'] = rep(inputs['ssd_norm'])
    wg = inputs['w_gate'].reshape(L, 4, 16, 128, 16, 128)
    wb = inputs['w_branch'].reshape(L, 4, 4, 128, 16, 128)
    wgb = np.empty((L, 16, 4, 128, 20, 128), f)
    wgb[:, :, :, :, 0:16, :] = wg.transpose(0, 4, 1, 3, 2, 5)
    wgb[:, :, :, :, 16:20, :] = wb.transpose(0, 4, 1, 3, 2, 5)
    P['wgb'] = wgb
    P['bgate'] = np.ascontiguousarray(inputs['b_gate'].reshape(L, 4, 16, 128).transpose(0, 3, 1, 2))
    P['wout'] = np.ascontiguousarray(inputs['w_out'].reshape(L, 16, 128, 4, 512).transpose(0, 3, 2, 1, 4))
    up = inputs['ffn_w_up'].reshape(L, 16, 128, 2, NPAIR, 128)
    wup = np.empty((L, 22, 128, 16, 4, 128), f)
    upr = up.transpose(0, 4, 2, 1, 3, 5)
    upr = upr.reshape(L, 22, 2, 128, 16, 2, 128)
    wup[:] = upr.transpose(0, 1, 3, 4, 2, 5, 6).reshape(L, 22, 128, 16, 4, 128)
    P['wup'] = wup.reshape(L, 22, 128, 16, 512)
    fcw = inputs['ffn_conv_w'].reshape(L, 3, 2, NPAIR, 128)
    P['fcw'] = np.ascontiguousarray(fcw.transpose(0, 4, 3, 2, 1).reshape(L, 128, 88, 3))
    fcb = inputs['ffn_conv_b'].reshape(L, 2, NPAIR, 128)
    P['fcb'] = np.ascontiguousarray(fcb.transpose(0, 3, 2, 1).reshape(L, 128, 88))
    P['wdn'] = np.ascontiguousarray(inputs['ffn_w_down'].reshape(L, NPAIR, 128, 4, 512).transpose(0, 3, 2, 1, 4))
    return P


SCRATCH = dict(
    qT=([4, 128, T], BF16), kT=([4, 128, T], BF16), v=([T, 512], BF16),
    gqkT=([4, 128, T], F32), gk=([T, 256], F32), gv=([T, 512], BF16), gout=([T, 512], F32), glowT=([16, T], F32),
    gluT=([4, 128, T], F32), xbcT=([8, 128, T], F32), z=([T, 512], F32), dt=([T, 8], F32),
    yT=([16, 128, T], BF16), mT=([16, 128, T], BF16), aT=([8, 128, NPAIR, 256], BF16),
    xm=([T, D], F32), xc=([T, D], F32),
)

PARAM_SHAPES = dict(
    win=[L, 13, 128, 16, 512], qkg=[L, 128, 2], mixn=[L, 128, D], ffnn=[L, 128, D], lam=[L, 128, 256], subln=[L, 128, 128],
    w2aug=[L, 32, 256], glan=[L, 128, 512], cdw=[L, 128, 4, 31], cdb=[L, 128, 4], clnw=[L, 128, 4], clnb=[L, 128, 4],
    scw=[L, 128, 8, 4], scb=[L, 128, 8], dtb=[L, 128, 8], alog=[L, 128, 8], ssdd=[L, 128, 512], ssdn=[L, 128, 512],
    wgb=[L, 16, 4, 128, 20, 128], bgate=[L, 128, 4, 16], wout=[L, 4, 128, 16, 512], wup=[L, 22, 128, 16, 512],
    fcw=[L, 128, 88, 3], fcb=[L, 128, 88], wdn=[L, 4, 128, NPAIR, 512],
)


def build(dbg_outs=(), nlayers=L, stop_after=None, skip=()):
    nc = bass.Bass("TRN2", target_bir_lowering=False)
    dr = {}
    x_in = nc.dram_tensor("x", [T, D], F32, kind="ExternalInput").ap()
    for name, shape in PARAM_SHAPES.items():
        dr[name] = nc.dram_tensor(name, shape, F32, kind="ExternalInput").ap()
    y_out = nc.dram_tensor("y", [T, D], F32, kind="ExternalOutput").ap()
    for name, (shape, dt_) in SCRATCH.items():
        kind = "ExternalOutput" if name in dbg_outs else "Internal"
        dr[name] = nc.dram_tensor("s_" + name, shape, dt_, kind=kind).ap()
    kb = KB(nc)
    cst = make_consts(kb)
    for e in ('pe', 'act', 'dve', 'sp'):
        kb.wait(e, cst['ev'])
    for l in range(nlayers):
        last = (l == nlayers - 1)
        lam_init = 0.8 - 0.6 * math.exp(-0.3 * l)
        x_src = x_in if l == 0 else dr['xc']
        x_dst = y_out if last else dr['xc']
        with ExitStack() as hes:
            hT = kb.sb("hT", [128, KC, T], BF16, hes)
            phase_norm(kb, x_src, dr['mixn'][l], hT, cst['ident'], cst['eps'])
            phase_inproj(kb, l, hT, dr, cst)
            if 'A' not in skip:
                mixer_attn(kb, l, dr, cst, lam_init)
            if 'B' not in skip:
                mixer_gla(kb, l, dr, cst)
            if 'C' not in skip:
                mixer_conv(kb, l, dr, cst)
            if 'D' not in skip:
                mixer_ssd(kb, l, dr, cst)
            if stop_after == 'mix':
                break
            phase_gates(kb, l, hT, dr, cst)
        phase_outproj(kb, l, x_src, dr['xm'], dr, cst)
        with ExitStack() as hes:
            hT = kb.sb("hT", [128, KC, T], BF16, hes)
            phase_norm(kb, dr['xm'], dr['ffnn'][l], hT, cst['ident'], cst['eps'])
            phase_ffn_up(kb, l, hT, dr, cst)
        phase_ffn_down(kb, l, dr['xm'], x_dst, dr, cst)
    kb.es.close()
    return nc


_NC_CACHE = {}


def kernel(**inputs):
    inputs = {k: np.asarray(v) for k, v in inputs.items()}
    P = prep_host(inputs)
    if 'nc' not in _NC_CACHE:
        _NC_CACHE['nc'] = build()
    nc = _NC_CACHE['nc']
    x = np.ascontiguousarray(inputs['x'], dtype=np.float32)
    n_cores = 4
    in_maps = []
    for c in range(n_cores):
        m = dict(P)
        m['x'] = x[c]
        in_maps.append(m)
    res = run_bass_kernel_spmd(nc, in_maps, core_ids=list(range(n_cores)))
    out = np.stack([np.asarray(res.results[c]['y'], dtype=np.float32) for c in range(n_cores)], axis=0)
    return out
```

```python
import math
import numpy as np
from contextlib import ExitStack
import concourse.bass as bass
import concourse.mybir as mybir
from concourse.bass_utils import run_bass_kernel_spmd

F32 = mybir.dt.float32
BF16 = mybir.dt.bfloat16
AF = mybir.ActivationFunctionType
ALU = mybir.AluOpType
AX = mybir.AxisListType

T = 2048
D = 2048
KC = 16
L = 4
EPS = 1e-6
DFF = 5632
NPAIR = 44
NEG = -30000.0

OFF = dict(aq=0, ak=512, av=1024, bq=1536, bk=1792, bv=2048, bo=2560, bl=3072,
           ca=3088, cg=3600, dz=4112, dx=4624, dt=5648)


def _win_groups():
    r = lambda a, n: list(range(a, a + n))
    g = []
    g.append(r(OFF['aq'], 512))
    g.append(r(OFF['ak'], 512))
    g.append(r(OFF['bq'], 256) + r(OFF['bk'], 256))
    g.append(r(OFF['ca'], 128) + r(OFF['cg'], 128) + r(OFF['ca'] + 128, 128) + r(OFF['cg'] + 128, 128))
    g.append(r(OFF['ca'] + 256, 128) + r(OFF['cg'] + 256, 128) + r(OFF['ca'] + 384, 128) + r(OFF['cg'] + 384, 128))
    g.append(r(OFF['dx'], 512))
    g.append(r(OFF['dx'] + 512, 512))
    g.append(r(OFF['bl'], 16))
    g.append(r(OFF['av'], 512))
    g.append(r(OFF['bv'], 512))
    g.append(r(OFF['bo'], 512))
    g.append(r(OFF['dz'], 512))
    g.append(r(OFF['bk'], 256) + r(OFF['dt'], 8))
    return g


WIN_GROUPS = _win_groups()
WIN_NCOLS = [len(g) for g in WIN_GROUPS]


class KB:
    def __init__(self, nc):
        self.nc = nc
        self.es = ExitStack()
        self.E = dict(pe=nc.tensor, act=nc.scalar, dve=nc.vector, pool=nc.gpsimd, sp=nc.sync)
        self.psem = {}
        self.pcnt = {}
        for e in self.E:
            self.psem[e] = self.es.enter_context(nc.semaphore("p_" + e))
            self.pcnt[e] = 0
        self.waited = {}
        self.nsem = 0
        self.nname = 0

    def newsem(self, name):
        if getattr(self, 'freelist', None):
            key = self.freelist.pop()
            self.live.append(key)
            return key
        if not hasattr(self, 'live'):
            self.live = []
            self.freelist = []
        self.nsem += 1
        key = f"d:{name}_{self.nsem}"
        self.psem[key] = self.es.enter_context(self.nc.semaphore(f"{name}_{self.nsem}"))
        self.pcnt[key] = 0
        self.live.append(key)
        return key

    def sb(self, name, shape, dt, es=None):
        self.nname += 1
        return (es or self.es).enter_context(self.nc.sbuf_tensor(f"{name}_{self.nname}", shape, dt))

    def ps(self, name, shape, dt, es=None):
        self.nname += 1
        return (es or self.es).enter_context(self.nc.psum_tensor(f"{name}_{self.nname}", shape, dt))

    def mark(self, e, ins):
        ins.then_inc(self.psem[e], 1)
        self.pcnt[e] += 1
        return (e, self.pcnt[e])

    def wait(self, e, *evs):
        for ev in evs:
            if ev is None:
                continue
            if isinstance(ev, list):
                self.wait(e, *ev)
                continue
            key, val = ev
            if self.waited.get((e, key), 0) >= val:
                continue
            self.E[e].wait_ge(self.psem[key], val)
            self.waited[(e, key)] = val

    def dma(self, q, out, in_, key, **kw):
        ins = self.E[q].dma_start(out=out, in_=in_, **kw)
        ins.then_inc(self.psem[key], 16)
        self.pcnt[key] += 16
        return (key, self.pcnt[key])

    def tick(self, n=1):
        for _ in range(n):
            if getattr(self, 'bg', None) is None:
                return
            try:
                next(self.bg)
            except StopIteration:
                self.bg = None

    def drain(self):
        while getattr(self, 'bg', None) is not None:
            self.tick()

    def barrier(self, extra=()):
        engines = ('pe', 'act', 'dve', 'sp')
        evs = [(e, self.pcnt[e]) for e in engines if self.pcnt[e] > 0] + [e for e in extra if e is not None]
        for e in engines:
            self.wait(e, *evs)
        self.last_barrier = evs
        if hasattr(self, 'live'):
            self.freelist.extend(self.live)
            self.live = []


class Ring:
    def __init__(self, bufs):
        self.bufs = bufs
        self.free = [None] * len(bufs)
        self.i = 0

    def get(self):
        i = self.i
        self.i = (self.i + 1) % len(self.bufs)
        return i, self.bufs[i], self.free[i]

    def release(self, i, ev):
        self.free[i] = ev


class WStream:
    def __init__(self, kb, slots, loads):
        self.kb = kb
        self.slots = slots
        self.n = len(slots)
        self.loads = loads
        self.keys = [kb.newsem("w") for _ in slots]
        self.free = [None] * self.n
        self.ev = {}
        kb.wait('pool', getattr(kb, 'last_barrier', None))
        for i in range(min(self.n, len(loads))):
            self._issue(i)

    def _issue(self, i):
        s = i % self.n
        self.kb.wait('pool', self.free[s])
        dram, view = self.loads[i]
        self.ev[i] = self.kb.dma('pool', view(self.slots[s]), dram, self.keys[s])

    def get(self, i):
        return self.slots[i % self.n], self.ev[i]

    def release(self, i, ev):
        self.free[i % self.n] = ev
        if i + self.n < len(self.loads):
            self._issue(i + self.n)


class Stager:
    def __init__(self, kb, bufs, q='sp'):
        self.kb = kb
        self.ring = Ring(bufs)
        self.keys = [kb.newsem("st") for _ in bufs]
        self.q = q
        self.last = []

    def get(self):
        return self.ring.get()

    def store(self, i, dst, src, ev):
        self.kb.wait(self.q, ev)
        e = self.kb.dma(self.q, dst, src, self.keys[i])
        self.ring.release(i, e)
        self.last.append(e)
        self.last = self.last[-len(self.keys):]
        return e


class Loader:
    def __init__(self, kb, bufs, q='sp'):
        self.kb = kb
        self.ring = Ring(bufs)
        self.keys = [kb.newsem("ld") for _ in bufs]
        self.q = q

    def load(self, fn):
        i, buf, free = self.ring.get()
        self.kb.wait(self.q, free)
        ev = None
        for o, s in fn(buf):
            ev = self.kb.dma(self.q, o, s, self.keys[i])
        return i, buf, ev

    def release(self, i, ev):
        self.ring.release(i, ev)


def phase_norm(kb, x_src, wb_dram, hT, ident, cst_eps):
    nc = kb.nc
    with ExitStack() as es:
        wb = kb.sb("nw", [128, D], F32, es)
        kw = kb.newsem("nw")
        ev_w = kb.dma('sp', wb[:], wb_dram, kw)
        ld = Loader(kb, [kb.sb("nx", [128, D], F32, es) for _ in range(2)])
        xn = Ring([kb.sb("nxn", [128, D], BF16, es) for _ in range(2)])
        junk = kb.sb("njunk", [128, D], BF16, es)
        ss = kb.sb("nss", [128, 16], F32, es)
        rstd = kb.sb("nrstd", [128, 16], F32, es)
        pst = Ring([kb.ps("npt", [128, 16, 128], BF16, es) for _ in range(2)])
        kb.wait('dve', ev_w)
        for tt in range(T // 128):
            li, xt, ev_x = ld.load(lambda b: [(b[:], x_src[tt * 128:(tt + 1) * 128, :])])
            kb.wait('act', ev_x)
            ev_sq = kb.mark('act', nc.scalar.activation(out=junk[:], in_=xt[:], func=AF.Square,
                                                         accum_out=ss[:, tt:tt + 1]))
            kb.wait('act', ev_sq)
            ev_sq = kb.mark('act', nc.scalar.activation(out=rstd[:, tt:tt + 1], in_=ss[:, tt:tt + 1], func=AF.Sqrt,
                                                         scale=1.0 / D, bias=cst_eps[:]))
            kb.wait('dve', ev_sq, ev_x)
            ev_rc = kb.mark('dve', nc.vector.reciprocal(out=rstd[:, tt:tt + 1], in_=rstd[:, tt:tt + 1]))
            kb.wait('dve', ev_rc)
            xi, xnb, xfree = xn.get()
            kb.wait('dve', xfree)
            ev_xn = kb.mark('dve', nc.vector.scalar_tensor_tensor(out=xnb[:], in0=xt[:], scalar=rstd[:, tt:tt + 1],
                                                                  in1=wb[:], op0=ALU.mult, op1=ALU.mult))
            ld.release(li, ev_xn)
            pi, pt, pfree = pst.get()
            kb.wait('pe', ev_xn, pfree)
            for c in range(KC):
                ins = nc.tensor.transpose(out=pt[:, c, :], in_=xnb[:, c * 128:(c + 1) * 128], identity=ident[:])
            ev_t = kb.mark('pe', ins)
            xn.release(xi, ev_t)
            e = 'act' if tt % 2 == 0 else 'dve'
            kb.wait(e, ev_t)
            if e == 'act':
                ins = nc.scalar.copy(out=hT[:, :, tt * 128:(tt + 1) * 128], in_=pt[:])
            else:
                ins = nc.vector.tensor_copy(out=hT[:, :, tt * 128:(tt + 1) * 128], in_=pt[:])
            pst.release(pi, kb.mark(e, ins))
        kb.barrier()


def phase_inproj(kb, l, hT, dr, cst):
    nc = kb.nc
    with ExitStack() as es:
        wsl = [kb.sb("w", [128, 16, 512], BF16, es) for _ in range(2)]
        loads = [(dr['win'][l, g, :, :, 0:WIN_NCOLS[g]], (lambda s, n=WIN_NCOLS[g]: s[:, :, 0:n])) for g in range(13)]
        ws = WStream(kb, wsl, loads)
        banks = Ring([kb.ps("b", [128, 512], F32, es) for _ in range(6)])
        sbank = Ring([kb.ps("sb", [128, 512], F32, es) for _ in range(2)])
        stf = Stager(kb, [kb.sb("stf", [128, 512], F32, es) for _ in range(3)])
        stb = Stager(kb, [kb.sb("stb", [128, 512], BF16, es) for _ in range(3)])
        sqr = Ring([kb.sb("sq", [128, 512], BF16, es) for _ in range(3)])
        rr = Ring([kb.sb("rr", [128, 512], F32, es) for _ in range(2)])
        sig = Ring([kb.sb("sig", [128, 512], F32, es) for _ in range(2)])
        qkg = kb.sb("qkg", [128, 2], F32, es)
        kq = kb.newsem("qkg")
        ev_g = kb.dma('sp', qkg[:], dr['qkg'][l], kq)
        kb.wait('dve', ev_g)
        ev_qkg = kb.mark('dve', nc.vector.tensor_scalar(out=qkg[:, 0:1], in0=qkg[:, 0:1], scalar1=0.125, scalar2=None, op0=ALU.mult))
        kb.wait('dve', ev_qkg)
        cnt = [0]

        def cp_eng():
            cnt[0] += 1
            return 'act' if cnt[0] % 2 == 0 else 'dve'

        def copy_out(e, out, in_):
            if e == 'act':
                return nc.scalar.copy(out=out, in_=in_)
            return nc.vector.tensor_copy(out=out, in_=in_)

        def mm_fm(wt, cc, tt, bank, m=128):
            for k in range(KC):
                ins = nc.tensor.matmul(bank[0:m, :], lhsT=wt[:, k, cc * 128:cc * 128 + m],
                                       rhs=hT[:, k, tt * 512:(tt + 1) * 512], start=(k == 0), stop=(k == KC - 1))
            return kb.mark('pe', ins)

        pending = []

        def flush_pending():
            while pending:
                pending.pop(0)()

        for g in (0, 1):
            wt, ev = ws.get(g)
            kb.wait('pe', ev)
            dst = dr['qT'] if g == 0 else dr['kT']
            for cc in range(4):
                for tt in range(4):
                    bi, bank, bfree = banks.get()
                    kb.wait('pe', bfree)
                    ev_mm = mm_fm(wt, cc, tt, bank)
                    flush_pending()
                    si, sq, sfree = sqr.get()
                    kb.wait('act', ev_mm, sfree)
                    ev_sq = kb.mark('act', nc.scalar.activation(out=sq[:], in_=bank[:], func=AF.Square))

                    def stats(bi=bi, bank=bank, si=si, sq=sq, ev_sq=ev_sq, cc=cc, tt=tt, g=g, dst=dst):
                        pi, pb, pfree = sbank.get()
                        kb.wait('pe', ev_sq, pfree)
                        ev_st = kb.mark('pe', nc.tensor.matmul(pb[:], lhsT=cst['onesblk'][:], rhs=sq[:], start=True, stop=True))
                        sqr.release(si, ev_st)
                        ri, r, rfree = rr.get()
                        kb.wait('dve', ev_st, rfree)
                        kb.wait('act', ev_st, rfree)
                        ev_r = kb.mark('act', nc.scalar.activation(out=r[:], in_=pb[:], func=AF.Sqrt, scale=1.0 / 64, bias=cst['eps'][:]))
                        kb.wait('dve', ev_r)
                        nc.vector.reciprocal(out=r[:], in_=r[:])
                        oi, ob, ofree = stb.get()
                        kb.wait('dve', ofree)
                        ev_o = kb.mark('dve', nc.vector.scalar_tensor_tensor(out=ob[:], in0=bank[:], scalar=qkg[:, g:g + 1],
                                                                           in1=r[:], op0=ALU.mult, op1=ALU.mult))
                        sbank.release(pi, ev_o)
                        rr.release(ri, ev_o)
                        banks.release(bi, ev_o)
                        stb.store(oi, dst[cc, :, tt * 512:(tt + 1) * 512], ob[:], ev_o)
                    pending.append(stats)
            flush_pending()
            ws.release(g, ev_mm)

        wt, ev = ws.get(2)
        kb.wait('pe', ev)
        for cc in range(4):
            for tt in range(4):
                bi, bank, bfree = banks.get()
                kb.wait('pe', bfree)
                ev_mm = mm_fm(wt, cc, tt, bank)
                e = cp_eng()
                oi, ob, ofree = stf.get()
                kb.wait(e, ev_mm, ofree)
                ev_o = kb.mark(e, copy_out(e, ob[:], bank[:]))
                banks.release(bi, ev_o)
                stf.store(oi, dr['gqkT'][cc, :, tt * 512:(tt + 1) * 512], ob[:], ev_o)
        ws.release(2, ev_mm)

        for g in (3, 4):
            wt, ev = ws.get(g)
            kb.wait('pe', ev)
            for pr in range(2):
                for tt in range(4):
                    bia, banka, bfree = banks.get()
                    kb.wait('pe', bfree)
                    ev_a = mm_fm(wt, 2 * pr, tt, banka)
                    big, bankg, bfree = banks.get()
                    kb.wait('pe', bfree)
                    ev_gm = mm_fm(wt, 2 * pr + 1, tt, bankg)
                    gi, sg, gfree = sig.get()
                    kb.wait('act', ev_gm, gfree)
                    ev_s = kb.mark('act', nc.scalar.activation(out=sg[:], in_=bankg[:], func=AF.Sigmoid))
                    banks.release(big, ev_s)
                    oi, ob, ofree = stf.get()
                    kb.wait('dve', ev_s, ev_a, ofree)
                    ev_o = kb.mark('dve', nc.vector.tensor_tensor(out=ob[:], in0=banka[:], in1=sg[:], op=ALU.mult))
                    banks.release(bia, ev_o)
                    sig.release(gi, ev_o)
                    c = (g - 3) * 2 + pr
                    stf.store(oi, dr['gluT'][c, :, tt * 512:(tt + 1) * 512], ob[:], ev_o)
            ws.release(g, ev_gm)

        for g in (5, 6):
            wt, ev = ws.get(g)
            kb.wait('pe', ev)
            for cc in range(4):
                for tt in range(4):
                    bi, bank, bfree = banks.get()
                    kb.wait('pe', bfree)
                    ev_mm = mm_fm(wt, cc, tt, bank)
                    e = cp_eng()
                    oi, ob, ofree = stf.get()
                    kb.wait(e, ev_mm, ofree)
                    ev_o = kb.mark(e, copy_out(e, ob[:], bank[:]))
                    banks.release(bi, ev_o)
                    stf.store(oi, dr['xbcT'][(g - 5) * 4 + cc, :, tt * 512:(tt + 1) * 512], ob[:], ev_o)
            ws.release(g, ev_mm)

        wt, ev = ws.get(7)
        kb.wait('pe', ev)
        for tt in range(4):
            bi, bank, bfree = banks.get()
            kb.wait('pe', bfree)
            ev_mm = mm_fm(wt, 0, tt, bank, m=16)
            e = cp_eng()
            oi, ob, ofree = stf.get()
            kb.wait(e, ev_mm, ofree)
            ev_o = kb.mark(e, copy_out(e, ob[0:16, :], bank[0:16, :]))
            banks.release(bi, ev_o)
            stf.store(oi, dr['glowT'][:, tt * 512:(tt + 1) * 512], ob[0:16, :], ev_o)
        ws.release(7, ev_mm)

        tm_dst = {8: ('v', BF16), 9: ('gv', BF16), 10: ('gout', F32), 11: ('z', F32)}
        for g in range(8, 13):
            wt, ev = ws.get(g)
            kb.wait('pe', ev)
            n = WIN_NCOLS[g]
            for tt in range(T // 128):
                bi, bank, bfree = banks.get()
                kb.wait('pe', bfree)
                for k in range(KC):
                    ins = nc.tensor.matmul(bank[:, 0:n], lhsT=hT[:, k, tt * 128:(tt + 1) * 128], rhs=wt[:, k, 0:n],
                                           start=(k == 0), stop=(k == KC - 1))
                ev_mm = kb.mark('pe', ins)
                e = cp_eng()
                if g == 12:
                    oi, ob, ofree = stf.get()
                    kb.wait(e, ev_mm, ofree)
                    ev_o = kb.mark(e, copy_out(e, ob[:, 0:n], bank[:, 0:n]))
                    banks.release(bi, ev_o)
                    kb.wait('sp', ev_o)
                    kb.dma('sp', dr['gk'][tt * 128:(tt + 1) * 128, :], ob[:, 0:256], stf.keys[oi])
                    stf.store(oi, dr['dt'][tt * 128:(tt + 1) * 128, :], ob[:, 256:264], ev_o)
                else:
                    name, dt_ = tm_dst[g]
                    st = stb if dt_ == BF16 else stf
                    oi, ob, ofree = st.get()
                    kb.wait(e, ev_mm, ofree)
                    ev_o = kb.mark(e, copy_out(e, ob[:], bank[:]))
                    banks.release(bi, ev_o)
                    st.store(oi, dr[name][tt * 128:(tt + 1) * 128, :], ob[:], ev_o)
            ws.release(g, ev_mm)
        kb.barrier(extra=stf.last + stb.last)


def mixer_conv_gen(kb, l, dr, cst):
    nc = kb.nc
    with ExitStack() as es:
        cw = kb.sb("cw", [128, 4, 31], F32, es)
        cb = kb.sb("cb", [128, 4], F32, es)
        lw = kb.sb("lw", [128, 4], F32, es)
        lb = kb.sb("lb", [128, 4], F32, es)
        kp = kb.newsem("cp")
        kb.dma('sp', cw[:], dr['cdw'][l], kp)
        kb.dma('sp', cb[:], dr['cdb'][l], kp)
        kb.dma('sp', lw[:], dr['clnw'][l], kp)
        ev_p = kb.dma('sp', lb[:], dr['clnb'][l], kp)
        glu = [kb.sb("glu", [128, 30 + T], F32, es) for _ in range(4)]
        acc = [kb.sb("acc", [128, T], F32, es) for _ in range(4)]
        kg = kb.newsem("cg")
        ev_acc = []
        for c in range(4):
            e = 'dve'
            eng = kb.E[e]
            ev_z = kb.mark(e, eng.memset(glu[c][:, 0:30], 0.0))
            ev_l = kb.dma('sp', glu[c][:, 30:30 + T], dr['gluT'][c], kg)
            kb.wait(e, ev_l, ev_p, ev_z)
            eng.tensor_scalar(out=acc[c][:], in0=glu[c][:, 0:T], scalar1=cw[:, c, 0:1], scalar2=cb[:, c:c + 1],
                              op0=ALU.mult, op1=ALU.add)
            for j in range(1, 31):
                ins = eng.scalar_tensor_tensor(out=acc[c][:], in0=glu[c][:, j:j + T], scalar=cw[:, c, j:j + 1],
                                               in1=acc[c][:], op0=ALU.mult, op1=ALU.add)
                if j < 30:
                    yield
            ev_acc.append(kb.mark(e, ins))
        ps1 = kb.ps("cs1", [128, 512], F32, es)
        ps2 = kb.ps("cs2", [128, 512], F32, es)
        sq = Ring([kb.sb("csq", [128, 512], F32, es) for _ in range(2)])
        mean = kb.sb("cmean", [128, 512], F32, es)
        msq = kb.sb("cmsq", [128, 512], F32, es)
        rstd = kb.sb("crstd", [128, 512], F32, es)
        tmp = Ring([kb.sb("ctmp", [128, 512], F32, es) for _ in range(2)])
        st = Stager(kb, [kb.sb("cst", [128, 512], BF16, es) for _ in range(2)])
        ev_prev = None
        for tt in range(4):
            sl = slice(tt * 512, (tt + 1) * 512)
            kb.wait('pe', ev_prev)
            for c in range(4):
                kb.wait('pe', ev_acc[c])
                nc.tensor.matmul(ps1[:], lhsT=cst['ones_f'][:], rhs=acc[c][:, sl], start=(c == 0), stop=(c == 3))
            ev_s1 = None
            for c in range(4):
                si, sb_, sfree = sq.get()
                kb.wait('act', ev_acc[c], sfree)
                ev_q = kb.mark('act', nc.scalar.activation(out=sb_[:], in_=acc[c][:, sl], func=AF.Square))
                kb.wait('pe', ev_q)
                ev_m = kb.mark('pe', nc.tensor.matmul(ps2[:], lhsT=cst['ones_f'][:], rhs=sb_[:], start=(c == 0), stop=(c == 3)))
                sq.release(si, ev_m)
            kb.wait('dve', ev_m)
            nc.vector.tensor_scalar(out=mean[:], in0=ps1[:], scalar1=1.0 / 512, scalar2=None, op0=ALU.mult)
            nc.vector.tensor_tensor(out=msq[:], in0=mean[:], in1=mean[:], op=ALU.mult)
            ev_v = kb.mark('dve', nc.vector.scalar_tensor_tensor(out=rstd[:], in0=ps2[:], scalar=1.0 / 512, in1=msq[:],
                                                                op0=ALU.mult, op1=ALU.subtract))
            kb.wait('act', ev_v)
            ev_sd = kb.mark('act', nc.scalar.activation(out=rstd[:], in_=rstd[:], func=AF.Sqrt, bias=cst['eps'][:]))
            kb.wait('dve', ev_sd)
            nc.vector.reciprocal(out=rstd[:], in_=rstd[:])
            for c in range(4):
                ti, tb, tfree = tmp.get()
                kb.wait('dve', tfree)
                nc.vector.tensor_tensor(out=tb[:], in0=acc[c][:, sl], in1=mean[:], op=ALU.subtract)
                ev_t = kb.mark('dve', nc.vector.tensor_tensor(out=tb[:], in0=tb[:], in1=rstd[:], op=ALU.mult))
                oi, ob, ofree = st.get()
                kb.wait('act', ev_t, ofree, ev_p)
                ev_y = kb.mark('act', nc.scalar.activation(out=ob[:], in_=tb[:], func=AF.Silu, scale=lw[:, c:c + 1], bias=lb[:, c:c + 1]))
                tmp.release(ti, ev_y)
                st.store(oi, dr['yT'][8 + c, :, sl], ob[:], ev_y)
                yield
            ev_prev = ev_t
        kb.bg_extra = list(st.last)


def mixer_conv(kb, l, dr, cst):
    for _ in mixer_conv_gen(kb, l, dr, cst):
        pass
    kb.barrier(extra=kb.bg_extra)


def phase_gates(kb, l, hT, dr, cst):
    nc = kb.nc
    with ExitStack() as es:
        yT = kb.sb("yT", [128, 16, T], BF16, es)
        ky = kb.newsem("yT")
        for c in range(16):
            ev_y = kb.dma('sp', yT[:, c, :], dr['yT'][c], ky)
        bg = kb.sb("bg", [128, 4, 16], F32, es)
        ev_b = kb.dma('sp', bg[:], dr['bgate'][l], ky)
        wsl = [kb.sb("wg", [128, 20, 128], BF16, es) for _ in range(2)]
        loads = [(dr['wgb'][l, cc, i], (lambda s: s[:])) for cc in range(16) for i in range(4)]
        ws = WStream(kb, wsl, loads)
        gb = Ring([kb.ps("gb", [128, 512], F32, es) for _ in range(4)])
        bb = Ring([kb.ps("bb", [128, 512], F32, es) for _ in range(4)])
        sig = Ring([kb.sb("gsig", [128, 512], F32, es) for _ in range(2)])
        macc = Ring([kb.sb("macc", [128, T], F32, es) for _ in range(2)])
        tmp = kb.sb("gtmp", [128, 512], F32, es)
        st = Stager(kb, [kb.sb("gst", [128, T], BF16, es) for _ in range(2)])
        kb.wait('pe', ev_y)
        kb.wait('act', ev_b)
        n = 0
        for cc in range(16):
            mi, mb, mfree = macc.get()
            for i in range(4):
                wt, ev = ws.get(n)
                kb.wait('pe', ev)
                for tt in range(4):
                    sl = slice(tt * 512, (tt + 1) * 512)
                    gi, gbank, gfree = gb.get()
                    kb.wait('pe', gfree)
                    for k in range(KC):
                        ins = nc.tensor.matmul(gbank[:], lhsT=wt[:, k, :], rhs=hT[:, k, sl], start=(k == 0), stop=(k == KC - 1))
                    ev_g = kb.mark('pe', ins)
                    bi, bbank, bfree = bb.get()
                    kb.wait('pe', bfree)
                    for k in range(4):
                        ins = nc.tensor.matmul(bbank[:], lhsT=wt[:, 16 + k, :], rhs=yT[:, 4 * i + k, sl], start=(k == 0), stop=(k == 3))
                    ev_br = kb.mark('pe', ins)
                    si, sg, sfree = sig.get()
                    kb.wait('act', ev_g, sfree)
                    ev_s = kb.mark('act', nc.scalar.activation(out=sg[:], in_=gbank[:], func=AF.Sigmoid, bias=bg[:, i, cc:cc + 1]))
                    gb.release(gi, ev_s)
                    kb.wait('dve', ev_s, ev_br)
                    if i == 0:
                        kb.wait('dve', mfree)
                        ev_m = kb.mark('dve', nc.vector.tensor_tensor(out=mb[:, sl], in0=bbank[:], in1=sg[:], op=ALU.mult))
                    else:
                        nc.vector.tensor_tensor(out=tmp[:], in0=bbank[:], in1=sg[:], op=ALU.mult)
                        ev_m = kb.mark('dve', nc.vector.tensor_tensor(out=mb[:, sl], in0=mb[:, sl], in1=tmp[:], op=ALU.add))
                    bb.release(bi, ev_m)
                    sig.release(si, ev_m)
                ws.release(n, ev_br)
                n += 1
            oi, ob, ofree = st.get()
            kb.wait('act', ev_m, ofree)
            ev_c = kb.mark('act', nc.scalar.copy(out=ob[:], in_=mb[:]))
            macc.release(mi, ev_c)
            st.store(oi, dr['mT'][cc], ob[:], ev_c)
        kb.barrier(extra=st.last)


def phase_outproj(kb, l, x_src, x_dst, dr, cst):
    nc = kb.nc
    with ExitStack() as es:
        mT = kb.sb("mT", [128, 16, T], BF16, es)
        km = kb.newsem("mT")
        for c in range(16):
            ev_m = kb.dma('sp', mT[:, c, :], dr['mT'][c], km)
        wsl = [kb.sb("wo", [128, 16, 512], BF16, es) for _ in range(2)]
        loads = [(dr['wout'][l, g], (lambda s: s[:])) for g in range(4)]
        ws = WStream(kb, wsl, loads)
        banks = Ring([kb.ps("ob", [128, 512], F32, es) for _ in range(4)])
        ld = Loader(kb, [kb.sb("ox", [128, 512], F32, es) for _ in range(4)])
        st = Stager(kb, [kb.sb("oo", [128, 512], F32, es) for _ in range(4)])
        kb.wait('pe', ev_m)
        xseq = [(g, tt) for g in range(4) for tt in range(16)]
        x_t = {}

        def issue_x(m):
            if m < len(xseq):
                g_, tt_ = xseq[m]
                x_t[m] = ld.load(lambda b: [(b[:], x_src[tt_ * 128:(tt_ + 1) * 128, g_ * 512:(g_ + 1) * 512])])

        issue_x(0)
        issue_x(1)
        m = 0
        for g in range(4):
            wt, ev = ws.get(g)
            kb.wait('pe', ev)
            for tt in range(16):
                rows = slice(tt * 128, (tt + 1) * 128)
                cols = slice(g * 512, (g + 1) * 512)
                issue_x(m + 2)
                li, xb, ev_x = x_t.pop(m)
                m += 1
                bi, bank, bfree = banks.get()
                kb.wait('pe', bfree)
                for k in range(KC):
                    ins = nc.tensor.matmul(bank[:], lhsT=mT[:, k, rows], rhs=wt[:, k, :], start=(k == 0), stop=(k == KC - 1))
                ev_mm = kb.mark('pe', ins)
                oi, ob, ofree = st.get()
                kb.wait('dve', ev_mm, ev_x, ofree)
                ev_o = kb.mark('dve', nc.vector.tensor_tensor(out=ob[:], in0=bank[:], in1=xb[:], op=ALU.add))
                banks.release(bi, ev_o)
                ld.release(li, ev_o)
                st.store(oi, x_dst[rows, cols], ob[:], ev_o)
            ws.release(g, ev_mm)
        kb.barrier(extra=st.last)


def phase_ffn_up(kb, l, hT, dr, cst):
    nc = kb.nc
    with ExitStack() as es:
        fw = kb.sb("fw", [128, 88, 3], F32, es)
        fb = kb.sb("fb", [128, 88], F32, es)
        kf = kb.newsem("fp")
        kb.dma('sp', fw[:], dr['fcw'][l], kf)
        ev_p = kb.dma('sp', fb[:], dr['fcb'][l], kf)
        wsl = [kb.sb("wu", [128, 16, 512], BF16, es) for _ in range(2)]
        loads = [(dr['wup'][l, g], (lambda s: s[:])) for g in range(22)]
        ws = WStream(kb, wsl, loads)
        banks = Ring([kb.ps("ub", [128, 512], F32, es) for _ in range(8)])
        ubuf = Ring([kb.sb("uu", [128, 2 + T], F32, es) for _ in range(4)])
        accg = kb.sb("accg", [128, T], F32, es)
        accv = kb.sb("accv", [128, T], F32, es)
        st = Stager(kb, [kb.sb("ast", [128, 2, T], BF16, es) for _ in range(2)])
        for i in range(4):
            ev_z = kb.mark('dve', nc.vector.memset(ubuf.bufs[i][:, 0:2], 0.0))
        kb.wait('act', ev_z)
        kb.wait('dve', ev_p)
        sti = None
        for g in range(22):
            wt, ev = ws.get(g)
            kb.wait('pe', ev)
            for pr in range(2):
                j = 2 * g + pr
                us = []
                for half in range(2):
                    cc = 2 * pr + half
                    ui, ub, ufree = ubuf.get()
                    evs = []
                    for tt in range(4):
                        bi, bank, bfree = banks.get()
                        kb.wait('pe', bfree)
                        for k in range(KC):
                            ins = nc.tensor.matmul(bank[:], lhsT=wt[:, k, cc * 128:(cc + 1) * 128], rhs=hT[:, k, tt * 512:(tt + 1) * 512],
                                                   start=(k == 0), stop=(k == KC - 1))
                        ev_mm = kb.mark('pe', ins)
                        kb.wait('act', ev_mm, ufree)
                        ev_c = kb.mark('act', nc.scalar.copy(out=ub[:, 2 + tt * 512:2 + (tt + 1) * 512], in_=bank[:]))
                        banks.release(bi, ev_c)
                    us.append((ui, ub, ev_c))
                outs = []
                for half, accb in ((0, accg), (1, accv)):
                    ui, ub, ev_c = us[half]
                    q = 2 * j + half
                    kb.wait('dve', ev_c)
                    nc.vector.tensor_scalar(out=accb[:], in0=ub[:, 0:T], scalar1=fw[:, q, 0:1], scalar2=fb[:, q:q + 1],
                                            op0=ALU.mult, op1=ALU.add)
                    nc.vector.scalar_tensor_tensor(out=accb[:], in0=ub[:, 1:1 + T], scalar=fw[:, q, 1:2], in1=accb[:],
                                                   op0=ALU.mult, op1=ALU.add)
                    ev_a = kb.mark('dve', nc.vector.scalar_tensor_tensor(out=accb[:], in0=ub[:, 2:2 + T], scalar=fw[:, q, 2:3],
                                                                        in1=accb[:], op0=ALU.mult, op1=ALU.add))
                    outs.append(ev_a)
                ui_g, ub_g, _ = us[0]
                kb.wait('act', outs[0])
                ev_s = kb.mark('act', nc.scalar.activation(out=ub_g[:, 2:2 + T], in_=accg[:], func=AF.Silu))
                if pr == 0:
                    sti, sob, sofree = st.get()
                kb.wait('dve', ev_s, sofree)
                ev_o = kb.mark('dve', nc.vector.tensor_tensor(out=sob[:, pr, :], in0=ub_g[:, 2:2 + T], in1=accv[:], op=ALU.mult))
                ubuf.release(us[0][0], ev_o)
                ubuf.release(us[1][0], outs[1])
            kb.wait('sp', ev_o)
            j0 = 2 * g
            for t8 in range(8):
                e_st = kb.dma('sp', dr['aT'][t8, :, j0:j0 + 2, :], sob[:, :, t8 * 256:(t8 + 1) * 256], st.keys[sti])
            st.ring.release(sti, e_st)
            st.last.append(e_st)
            ws.release(g, ev_mm)
        kb.barrier(extra=st.last[-2:])


def phase_ffn_down(kb, l, x_src, x_dst, dr, cst):
    nc = kb.nc
    with ExitStack() as es:
        wsl = [kb.sb("wd", [128, 44, 512], BF16, es) for _ in range(2)]
        loads = [(dr['wdn'][l, g], (lambda s: s[:])) for g in range(4)]
        ws = WStream(kb, wsl, loads)
        banks = Ring([kb.ps("db", [128, 512], F32, es) for _ in range(4)])
        la = Loader(kb, [kb.sb("da", [128, 44, 256], BF16, es) for _ in range(2)])
        ld = Loader(kb, [kb.sb("dx", [128, 512], F32, es) for _ in range(3)])
        st = Stager(kb, [kb.sb("do", [128, 512], F32, es) for _ in range(3)])
        seq = [(g, t8) for g in range(4) for t8 in range(8)]
        xseq = [(g, t8, h2) for (g, t8) in seq for h2 in range(2)]
        a_t = {}
        x_t = {}

        def issue_a(n):
            if n < len(seq):
                a_t[n] = la.load(lambda b: [(b[:], dr['aT'][seq[n][1]])])

        def issue_x(m):
            if m < len(xseq):
                g_, t8_, h2_ = xseq[m]
                r_ = slice(t8_ * 256 + h2_ * 128, t8_ * 256 + (h2_ + 1) * 128)
                x_t[m] = ld.load(lambda b: [(b[:], x_src[r_, g_ * 512:(g_ + 1) * 512])])

        issue_a(0)
        issue_x(0)
        m = 0
        for n, (g, t8) in enumerate(seq):
            if t8 == 0:
                wt, ev = ws.get(g)
                kb.wait('pe', ev)
            cols = slice(g * 512, (g + 1) * 512)
            issue_a(n + 1)
            ai, ab, ev_a = a_t.pop(n)
            kb.wait('pe', ev_a)
            for h2 in range(2):
                rows = slice(t8 * 256 + h2 * 128, t8 * 256 + (h2 + 1) * 128)
                issue_x(m + 1)
                li, xb, ev_x = x_t.pop(m)
                m += 1
                bi, bank, bfree = banks.get()
                kb.wait('pe', bfree)
                for k in range(NPAIR):
                    ins = nc.tensor.matmul(bank[:], lhsT=ab[:, k, h2 * 128:(h2 + 1) * 128], rhs=wt[:, k, :],
                                           start=(k == 0), stop=(k == NPAIR - 1))
                ev_mm = kb.mark('pe', ins)
                oi, ob, ofree = st.get()
                kb.wait('dve', ev_mm, ev_x, ofree)
                ev_o = kb.mark('dve', nc.vector.tensor_tensor(out=ob[:], in0=bank[:], in1=xb[:], op=ALU.add))
                banks.release(bi, ev_o)
                ld.release(li, ev_o)
                st.store(oi, x_dst[rows, cols], ob[:], ev_o)
            la.release(ai, ev_mm)
            if t8 == 7:
                ws.release(g, ev_mm)
        kb.barrier(extra=st.last)


def mixer_attn(kb, l, dr, cst, lam_init, bg=None):
    nc = kb.nc
    with ExitStack() as es:
        lam = kb.sb("lam", [128, 256], F32, es)
        sub = kb.sb("subln", [128, 128], F32, es)
        kp = kb.newsem("ap")
        kb.dma('sp', lam[:], dr['lam'][l], kp)
        ev_p = kb.dma('sp', sub[:], dr['subln'][l], kp)
        prod = kb.sb("aprod", [128, 2, 64], F32, es)
        s2 = kb.sb("as2", [128, 2], F32, es)
        nl = kb.sb("anl", [128, 1], F32, es)
        kb.wait('dve', ev_p)
        nc.vector.tensor_tensor(out=prod[:, 0, :], in0=lam[:, 0:64], in1=lam[:, 64:128], op=ALU.mult)
        nc.vector.tensor_tensor(out=prod[:, 1, :], in0=lam[:, 128:192], in1=lam[:, 192:256], op=ALU.mult)
        ev = kb.mark('dve', nc.vector.reduce_sum(out=s2[:], in_=prod[:], axis=AX.X))
        kb.wait('act', ev)
        ev = kb.mark('act', nc.scalar.activation(out=s2[:], in_=s2[:], func=AF.Exp))
        kb.wait('dve', ev)
        ev = kb.mark('dve', nc.vector.tensor_tensor(out=nl[:], in0=s2[:, 1:2], in1=s2[:, 0:1], op=ALU.subtract))
        kb.wait('dve', ev)
        nc.vector.tensor_scalar(out=nl[:], in0=nl[:], scalar1=-lam_init, scalar2=None, op0=ALU.add)
        ev_nl = kb.mark('dve', nc.vector.tensor_scalar(out=sub[:], in0=sub[:], scalar1=1.0 - lam_init, scalar2=None, op0=ALU.mult))
        kb.wait('dve', ev_nl)

        qk = Loader(kb, [kb.sb("aqk", [128, 2, T], BF16, es) for _ in range(2)])
        va = Loader(kb, [kb.sb("ava", [128, 16, 129], BF16, es) for _ in range(2)])
        for b in va.ring.bufs:
            ev_one = kb.mark('dve', nc.vector.memset(b[:, :, 128:129], 1.0))
        kb.wait('pe', ev_one)
        sbank = Ring([kb.ps("asb", [128, 512], F32, es) for _ in range(3)])
        obank = Ring([kb.ps("aob", [128, 2, 129], F32, es) for _ in range(2)])
        tb_all = kb.ps("atb", [128, 2, 128], BF16, es)
        tbank = Ring([tb_all[:, 0, :], tb_all[:, 1, :]])
        pT = Ring([kb.sb("apT", [128, 512], BF16, es) for _ in range(3)])
        rc = Ring([kb.sb("arc", [128, 4], F32, es) for _ in range(2)])
        t1 = kb.sb("at1", [128, 128], F32, es)
        ob = Ring([kb.sb("ao", [128, 128], F32, es) for _ in range(2)])
        junk = kb.sb("ajunk", [128, 128], F32, es)
        yb = Ring([kb.sb("ay", [128, 128], BF16, es) for _ in range(2)])
        st = Stager(kb, [kb.sb("ayT", [128, T], BF16, es) for _ in range(2)])
        vview = dr['v'].rearrange("(j p) c -> p j c", p=128)
        kb.bg_extra = []
        kb.bg = bg
        groups = [(i, m, jg) for i in range(16) for m in range(2) for jg in range(0, i + 1, 4)]
        for h in range(4):
            qi, qkb, ev_q = qk.load(lambda b: [(b[:, 0, :], dr['qT'][h]), (b[:, 1, :], dr['kT'][h])])
            vi, vb, ev_v = va.load(lambda b: [(b[:, :, 0:128], vview[:, :, h * 128:(h + 1) * 128])])
            kb.wait('pe', ev_q, ev_v)
            sti, yTh, yfree = st.get()
            cur_ob = {}
            last = {}

            def emit_scores(i, m, jg):
                rows = slice(m * 64, (m + 1) * 64)
                je = min(jg + 4, i + 1)
                w = (je - jg) * 128
                si, sbk, sfree = sbank.get()
                kb.wait('pe', sfree)
                for jj in range(jg, je):
                    ins = nc.tensor.matmul(sbk[:, (jj - jg) * 128:(jj - jg + 1) * 128], lhsT=qkb[rows, 1, jj * 128:(jj + 1) * 128],
                                           rhs=qkb[rows, 0, i * 128:(i + 1) * 128], start=True, stop=True)
                ev_s = kb.mark('pe', ins)
                pi, pb, pfree = pT.get()
                kb.wait('act', ev_s, pfree)
                ev_e = kb.mark('act', nc.scalar.activation(out=pb[:, 0:w], in_=sbk[:, 0:w], func=AF.Exp))
                sbank.release(si, ev_e)
                if je == i + 1:
                    off = (i - jg) * 128
                    kb.wait('dve', ev_e)
                    ev_e = kb.mark('dve', nc.vector.memset(pb[64:128, off:off + 64], 0.0))
                return (i, m, jg, je, pi, pb, ev_e)

            def emit_tail(i, oi, obk, ev_pv):
                ri, rcb, rfree = rc.get()
                kb.wait('dve', ev_pv, rfree)
                ev = kb.mark('dve', nc.vector.reciprocal(out=rcb[:, 0:2], in_=obk[:, :, 128]))
                kb.wait('dve', ev)
                ev = kb.mark('dve', nc.vector.tensor_tensor(out=rcb[:, 1:2], in0=rcb[:, 1:2], in1=nl[:], op=ALU.mult))
                kb.wait('dve', ev)
                nc.vector.tensor_scalar(out=t1[:], in0=obk[:, 0, 0:128], scalar1=rcb[:, 0:1], scalar2=None, op0=ALU.mult)
                bi, obuf, bfree = ob.get()
                kb.wait('dve', bfree)
                ev_o = kb.mark('dve', nc.vector.scalar_tensor_tensor(out=obuf[:], in0=obk[:, 1, 0:128], scalar=rcb[:, 1:2], in1=t1[:],
                                                                    op0=ALU.mult, op1=ALU.add))
                obank.release(oi, ev_o)
                kb.wait('act', ev_o)
                ev = kb.mark('act', nc.scalar.activation(out=junk[:], in_=obuf[:], func=AF.Square, accum_out=rcb[:, 2:3]))
                kb.wait('act', ev)
                ev = kb.mark('act', nc.scalar.activation(out=rcb[:, 3:4], in_=rcb[:, 2:3], func=AF.Sqrt, scale=1.0 / 128, bias=cst['eps'][:]))
                kb.wait('dve', ev)
                ev = kb.mark('dve', nc.vector.reciprocal(out=rcb[:, 3:4], in_=rcb[:, 3:4]))
                kb.wait('dve', ev)
                yi, ybuf, yfree2 = yb.get()
                kb.wait('dve', yfree2)
                ev_y = kb.mark('dve', nc.vector.scalar_tensor_tensor(out=ybuf[:], in0=obuf[:], scalar=rcb[:, 3:4], in1=sub[:],
                                                                    op0=ALU.mult, op1=ALU.mult))
                ob.release(bi, ev_y)
                rc.release(ri, ev_y)

                def tr():
                    ti, tb, tfree = tbank.get()
                    kb.wait('pe', ev_y, tfree)
                    ev_t = kb.mark('pe', nc.tensor.transpose(out=tb[:], in_=ybuf[:], identity=cst['ident'][:]))
                    yb.release(yi, ev_t)
                    kb.wait('act', ev_t, yfree)
                    ev_c = kb.mark('act', nc.scalar.copy(out=yTh[:, i * 128:(i + 1) * 128], in_=tb[:]))
                    tbank.release(ti, ev_c)
                    last['ev_c'] = ev_c
                return tr

            trs = []

            def emit_pv(stt):
                i, m, jg, je, pi, pb, ev_e = stt
                if m == 0 and jg == 0:
                    oi, obk, ofree = obank.get()
                    kb.wait('pe', ofree)
                    cur_ob['v'] = (oi, obk)
                oi, obk = cur_ob['v']
                kb.wait('pe', ev_e)
                for jj in range(jg, je):
                    ins = nc.tensor.matmul(obk[:, m, :], lhsT=pb[:, (jj - jg) * 128:(jj - jg + 1) * 128], rhs=vb[:, jj, :],
                                           start=(jj == 0), stop=(jj == i))
                ev_pv = kb.mark('pe', ins)
                pT.release(pi, ev_pv)
                last['ev_pv'] = ev_pv
                if m == 1 and je == i + 1:
                    trs.append(emit_tail(i, oi, obk, ev_pv))

            prev = None
            for gidx, (i, m, jg) in enumerate(groups):
                cur = emit_scores(i, m, jg)
                ready = trs[:]
                del trs[:]
                if prev is not None:
                    emit_pv(prev)
                for t_ in ready:
                    t_()
                prev = cur
                kb.tick(1)
            emit_pv(prev)
            for t_ in trs:
                t_()
            del trs[:]
            qk.release(qi, last['ev_pv'])
            va.release(vi, last['ev_pv'])
            st.store(sti, dr['yT'][h], yTh[:], last['ev_c'])
        kb.drain()
        kb.barrier(extra=st.last + kb.bg_extra)


def mixer_gla(kb, l, dr, cst):
    nc = kb.nc
    with ExitStack() as es:
        kp = kb.newsem("bp")
        w2f = kb.sb("w2f", [32, 256], F32, es)
        w2a = kb.sb("w2a", [32, 256], BF16, es)
        gn = kb.sb("gn", [64, 512], F32, es)
        glf = kb.sb("glf", [32, T], F32, es)
        glb = kb.sb("glb", [32, T], BF16, es)
        ev_m = kb.mark('dve', nc.vector.memset(glf[:], 1.0))
        kb.wait('sp', ev_m)
        kb.dma('sp', w2f[:], dr['w2aug'][l], kp)
        kb.dma('sp', gn[:], dr['glan'][l, 0:64, :], kp)
        ev_p = kb.dma('sp', glf[0:16, :], dr['glowT'][:, :], kp)
        kb.wait('dve', ev_p)
        nc.vector.tensor_copy(out=w2a[:], in_=w2f[:])
        ev_gl = kb.mark('dve', nc.vector.tensor_copy(out=glb[:], in_=glf[:]))
        g_all = kb.sb("g_all", [64, 32, 256], BF16, es)
        qtT = [kb.sb("qtT", [128, T], BF16, es) for _ in range(2)]
        ktT = [kb.sb("ktT", [128, T], BF16, es) for _ in range(2)]
        elast = kb.sb("elast", [128, 2, 32], F32, es)
        S = [kb.sb("S", [128, 256], F32, es) for _ in range(2)]
        Sbf = [kb.sb("Sbf", [128, 33, 128], BF16, es) for _ in range(2)]
        yTst = kb.sb("byT", [128, 4, T], BF16, es)
        for hp in range(2):
            nc.vector.memset(S[hp][:], 0.0)
            ev_z = kb.mark('dve', nc.vector.memset(Sbf[hp][:, 0, :], 0.0))
        banks = Ring([kb.ps("bb", [128, 512], F32, es) for _ in range(6)])
        tbank = Ring([kb.ps("btb", [128, 4, 64], BF16, es) for _ in range(2)])
        f1 = Ring([kb.sb("bf1", [128, 512], F32, es) for _ in range(3)])
        f2 = Ring([kb.sb("bf2", [128, 512], F32, es) for _ in range(3)])
        kb.wait('pe', ev_gl)
        for c in range(32):
            bi, bank, bfree = banks.get()
            kb.wait('pe', bfree)
            ev_mm = kb.mark('pe', nc.tensor.matmul(bank[0:64, 0:256], lhsT=glb[:, c * 64:(c + 1) * 64], rhs=w2a[:], start=True, stop=True))
            fi, fb, ffree = f1.get()
            kb.wait('act', ev_mm, ffree)
            nc.scalar.activation(out=fb[0:64, 0:256], in_=bank[0:64, 0:256], func=AF.Exp, scale=-1.0)
            ev_a = kb.mark('act', nc.scalar.activation(out=fb[0:64, 0:256], in_=fb[0:64, 0:256], func=AF.Ln, bias=cst['one'][0:64, :]))
            banks.release(bi, ev_a)
            kb.wait('dve', ev_a)
            ev_g = kb.mark('dve', nc.vector.tensor_scalar(out=g_all[:, c, :], in0=fb[0:64, 0:256], scalar1=-1.0 / 16, scalar2=None, op0=ALU.mult))
            f1.release(fi, ev_g)
        import os
        if os.environ.get('BSTAGE') == '1':
            kb.barrier()
            return
        ldq = Loader(kb, [kb.sb("bldq", [128, 2, 512], F32, es) for _ in range(2)])
        kb.wait('pe', ev_g)
        for t4 in range(4):
            sl = slice(t4 * 512, (t4 + 1) * 512)
            for hp in range(2):
                li, qb, ev_q = ldq.load(lambda b: [(b[:, 0, :], dr['gqkT'][hp, :, sl]), (b[:, 1, :], dr['gqkT'][2 + hp, :, sl])])
                bi, bank, bfree = banks.get()
                kb.wait('pe', bfree)
                for cj in range(8):
                    c = t4 * 8 + cj
                    ins = nc.tensor.matmul(bank[:, cj * 64:(cj + 1) * 64], lhsT=g_all[:, c, hp * 128:(hp + 1) * 128], rhs=cst['tri_b'][:],
                                           start=True, stop=True)
                ev_mm = kb.mark('pe', ins)
                ai, eq, afree = f1.get()
                ci, ek, cfree = f2.get()
                kb.wait('act', ev_mm, afree, cfree)
                nc.scalar.activation(out=eq[:], in_=bank[:], func=AF.Exp)
                ev_e = kb.mark('act', nc.scalar.activation(out=ek[:], in_=bank[:], func=AF.Exp, scale=-1.0))
                banks.release(bi, ev_e)
                kb.wait('dve', ev_e, ev_q)
                nc.vector.scalar_tensor_tensor(out=qtT[hp][:, sl], in0=qb[:, 0, :], scalar=0.125, in1=eq[:], op0=ALU.mult, op1=ALU.mult)
                nc.vector.tensor_tensor(out=ktT[hp][:, sl], in0=qb[:, 1, :], in1=ek[:], op=ALU.mult)
                ev_d = kb.mark('dve', nc.vector.tensor_copy(out=elast[:, hp, t4 * 8:(t4 + 1) * 8], in_=eq[:, 63:512:64]))
                f1.release(ai, ev_d)
                f2.release(ci, ev_d)
                ldq.release(li, ev_d)
        kb.wait('pe', ev_d)
        kb.wait('dve', ev_d)
        if os.environ.get('BSTAGE') == '2':
            kb.barrier()
            return
        ldk = Loader(kb, [kb.sb("bldk", [64, 256], F32, es) for _ in range(2)])
        ldv = Loader(kb, [kb.sb("bldv", [64, 512], BF16, es) for _ in range(2)])
        ldo = Loader(kb, [kb.sb("bldo", [64, 512], F32, es) for _ in range(2)])
        khat = Ring([kb.sb("khat", [64, 256], BF16, es) for _ in range(2)])
        am = Ring([kb.sb("bam", [64, 4, 64], BF16, es) for _ in range(2)])
        sm = Ring([kb.sb("bsm", [64, 8], F32, es) for _ in range(2)])
        yb = Ring([kb.sb("bby", [64, 512], BF16, es) for _ in range(2)])
        bjunk = kb.sb("bjunk", [64, 128], F32, es)
        ev_state = [ev_z, ev_z]
        for c in range(32):
            rows = slice(c * 64, (c + 1) * 64)
            ki, kbuf, ev_k = ldk.load(lambda b: [(b[:], dr['gk'][rows, :])])
            vi, vbuf, ev_v = ldv.load(lambda b: [(b[:], dr['gv'][rows, :])])
            oi, obuf, ev_o = ldo.load(lambda b: [(b[:], dr['gout'][rows, :])])
            bi, bank, bfree = banks.get()
            kb.wait('pe', bfree)
            ev_mm = kb.mark('pe', nc.tensor.matmul(bank[0:64, 0:256], lhsT=cst['slow_b'][:], rhs=g_all[:, c, :], start=True, stop=True))
            fi, fb, ffree = f1.get()
            kb.wait('act', ev_mm, ffree)
            ev_e = kb.mark('act', nc.scalar.activation(out=fb[0:64, 0:256], in_=bank[0:64, 0:256], func=AF.Exp))
            banks.release(bi, ev_e)
            hi, kh, hfree = khat.get()
            kb.wait('dve', ev_e, ev_k, hfree)
            ev_kh = kb.mark('dve', nc.vector.tensor_tensor(out=kh[:], in0=kbuf[:], in1=fb[0:64, 0:256], op=ALU.mult))
            f1.release(fi, ev_kh)
            ldk.release(ki, ev_kh)
            sb_ = []
            for par in range(2):
                bi, sbk, bfree = banks.get()
                kb.wait('pe', bfree)
                r_ = slice(par * 64, (par + 1) * 64)
                for hh in range(2):
                    ins = nc.tensor.matmul(sbk[0:64, hh * 64:(hh + 1) * 64], lhsT=ktT[hh][r_, rows], rhs=qtT[hh][r_, rows], start=True, stop=True)
                sb_.append((bi, sbk))
            ev_s = kb.mark('pe', ins)
            ai, amb, afree = am.get()
            kb.wait('dve', ev_s, afree)
            for par in range(2):
                ev_am = kb.mark('dve', nc.vector.tensor_tensor(out=amb[:, par::2, :], in0=sb_[par][1][0:64, 0:128].rearrange("p (h t) -> p h t", h=2),
                                                                in1=cst['tri4'][:, 0:2, :], op=ALU.mult))
            banks.release(sb_[0][0], ev_am)
            banks.release(sb_[1][0], ev_am)
            bo, obk, bfree = banks.get()
            kb.wait('pe', bfree, ev_am, ev_v, ev_state[0], ev_state[1])
            for h in range(4):
                ins = nc.tensor.matmul(obk[0:64, h * 128:(h + 1) * 128], lhsT=amb[:, h, :], rhs=vbuf[:, h * 128:(h + 1) * 128], start=True, stop=True)
            o2 = []
            for par in range(2):
                b2, ob2, bfree = banks.get()
                kb.wait('pe', bfree)
                r_ = slice(par * 64, (par + 1) * 64)
                for hh in range(2):
                    ins = nc.tensor.matmul(ob2[0:64, hh * 128:(hh + 1) * 128], lhsT=qtT[hh][r_, rows], rhs=Sbf[hh][r_, c, :], start=True, stop=True)
                o2.append((b2, ob2))
            ev_ob = kb.mark('pe', ins)
            am.release(ai, ev_ob)
            kb.wait('pe', ev_kh)
            for hp in range(2):
                bk, kvb, bfree = banks.get()
                kb.wait('pe', bfree)
                ev_kv = kb.mark('pe', nc.tensor.matmul(kvb[:, 0:256], lhsT=kh[:, hp * 128:(hp + 1) * 128], rhs=vbuf[:, hp * 256:(hp + 1) * 256],
                                                       start=True, stop=True))
                kb.wait('dve', ev_kv)
                ev_S = kb.mark('dve', nc.vector.scalar_tensor_tensor(out=S[hp][:], in0=S[hp][:], scalar=elast[:, hp, c:c + 1], in1=kvb[:, 0:256],
                                                                    op0=ALU.mult, op1=ALU.add))
                banks.release(bk, ev_S)
                kb.wait('act', ev_S)
                nc.scalar.copy(out=Sbf[hp][0:64, c + 1, :], in_=S[hp][0:64, 0:128])
                ev_state[hp] = kb.mark('act', nc.scalar.copy(out=Sbf[hp][64:128, c + 1, :], in_=S[hp][64:128, 128:256]))
                kb.wait('dve', ev_state[hp])
            khat.release(hi, ev_kv)
            ldv.release(vi, ev_kv)
            fi, sq, ffree = f1.get()
            kb.wait('act', ev_ob, ffree)
            for par in range(2):
                ev_q = kb.mark('act', nc.scalar.copy(out=sq[0:64, :].rearrange("p (h d) -> p h d", h=4)[:, par::2, :],
                                                      in_=o2[par][1][0:64, 0:256].rearrange("p (h d) -> p h d", h=2)))
            banks.release(o2[0][0], ev_q)
            banks.release(o2[1][0], ev_q)
            kb.wait('dve', ev_q)
            ev_os = kb.mark('dve', nc.vector.tensor_tensor(out=sq[0:64, :], in0=obk[0:64, :], in1=sq[0:64, :], op=ALU.add))
            banks.release(bo, ev_os)
            si, smb, sfree = sm.get()
            kb.wait('act', ev_os, sfree)
            for h in range(4):
                ev = kb.mark('act', nc.scalar.activation(out=bjunk[:], in_=sq[0:64, h * 128:(h + 1) * 128], func=AF.Square, accum_out=smb[:, h:h + 1]))
            kb.wait('act', ev)
            ev = kb.mark('act', nc.scalar.activation(out=smb[:, 4:8], in_=smb[:, 0:4], func=AF.Sqrt, scale=1.0 / 128, bias=cst['eps'][0:64, :]))
            gi, sg, gfree = f2.get()
            kb.wait('act', ev_o, gfree)
            ev_sg = kb.mark('act', nc.scalar.activation(out=sg[0:64, :], in_=obuf[:], func=AF.Silu))
            ldo.release(oi, ev_sg)
            kb.wait('dve', ev)
            ev = kb.mark('dve', nc.vector.reciprocal(out=smb[:, 4:8], in_=smb[:, 4:8]))
            kb.wait('dve', ev)
            for h in range(4):
                nc.vector.tensor_scalar(out=sq[0:64, h * 128:(h + 1) * 128], in0=sq[0:64, h * 128:(h + 1) * 128], scalar1=smb[:, 4 + h:5 + h],
                                        scalar2=None, op0=ALU.mult)
            kb.wait('dve', ev_sg)
            nc.vector.tensor_tensor(out=sq[0:64, :], in0=sq[0:64, :], in1=sg[0:64, :], op=ALU.mult)
            yi, ybuf, yfree = yb.get()
            kb.wait('dve', yfree)
            ev_y = kb.mark('dve', nc.vector.tensor_tensor(out=ybuf[:], in0=sq[0:64, :], in1=gn[:], op=ALU.mult))
            f1.release(fi, ev_y)
            f2.release(gi, ev_y)
            sm.release(si, ev_y)
            ti, tb, tfree = tbank.get()
            kb.wait('pe', ev_y, tfree)
            for h in range(4):
                ins = nc.tensor.transpose(out=tb[:, h, :], in_=ybuf[:, h * 128:(h + 1) * 128], identity=cst['ident'][0:64, 0:64])
            ev_t = kb.mark('pe', ins)
            yb.release(yi, ev_t)
            kb.wait('act', ev_t)
            ev_c = kb.mark('act', nc.scalar.copy(out=yTst[:, :, rows], in_=tb[:]))
            tbank.release(ti, ev_c)
        kb.wait('sp', ev_c)
        ks = kb.newsem("bst")
        evs = [kb.dma('sp', dr['yT'][4 + h], yTst[:, h, :], ks) for h in range(4)]
        kb.barrier(extra=evs[-1:])


def mixer_ssd(kb, l, dr, cst):
    nc = kb.nc
    with ExitStack() as es:
        kp = kb.newsem("dp")
        scw = kb.sb("scw", [128, 8, 4], F32, es)
        scb = kb.sb("scb", [128, 8], F32, es)
        dtb = kb.sb("dtb", [64, 8], F32, es)
        aneg = kb.sb("aneg", [64, 8], F32, es)
        dsk = kb.sb("dsk", [64, 512], F32, es)
        nw = kb.sb("dnw", [64, 512], F32, es)
        kb.dma('sp', scw[:], dr['scw'][l], kp)
        kb.dma('sp', scb[:], dr['scb'][l], kp)
        kb.dma('sp', dtb[:], dr['dtb'][l, 0:64, :], kp)
        kb.dma('sp', aneg[:], dr['alog'][l, 0:64, :], kp)
        kb.dma('sp', dsk[:], dr['ssdd'][l, 0:64, :], kp)
        ev_p = kb.dma('sp', nw[:], dr['ssdn'][l, 0:64, :], kp)
        kb.wait('act', ev_p)
        ev = kb.mark('act', nc.scalar.activation(out=aneg[:], in_=aneg[:], func=AF.Exp))
        kb.wait('dve', ev, ev_p)
        ev_an = kb.mark('dve', nc.vector.tensor_scalar(out=aneg[:], in0=aneg[:], scalar1=-1.0, scalar2=None, op0=ALU.mult))
        kb.wait('pool', ev_p)
        xc = [kb.sb("xc", [128, T], BF16, es) for _ in range(8)]
        cbuf = [kb.sb("dcb", [128, 3 + T], F32, es) for _ in range(2)]
        cacc = [kb.sb("dca", [128, T], F32, es) for _ in range(2)]
        kc = [kb.newsem("dc"), kb.newsem("dc")]
        cfree = [None, None]
        afree = [None, None]
        ev_x = None
        for ch in range(8):
            p = ch % 2
            e = 'dve'
            eng = kb.E[e]
            kb.wait(e, cfree[p])
            ev_z = kb.mark(e, eng.memset(cbuf[p][:, 0:3], 0.0))
            kb.wait('sp', cfree[p])
            ev_l = kb.dma('sp', cbuf[p][:, 3:3 + T], dr['xbcT'][ch], kc[p])
            kb.wait(e, ev_l, ev_z, afree[p])
            eng.tensor_scalar(out=cacc[p][:], in0=cbuf[p][:, 0:T], scalar1=scw[:, ch, 0:1], scalar2=scb[:, ch:ch + 1], op0=ALU.mult, op1=ALU.add)
            for j in range(1, 4):
                ins = eng.scalar_tensor_tensor(out=cacc[p][:], in0=cbuf[p][:, j:j + T], scalar=scw[:, ch, j:j + 1], in1=cacc[p][:],
                                               op0=ALU.mult, op1=ALU.add)
            ev_a = kb.mark(e, ins)
            cfree[p] = ev_a
            kb.wait('act', ev_a)
            ev_x = kb.mark('act', nc.scalar.activation(out=xc[ch][:], in_=cacc[p][:], func=AF.Silu))
            afree[p] = ev_x
        kb.wait('pe', ev_x)
        H = [kb.sb("H", [128, 256], F32, es) for _ in range(2)]
        Hbf = [kb.sb("Hbf", [128, 256], BF16, es) for _ in range(2)]
        yTst = kb.sb("dyT", [128, 4, T], BF16, es)
        for g in range(2):
            nc.vector.memset(H[g][:], 0.0)
            ev_h0 = kb.mark('dve', nc.vector.memset(Hbf[g][:], 0.0))
        ev_hbf = [ev_h0, ev_h0]
        banks = Ring([kb.ps("db", [128, 512], F32, es) for _ in range(6)])
        tbank = Ring([kb.ps("dtb", [128, 1024], BF16, es) for _ in range(2)])
        ldt = Loader(kb, [kb.sb("dldt", [64, 8], F32, es) for _ in range(2)])
        ldz = Loader(kb, [kb.sb("dldz", [64, 512], F32, es) for _ in range(2)])
        xB = Ring([kb.sb("dxB", [64, 768], BF16, es) for _ in range(2)])
        sm = Ring([kb.sb("dsm", [128, 48], F32, es) for _ in range(2)])
        lab = Ring([kb.sb("dlab", [64, 8], BF16, es) for _ in range(2)])
        rseg = Ring([kb.sb("drs", [64, 8, 64], BF16, es) for _ in range(2)])
        Bw = Ring([kb.sb("dBw", [64, 8, 128], BF16, es) for _ in range(2)])
        Eb = Ring([kb.sb("dE", [64, 512], F32, es) for _ in range(2)])
        cbs = Ring([kb.sb("dcbs", [64, 128], F32, es) for _ in range(2)])
        MT = Ring([kb.sb("dMT", [64, 8, 64], BF16, es) for _ in range(2)])
        ydsb = Ring([kb.sb("dyd", [64, 512], F32, es) for _ in range(2)])
        ybuf = Ring([kb.sb("dy", [64, 512], F32, es) for _ in range(2)])
        szb = Ring([kb.sb("dsz", [64, 512], F32, es) for _ in range(2)])
        yo16 = Ring([kb.sb("dyo", [64, 512], BF16, es) for _ in range(2)])
        junk = kb.sb("djunk", [64, 256], F32, es)
        kb.wait('dve', ev_an)
        for c in range(32):
            rows = slice(c * 64, (c + 1) * 64)
            di, dtr, ev_d = ldt.load(lambda b: [(b[:], dr['dt'][rows, :])])
            zi, zb, ev_zl = ldz.load(lambda b: [(b[:], dr['z'][rows, :])])
            ti, tb, tfree = tbank.get()
            kb.wait('pe', tfree)
            for k in range(6):
                ins = nc.tensor.transpose(out=tb[0:64, k * 128:(k + 1) * 128], in_=xc[k][:, rows], identity=cst['ident'][:])
            ev_t = kb.mark('pe', ins)
            xi, xb, xfree = xB.get()
            kb.wait('act', ev_t, xfree)
            ev_xb = kb.mark('act', nc.scalar.copy(out=xb[:], in_=tb[0:64, 0:768]))
            tbank.release(ti, ev_xb)
            si, s_, sfree = sm.get()
            kb.wait('dve', ev_d, sfree)
            ev = kb.mark('dve', nc.vector.tensor_tensor(out=s_[0:64, 0:8], in0=dtr[:], in1=dtb[:], op=ALU.add))
            ldt.release(di, ev)
            kb.wait('act', ev)
            ev = kb.mark('act', nc.scalar.activation(out=s_[0:64, 0:8], in_=s_[0:64, 0:8], func=AF.Exp))
            kb.wait('act', ev)
            ev = kb.mark('act', nc.scalar.activation(out=s_[0:64, 0:8], in_=s_[0:64, 0:8], func=AF.Ln, bias=cst['one'][0:64, :]))
            kb.wait('dve', ev)
            ev = kb.mark('dve', nc.vector.tensor_tensor(out=s_[0:64, 8:16], in0=s_[0:64, 0:8], in1=aneg[:], op=ALU.mult))
            kb.wait('dve', ev)
            li, lb_, lfree = lab.get()
            kb.wait('dve', lfree)
            ev_la = kb.mark('dve', nc.vector.tensor_copy(out=lb_[:], in_=s_[0:64, 8:16]))
            ri, rs_, rfree = rseg.get()
            kb.wait('act', ev_la, rfree)
            for e8 in range(8):
                ins = nc.scalar.mul(out=rs_[:, e8, :], in_=cst['tri_f'][:], mul=s_[0:64, 8 + e8:9 + e8])
            ev_rs = kb.mark('act', ins)
            bi, cb_, bfree = banks.get()
            kb.wait('pe', ev_la, bfree)
            nc.tensor.matmul(cb_[0:64, 0:8], lhsT=cst['tri_b'][:], rhs=lb_[:], start=True, stop=True)
            nc.tensor.matmul(cb_[0:64, 8:16], lhsT=cst['slow_b'][:], rhs=lb_[:], start=True, stop=True)
            ev_cm = kb.mark('pe', nc.tensor.matmul(cb_[:, 16:24], lhsT=cst['ones_b'][0:64, :], rhs=lb_[:], start=True, stop=True))
            lab.release(li, ev_cm)
            kb.wait('act', ev_cm)
            nc.scalar.activation(out=s_[0:64, 16:32], in_=cb_[0:64, 0:16], func=AF.Exp)
            ev_ex = kb.mark('act', nc.scalar.activation(out=s_[:, 32:40], in_=cb_[:, 16:24], func=AF.Exp))
            banks.release(bi, ev_ex)
            kb.wait('dve', ev_ex)
            ev_w = kb.mark('dve', nc.vector.tensor_tensor(out=s_[0:64, 40:48], in0=s_[0:64, 24:32], in1=s_[0:64, 0:8], op=ALU.mult))
            wi, bw, wfree = Bw.get()
            kb.wait('act', ev_w, ev_xb, wfree)
            for e8 in range(8):
                g = e8 // 4
                ins = nc.scalar.mul(out=bw[:, e8, :], in_=xb[:, 512 + g * 128:512 + (g + 1) * 128], mul=s_[0:64, 40 + e8:41 + e8])
            ev_bw = kb.mark('act', ins)
            b1, segb, bfree = banks.get()
            kb.wait('pe', ev_rs, bfree)
            for g in range(2):
                nc.tensor.matmul(segb[0:64, g * 256:(g + 1) * 256], lhsT=cst['slow_b'][:], rhs=rs_[:, g * 4:(g + 1) * 4, :], start=True, stop=False)
                ins = nc.tensor.matmul(segb[0:64, g * 256:(g + 1) * 256], lhsT=cst['ident'][0:64, 0:64], rhs=cst['negm4'][:], start=False, stop=True)
            ev_sg = kb.mark('pe', ins)
            rseg.release(ri, ev_sg)
            b2, cbb, bfree = banks.get()
            kb.wait('pe', bfree)
            for g in range(2):
                ins = nc.tensor.matmul(cbb[0:64, g * 64:(g + 1) * 64], lhsT=xc[4 + g][:, rows], rhs=xc[6 + g][:, rows], start=True, stop=True)
            ev_cb = kb.mark('pe', ins)
            ei, E, efree = Eb.get()
            ci, cs_, cfree2 = cbs.get()
            kb.wait('act', ev_sg, ev_cb, efree, cfree2)
            nc.scalar.activation(out=E[:], in_=segb[0:64, :], func=AF.Exp)
            ev_E = kb.mark('act', nc.scalar.copy(out=cs_[:], in_=cbb[0:64, 0:128]))
            banks.release(b1, ev_E)
            banks.release(b2, ev_E)
            mi, mt, mfree = MT.get()
            kb.wait('dve', ev_E, mfree)
            for e8 in range(8):
                g = e8 // 4
                ins = nc.vector.scalar_tensor_tensor(out=mt[:, e8, :], in0=E[:, e8 * 64:(e8 + 1) * 64], scalar=s_[0:64, e8:e8 + 1],
                                                     in1=cs_[:, g * 64:(g + 1) * 64], op0=ALU.mult, op1=ALU.mult)
            ev_mt = kb.mark('dve', ins)
            Eb.release(ei, ev_mt)
            cbs.release(ci, ev_mt)
            b3, ydb, bfree = banks.get()
            kb.wait('pe', ev_mt, ev_xb, bfree)
            for e8 in range(8):
                ins = nc.tensor.matmul(ydb[0:64, e8 * 64:(e8 + 1) * 64], lhsT=mt[:, e8, :], rhs=xb[:, e8 * 64:(e8 + 1) * 64], start=True, stop=True)
            ev_yd = kb.mark('pe', ins)
            MT.release(mi, ev_yd)
            b4, yob, bfree = banks.get()
            kb.wait('pe', bfree, ev_hbf[0], ev_hbf[1])
            for g in range(2):
                ins = nc.tensor.matmul(yob[0:64, g * 256:(g + 1) * 256], lhsT=xc[6 + g][:, rows], rhs=Hbf[g][:], start=True, stop=True)
            ev_yo = kb.mark('pe', ins)
            b5, csb, bfree = banks.get()
            kb.wait('pe', ev_bw, bfree)
            for e8 in range(8):
                ins = nc.tensor.matmul(csb[:, e8 * 64:(e8 + 1) * 64], lhsT=bw[:, e8, :], rhs=xb[:, e8 * 64:(e8 + 1) * 64], start=True, stop=True)
            ev_cs = kb.mark('pe', ins)
            Bw.release(wi, ev_cs)
            kb.wait('dve', ev_cs)
            for e8 in range(8):
                g = e8 // 4
                cs2 = slice((e8 % 4) * 64, (e8 % 4 + 1) * 64)
                ins = nc.vector.scalar_tensor_tensor(out=H[g][:, cs2], in0=H[g][:, cs2], scalar=s_[:, 32 + e8:33 + e8],
                                                     in1=csb[:, e8 * 64:(e8 + 1) * 64], op0=ALU.mult, op1=ALU.add)
            ev_H = kb.mark('dve', ins)
            banks.release(b5, ev_H)
            kb.wait('act', ev_H, ev_yo)
            nc.scalar.copy(out=Hbf[0][:], in_=H[0][:])
            ev_hb = kb.mark('act', nc.scalar.copy(out=Hbf[1][:], in_=H[1][:]))
            ev_hbf = [ev_hb, ev_hb]
            kb.wait('dve', ev_hb)
            yi_, ydv, yfree = ydsb.get()
            kb.wait('act', ev_yd, yfree)
            ev_ydc = kb.mark('act', nc.scalar.copy(out=ydv[:], in_=ydb[0:64, :]))
            banks.release(b3, ev_ydc)
            yb_i, y, ybfree = ybuf.get()
            kb.wait('dve', ev_ydc, ev_yo, ybfree)
            for e8 in range(8):
                cs2 = slice(e8 * 64, (e8 + 1) * 64)
                ins = nc.vector.scalar_tensor_tensor(out=y[:, cs2], in0=yob[0:64, cs2], scalar=s_[0:64, 16 + e8:17 + e8], in1=ydv[:, cs2],
                                                     op0=ALU.mult, op1=ALU.add)
            ev_c1 = kb.mark('dve', ins)
            banks.release(b4, ev_c1)
            ydsb.release(yi_, ev_c1)
            zi2, sz, zfree = szb.get()
            kb.wait('act', ev_zl, zfree)
            ev_sz = kb.mark('act', nc.scalar.activation(out=sz[:], in_=zb[:], func=AF.Silu))
            ldz.release(zi, ev_sz)
            tmpd = ydv
            nc.vector.tensor_tensor(out=tmpd[:], in0=xb[:, 0:512], in1=dsk[:], op=ALU.mult)
            nc.vector.tensor_tensor(out=y[:], in0=y[:], in1=tmpd[:], op=ALU.add)
            kb.wait('dve', ev_sz)
            ev_y3 = kb.mark('dve', nc.vector.tensor_tensor(out=y[:], in0=y[:], in1=sz[:], op=ALU.mult))
            ydsb.release(yi_, ev_y3)
            xB.release(xi, ev_y3)
            szb.release(zi2, ev_y3)
            kb.wait('act', ev_y3)
            for g in range(2):
                ev = kb.mark('act', nc.scalar.activation(out=junk[:], in_=y[:, g * 256:(g + 1) * 256], func=AF.Square, accum_out=s_[0:64, 24 + g:25 + g]))
            kb.wait('act', ev)
            ev = kb.mark('act', nc.scalar.activation(out=s_[0:64, 26:28], in_=s_[0:64, 24:26], func=AF.Sqrt, scale=1.0 / 256, bias=cst['eps'][0:64, :]))
            kb.wait('dve', ev)
            ev = kb.mark('dve', nc.vector.reciprocal(out=s_[0:64, 26:28], in_=s_[0:64, 26:28]))
            kb.wait('dve', ev)
            oi, yo_, ofree = yo16.get()
            kb.wait('dve', ofree)
            for g in range(2):
                ins = nc.vector.scalar_tensor_tensor(out=yo_[:, g * 256:(g + 1) * 256], in0=y[:, g * 256:(g + 1) * 256], scalar=s_[0:64, 26 + g:27 + g],
                                                     in1=nw[:, g * 256:(g + 1) * 256], op0=ALU.mult, op1=ALU.mult)
            ev_yf = kb.mark('dve', ins)
            ybuf.release(yb_i, ev_yf)
            sm.release(si, ev_yf)
            ti, tb, tfree = tbank.get()
            kb.wait('pe', ev_yf, tfree)
            for h in range(4):
                ins = nc.tensor.transpose(out=tb[:, h * 64:(h + 1) * 64], in_=yo_[:, h * 128:(h + 1) * 128], identity=cst['ident'][0:64, 0:64])
            ev_t2 = kb.mark('pe', ins)
            yo16.release(oi, ev_t2)
            kb.wait('act', ev_t2)
            ev_c = kb.mark('act', nc.scalar.copy(out=yTst[:, :, rows], in_=tb[:, 0:256].rearrange("p (h t) -> p h t", h=4)))
            tbank.release(ti, ev_c)
        kb.wait('sp', ev_c)
        ks = kb.newsem("dst")
        evs = [kb.dma('sp', dr['yT'][12 + h], yTst[:, h, :], ks) for h in range(4)]
        kb.barrier(extra=evs[-1:] + [('pool', kb.pcnt['pool'])])

def make_consts(kb):
    nc = kb.nc
    c = {}
    g = nc.gpsimd
    ones_f = kb.sb("onesf", [128, 128], F32)
    g.memset(ones_f[:], 1.0)
    zeros_f = kb.sb("zerosf", [128, 256], F32)
    g.memset(zeros_f[:], 0.0)
    ident_f = kb.sb("identf", [128, 128], F32)
    g.affine_select(out=ident_f[:], in_=ones_f[:], pattern=[[-1, 128]], compare_op=ALU.is_equal, fill=0.0, base=0, channel_multiplier=1)
    ident = kb.sb("ident", [128, 128], BF16)
    g.tensor_copy(out=ident[:], in_=ident_f[:])
    ones_b = kb.sb("onesb", [128, 128], BF16)
    g.tensor_copy(out=ones_b[:], in_=ones_f[:])
    ob = kb.sb("onesblk", [128, 128], BF16)
    g.memset(ob[:], 0.0)
    g.memset(ob[0:64, 0:64], 1.0)
    g.memset(ob[64:128, 64:128], 1.0)
    eps = kb.sb("epsc", [128, 1], F32)
    g.memset(eps[:], EPS)
    one = kb.sb("onec", [128, 1], F32)
    g.memset(one[:], 1.0)
    tri_f = kb.sb("trif", [64, 64], F32)
    g.affine_select(out=tri_f[:], in_=ones_f[0:64, 0:64], pattern=[[1, 64]], compare_op=ALU.is_ge, fill=0.0, base=0, channel_multiplier=-1)
    tri_b = kb.sb("trib", [64, 64], BF16)
    g.tensor_copy(out=tri_b[:], in_=tri_f[:])
    slow_f = kb.sb("slowf", [64, 64], F32)
    g.affine_select(out=slow_f[:], in_=ones_f[0:64, 0:64], pattern=[[-1, 64]], compare_op=ALU.is_gt, fill=0.0, base=0, channel_multiplier=1)
    slow_b = kb.sb("slowb", [64, 64], BF16)
    g.tensor_copy(out=slow_b[:], in_=slow_f[:])
    tri4 = kb.sb("tri4", [64, 4, 64], F32)
    for a in range(4):
        g.tensor_copy(out=tri4[:, a, :], in_=tri_f[:])
    negf = kb.sb("negf", [64, 64], F32)
    g.affine_select(out=negf[:], in_=zeros_f[0:64, 0:64], pattern=[[1, 64]], compare_op=ALU.is_ge, fill=NEG, base=0, channel_multiplier=-1)
    negm4 = kb.sb("negm4", [64, 256], BF16)
    for a in range(4):
        ins = g.tensor_copy(out=negm4[:, a * 64:(a + 1) * 64], in_=negf[:])
    ev = kb.mark('pool', ins)
    c.update(ident=ident, ident_f=ident_f, ones_f=ones_f, ones_b=ones_b, onesblk=ob, eps=eps, one=one, tri_f=tri_f, tri_b=tri_b,
             slow_b=slow_b, tri4=tri4, negm4=negm4, ev=ev)
    return c


def prep_host(inputs):
    f = np.float32
    P = {}
    rep = lambda a, n=128: np.ascontiguousarray(np.broadcast_to(a[:, None, :], (a.shape[0], n, a.shape[-1])))
    w_in = inputs['w_in']
    win = np.zeros((L, 13, 128, 16, 512), f)
    for g, cols in enumerate(WIN_GROUPS):
        sub = w_in[:, :, cols]
        win[:, g, :, :, 0:len(cols)] = sub.reshape(L, 16, 128, len(cols)).transpose(0, 2, 1, 3)
    P['win'] = win
    q = inputs['diff_qk_norm']
    P['qkg'] = np.ascontiguousarray(np.concatenate([q, q], axis=2).transpose(0, 2, 1))
    P['mixn'] = rep(inputs['mix_norm'])
    P['ffnn'] = rep(inputs['ffn_norm'])
    P['lam'] = rep(inputs['diff_lambda'].reshape(L, 256))
    P['subln'] = rep(inputs['diff_subln'])
    w2a = np.zeros((L, 32, 256), f)
    w2a[:, 0:16] = inputs['gla_gk_w2']
    w2a[:, 16] = inputs['gla_gk_b']
    P['w2aug'] = w2a
    P['glan'] = rep(np.tile(inputs['gla_norm'], (1, 4)))
    P['cdw'] = np.ascontiguousarray(inputs['conv_dw_w'].reshape(L, 31, 4, 128).transpose(0, 3, 2, 1))
    fm = lambda a, n: np.ascontiguousarray(a.reshape(L, n, 128).transpose(0, 2, 1))
    P['cdb'] = fm(inputs['conv_dw_b'], 4)
    P['clnw'] = fm(inputs['conv_ln_w'], 4)
    P['clnb'] = fm(inputs['conv_ln_b'], 4)
    P['scw'] = np.ascontiguousarray(inputs['ssd_conv_w'].reshape(L, 4, 8, 128).transpose(0, 3, 2, 1))
    P['scb'] = fm(inputs['ssd_conv_b'], 8)
    P['dtb'] = rep(inputs['ssd_dt_bias'])
    P['alog'] = rep(inputs['ssd_a_log'])
    P['ssdd'] = rep(np.repeat(inputs['ssd_d'], 64, axis=1))
    P['ssdn'] = rep(inputs['ssd_norm'])
    wg = inputs['w_gate'].reshape(L, 4, 16, 128, 16, 128)
    wb = inputs['w_branch'].reshape(L, 4, 4, 128, 16, 128)
    wgb = np.empty((L, 16, 4, 128, 20, 128), f)
    wgb[:, :, :, :, 0:16, :] = wg.transpose(0, 4, 1, 3, 2, 5)
    wgb[:, :, :, :, 16:20, :] = wb.transpose(0, 4, 1, 3, 2, 5)
    P['wgb'] = wgb
    P['bgate'] = np.ascontiguousarray(inputs['b_gate'].reshape(L, 4, 16, 128).transpose(0, 3, 1, 2))
    P['wout'] = np.ascontiguousarray(inputs['w_out'].reshape(L, 16, 128, 4, 512).transpose(0, 3, 2, 1, 4))
    up = inputs['ffn_w_up'].reshape(L, 16, 128, 2, NPAIR, 128)
    wup = np.empty((L, 22, 128, 16, 4, 128), f)
    upr = up.transpose(0, 4, 2, 1, 3, 5)
    upr = upr.reshape(L, 22, 2, 128, 16, 2, 128)
    wup[:] = upr.transpose(0, 1, 3, 4, 2, 5, 6).reshape(L, 22, 128, 16, 4, 128)
    P['wup'] = wup.reshape(L, 22, 128, 16, 512)
    fcw = inputs['ffn_conv_w'].reshape(L, 3, 2, NPAIR, 128)
    P['fcw'] = np.ascontiguousarray(fcw.transpose(0, 4, 3, 2, 1).reshape(L, 128, 88, 3))
    fcb = inputs['ffn_conv_b'].reshape(L, 2, NPAIR, 128)
    P['fcb'] = np.ascontiguousarray(fcb.transpose(0, 3, 2, 1).reshape(L, 128, 88))
    P['wdn'] = np.ascontiguousarray(inputs['ffn_w_down'].reshape(L, NPAIR, 128, 4, 512).transpose(0, 3, 2, 1, 4))
    return P


SCRATCH = dict(
    qT=([4, 128, T], BF16), kT=([4, 128, T], BF16), v=([T, 512], BF16),
    gqkT=([4, 128, T], F32), gk=([T, 256], F32), gv=([T, 512], BF16), gout=([T, 512], F32), glowT=([16, T], F32),
    gluT=([4, 128, T], F32), xbcT=([8, 128, T], F32), z=([T, 512], F32), dt=([T, 8], F32),
    yT=([16, 128, T], BF16), mT=([16, 128, T], BF16), aT=([8, 128, NPAIR, 256], BF16),
    xm=([T, D], F32), xc=([T, D], F32),
)

PARAM_SHAPES = dict(
    win=[L, 13, 128, 16, 512], qkg=[L, 128, 2], mixn=[L, 128, D], ffnn=[L, 128, D], lam=[L, 128, 256], subln=[L, 128, 128],
    w2aug=[L, 32, 256], glan=[L, 128, 512], cdw=[L, 128, 4, 31], cdb=[L, 128, 4], clnw=[L, 128, 4], clnb=[L, 128, 4],
    scw=[L, 128, 8, 4], scb=[L, 128, 8], dtb=[L, 128, 8], alog=[L, 128, 8], ssdd=[L, 128, 512], ssdn=[L, 128, 512],
    wgb=[L, 16, 4, 128, 20, 128], bgate=[L, 128, 4, 16], wout=[L, 4, 128, 16, 512], wup=[L, 22, 128, 16, 512],
    fcw=[L, 128, 88, 3], fcb=[L, 128, 88], wdn=[L, 4, 128, NPAIR, 512],
)


def build(dbg_outs=(), nlayers=L, stop_after=None, skip=()):
    nc = bass.Bass("TRN2", target_bir_lowering=False)
    dr = {}
    x_in = nc.dram_tensor("x", [T, D], F32, kind="ExternalInput").ap()
    for name, shape in PARAM_SHAPES.items():
        dr[name] = nc.dram_tensor(name, shape, F32, kind="ExternalInput").ap()
    y_out = nc.dram_tensor("y", [T, D], F32, kind="ExternalOutput").ap()
    for name, (shape, dt_) in SCRATCH.items():
        kind = "ExternalOutput" if name in dbg_outs else "Internal"
        dr[name] = nc.dram_tensor("s_" + name, shape, dt_, kind=kind).ap()
    kb = KB(nc)
    cst = make_consts(kb)
    for e in ('pe', 'act', 'dve', 'sp'):
        kb.wait(e, cst['ev'])
    for l in range(nlayers):
        last = (l == nlayers - 1)
        lam_init = 0.8 - 0.6 * math.exp(-0.3 * l)
        x_src = x_in if l == 0 else dr['xc']
        x_dst = y_out if last else dr['xc']
        with ExitStack() as hes:
            hT = kb.sb("hT", [128, KC, T], BF16, hes)
            phase_norm(kb, x_src, dr['mixn'][l], hT, cst['ident'], cst['eps'])
            phase_inproj(kb, l, hT, dr, cst)
            if 'A' not in skip:
                mixer_attn(kb, l, dr, cst, lam_init, bg=(mixer_conv_gen(kb, l, dr, cst) if 'C' not in skip else None))
            elif 'C' not in skip:
                mixer_conv(kb, l, dr, cst)
            if 'B' not in skip:
                mixer_gla(kb, l, dr, cst)
            if 'D' not in skip:
                mixer_ssd(kb, l, dr, cst)
            if stop_after == 'mix':
                break
            phase_gates(kb, l, hT, dr, cst)
        phase_outproj(kb, l, x_src, dr['xm'], dr, cst)
        with ExitStack() as hes:
            hT = kb.sb("hT", [128, KC, T], BF16, hes)
            phase_norm(kb, dr['xm'], dr['ffnn'][l], hT, cst['ident'], cst['eps'])
            phase_ffn_up(kb, l, hT, dr, cst)
        phase_ffn_down(kb, l, dr['xm'], x_dst, dr, cst)
    kb.es.close()
    return nc


_NC_CACHE = {}


def kernel(**inputs):
    inputs = {k: np.asarray(v) for k, v in inputs.items()}
    P = prep_host(inputs)
    if 'nc' not in _NC_CACHE:
        _NC_CACHE['nc'] = build()
    nc = _NC_CACHE['nc']
    x = np.ascontiguousarray(inputs['x'], dtype=np.float32)
    n_cores = 4
    in_maps = []
    for c in range(n_cores):
        m = dict(P)
        m['x'] = x[c]
        in_maps.append(m)
    res = run_bass_kernel_spmd(nc, in_maps, core_ids=list(range(n_cores)))
    out = np.stack([np.asarray(res.results[c]['y'], dtype=np.float32) for c in range(n_cores)], axis=0)
    return out
```

```python
import math
import numpy as np
from contextlib import ExitStack
import concourse.bass as bass
import concourse.mybir as mybir
from concourse.bass_utils import run_bass_kernel_spmd

F32 = mybir.dt.float32
BF16 = mybir.dt.bfloat16
AF = mybir.ActivationFunctionType
ALU = mybir.AluOpType
AX = mybir.AxisListType

T = 2048
D = 2048
KC = 16
L = 4
EPS = 1e-6
DFF = 5632
NPAIR = 44
NEG = -30000.0

OFF = dict(aq=0, ak=512, av=1024, bq=1536, bk=1792, bv=2048, bo=2560, bl=3072,
           ca=3088, cg=3600, dz=4112, dx=4624, dt=5648)


def _win_groups():
    r = lambda a, n: list(range(a, a + n))
    g = []
    g.append(r(OFF['aq'], 512))
    g.append(r(OFF['ak'], 512))
    g.append(r(OFF['bq'], 256) + r(OFF['bk'], 256))
    g.append(r(OFF['ca'], 128) + r(OFF['cg'], 128) + r(OFF['ca'] + 128, 128) + r(OFF['cg'] + 128, 128))
    g.append(r(OFF['ca'] + 256, 128) + r(OFF['cg'] + 256, 128) + r(OFF['ca'] + 384, 128) + r(OFF['cg'] + 384, 128))
    g.append(r(OFF['dx'], 512))
    g.append(r(OFF['dx'] + 512, 512))
    g.append(r(OFF['bl'], 16))
    g.append(r(OFF['av'], 512))
    g.append(r(OFF['bv'], 512))
    g.append(r(OFF['bo'], 512))
    g.append(r(OFF['dz'], 512))
    g.append(r(OFF['bk'], 256) + r(OFF['dt'], 8))
    return g


WIN_GROUPS = _win_groups()
WIN_NCOLS = [len(g) for g in WIN_GROUPS]


class KB:
    def __init__(self, nc):
        self.nc = nc
        self.es = ExitStack()
        self.E = dict(pe=nc.tensor, act=nc.scalar, dve=nc.vector, pool=nc.gpsimd, sp=nc.sync)
        self.psem = {}
        self.pcnt = {}
        for e in self.E:
            self.psem[e] = self.es.enter_context(nc.semaphore("p_" + e))
            self.pcnt[e] = 0
        self.waited = {}
        self.nsem = 0
        self.nname = 0

    def newsem(self, name):
        if not hasattr(self, 'live'):
            self.live = []
            self.freelist = []
            self.pinned = []
            self.pinning = False
        if self.freelist:
            key = self.freelist.pop()
            (self.pinned if self.pinning else self.live).append(key)
            return key
        self.nsem += 1
        key = f"d:{name}_{self.nsem}"
        self.psem[key] = self.es.enter_context(self.nc.semaphore(f"{name}_{self.nsem}"))
        self.pcnt[key] = 0
        (self.pinned if self.pinning else self.live).append(key)
        return key

    def sb(self, name, shape, dt, es=None):
        self.nname += 1
        return (es or self.es).enter_context(self.nc.sbuf_tensor(f"{name}_{self.nname}", shape, dt))

    def ps(self, name, shape, dt, es=None):
        self.nname += 1
        return (es or self.es).enter_context(self.nc.psum_tensor(f"{name}_{self.nname}", shape, dt))

    def mark(self, e, ins):
        ins.then_inc(self.psem[e], 1)
        self.pcnt[e] += 1
        return (e, self.pcnt[e])

    def wait(self, e, *evs):
        for ev in evs:
            if ev is None:
                continue
            if isinstance(ev, list):
                self.wait(e, *ev)
                continue
            key, val = ev
            if self.waited.get((e, key), 0) >= val:
                continue
            self.E[e].wait_ge(self.psem[key], val)
            self.waited[(e, key)] = val

    def dma(self, q, out, in_, key, **kw):
        ins = self.E[q].dma_start(out=out, in_=in_, **kw)
        ins.then_inc(self.psem[key], 16)
        self.pcnt[key] += 16
        return (key, self.pcnt[key])

    def tick(self, n=1):
        for _ in range(n):
            if getattr(self, 'bg', None) is None:
                return
            try:
                next(self.bg)
            except StopIteration:
                self.bg = None

    def drain(self):
        while getattr(self, 'bg', None) is not None:
            self.tick()

    def barrier(self, extra=()):
        engines = ('pe', 'act', 'dve', 'sp')
        evs = [(e, self.pcnt[e]) for e in engines if self.pcnt[e] > 0] + [e for e in extra if e is not None]
        for e in engines:
            self.wait(e, *evs)
        self.last_barrier = evs
        if hasattr(self, 'live'):
            self.freelist.extend(self.live)
            self.live = []


class Ring:
    def __init__(self, bufs):
        self.bufs = bufs
        self.free = [None] * len(bufs)
        self.i = 0

    def get(self):
        i = self.i
        self.i = (self.i + 1) % len(self.bufs)
        return i, self.bufs[i], self.free[i]

    def release(self, i, ev):
        self.free[i] = ev


class WStream:
    def __init__(self, kb, slots, loads):
        self.kb = kb
        self.slots = slots
        self.n = len(slots)
        self.loads = loads
        self.keys = [kb.newsem("w") for _ in slots]
        self.free = [None] * self.n
        self.ev = {}
        kb.wait('pool', getattr(kb, 'last_barrier', None))
        for i in range(min(self.n, len(loads))):
            self._issue(i)

    def _issue(self, i):
        s = i % self.n
        self.kb.wait('pool', self.free[s])
        dram, view = self.loads[i]
        self.ev[i] = self.kb.dma('pool', view(self.slots[s]), dram, self.keys[s])

    def get(self, i):
        return self.slots[i % self.n], self.ev[i]

    def release(self, i, ev):
        self.free[i % self.n] = ev
        if i + self.n < len(self.loads):
            self._issue(i + self.n)


class Stager:
    def __init__(self, kb, bufs, q='sp'):
        self.kb = kb
        self.ring = Ring(bufs)
        self.keys = [kb.newsem("st") for _ in bufs]
        self.q = q
        self.last = []

    def get(self):
        return self.ring.get()

    def store(self, i, dst, src, ev):
        self.kb.wait(self.q, ev)
        e = self.kb.dma(self.q, dst, src, self.keys[i])
        self.ring.release(i, e)
        self.last.append(e)
        self.last = self.last[-len(self.keys):]
        return e


class Loader:
    def __init__(self, kb, bufs, q='sp'):
        self.kb = kb
        self.ring = Ring(bufs)
        self.keys = [kb.newsem("ld") for _ in bufs]
        self.q = q

    def load(self, fn):
        i, buf, free = self.ring.get()
        self.kb.wait(self.q, free)
        ev = None
        for o, s in fn(buf):
            ev = self.kb.dma(self.q, o, s, self.keys[i])
        return i, buf, ev

    def release(self, i, ev):
        self.ring.release(i, ev)


def phase_norm(kb, x_src, wb_dram, hT, ident, cst_eps):
    nc = kb.nc
    with ExitStack() as es:
        wb = kb.sb("nw", [128, D], F32, es)
        kw = kb.newsem("nw")
        ev_w = kb.dma('sp', wb[:], wb_dram, kw)
        ld = Loader(kb, [kb.sb("nx", [128, D], F32, es) for _ in range(2)])
        xn = Ring([kb.sb("nxn", [128, D], BF16, es) for _ in range(2)])
        junk = kb.sb("njunk", [128, D], BF16, es)
        ss = kb.sb("nss", [128, 16], F32, es)
        rstd = kb.sb("nrstd", [128, 16], F32, es)
        pst = Ring([kb.ps("npt", [128, 16, 128], BF16, es) for _ in range(2)])
        kb.wait('dve', ev_w)
        for tt in range(T // 128):
            li, xt, ev_x = ld.load(lambda b: [(b[:], x_src[tt * 128:(tt + 1) * 128, :])])
            kb.wait('act', ev_x)
            ev_sq = kb.mark('act', nc.scalar.activation(out=junk[:], in_=xt[:], func=AF.Square,
                                                         accum_out=ss[:, tt:tt + 1]))
            kb.wait('act', ev_sq)
            ev_sq = kb.mark('act', nc.scalar.activation(out=rstd[:, tt:tt + 1], in_=ss[:, tt:tt + 1], func=AF.Sqrt,
                                                         scale=1.0 / D, bias=cst_eps[:]))
            kb.wait('dve', ev_sq, ev_x)
            ev_rc = kb.mark('dve', nc.vector.reciprocal(out=rstd[:, tt:tt + 1], in_=rstd[:, tt:tt + 1]))
            kb.wait('dve', ev_rc)
            xi, xnb, xfree = xn.get()
            kb.wait('dve', xfree)
            ev_xn = kb.mark('dve', nc.vector.scalar_tensor_tensor(out=xnb[:], in0=xt[:], scalar=rstd[:, tt:tt + 1],
                                                                  in1=wb[:], op0=ALU.mult, op1=ALU.mult))
            ld.release(li, ev_xn)
            pi, pt, pfree = pst.get()
            kb.wait('pe', ev_xn, pfree)
            for c in range(KC):
                ins = nc.tensor.transpose(out=pt[:, c, :], in_=xnb[:, c * 128:(c + 1) * 128], identity=ident[:])
            ev_t = kb.mark('pe', ins)
            xn.release(xi, ev_t)
            e = 'act' if tt % 2 == 0 else 'dve'
            kb.wait(e, ev_t)
            if e == 'act':
                ins = nc.scalar.copy(out=hT[:, :, tt * 128:(tt + 1) * 128], in_=pt[:])
            else:
                ins = nc.vector.tensor_copy(out=hT[:, :, tt * 128:(tt + 1) * 128], in_=pt[:])
            pst.release(pi, kb.mark(e, ins))
        kb.barrier()


def phase_inproj(kb, l, hT, dr, cst):
    nc = kb.nc
    with ExitStack() as es:
        wsl = [kb.sb("w", [128, 16, 512], BF16, es) for _ in range(2)]
        loads = [(dr['win'][l, g, :, :, 0:WIN_NCOLS[g]], (lambda s, n=WIN_NCOLS[g]: s[:, :, 0:n])) for g in range(13)]
        ws = WStream(kb, wsl, loads)
        banks = Ring([kb.ps("b", [128, 512], F32, es) for _ in range(6)])
        sbank = Ring([kb.ps("sb", [128, 512], F32, es) for _ in range(2)])
        stf = Stager(kb, [kb.sb("stf", [128, 512], F32, es) for _ in range(3)])
        stb = Stager(kb, [kb.sb("stb", [128, 512], BF16, es) for _ in range(3)])
        sqr = Ring([kb.sb("sq", [128, 512], BF16, es) for _ in range(3)])
        rr = Ring([kb.sb("rr", [128, 512], F32, es) for _ in range(2)])
        sig = Ring([kb.sb("sig", [128, 512], F32, es) for _ in range(2)])
        qkg = kb.sb("qkg", [128, 2], F32, es)
        kq = kb.newsem("qkg")
        ev_g = kb.dma('sp', qkg[:], dr['qkg'][l], kq)
        kb.wait('dve', ev_g)
        ev_qkg = kb.mark('dve', nc.vector.tensor_scalar(out=qkg[:, 0:1], in0=qkg[:, 0:1], scalar1=0.125, scalar2=None, op0=ALU.mult))
        kb.wait('dve', ev_qkg)
        cnt = [0]

        def cp_eng():
            cnt[0] += 1
            return 'act' if cnt[0] % 2 == 0 else 'dve'

        def copy_out(e, out, in_):
            if e == 'act':
                return nc.scalar.copy(out=out, in_=in_)
            return nc.vector.tensor_copy(out=out, in_=in_)

        def mm_fm(wt, cc, tt, bank, m=128):
            for k in range(KC):
                ins = nc.tensor.matmul(bank[0:m, :], lhsT=wt[:, k, cc * 128:cc * 128 + m],
                                       rhs=hT[:, k, tt * 512:(tt + 1) * 512], start=(k == 0), stop=(k == KC - 1))
            return kb.mark('pe', ins)

        pending = []

        def flush_pending():
            while pending:
                pending.pop(0)()

        for g in (0, 1):
            wt, ev = ws.get(g)
            kb.wait('pe', ev)
            dst = dr['qT'] if g == 0 else dr['kT']
            for cc in range(4):
                for tt in range(4):
                    bi, bank, bfree = banks.get()
                    kb.wait('pe', bfree)
                    ev_mm = mm_fm(wt, cc, tt, bank)
                    flush_pending()
                    si, sq, sfree = sqr.get()
                    kb.wait('act', ev_mm, sfree)
                    ev_sq = kb.mark('act', nc.scalar.activation(out=sq[:], in_=bank[:], func=AF.Square))

                    def stats(bi=bi, bank=bank, si=si, sq=sq, ev_sq=ev_sq, cc=cc, tt=tt, g=g, dst=dst):
                        pi, pb, pfree = sbank.get()
                        kb.wait('pe', ev_sq, pfree)
                        ev_st = kb.mark('pe', nc.tensor.matmul(pb[:], lhsT=cst['onesblk'][:], rhs=sq[:], start=True, stop=True))
                        sqr.release(si, ev_st)
                        ri, r, rfree = rr.get()
                        kb.wait('dve', ev_st, rfree)
                        kb.wait('act', ev_st, rfree)
                        ev_r = kb.mark('act', nc.scalar.activation(out=r[:], in_=pb[:], func=AF.Sqrt, scale=1.0 / 64, bias=cst['eps'][:]))
                        kb.wait('dve', ev_r)
                        nc.vector.reciprocal(out=r[:], in_=r[:])
                        oi, ob, ofree = stb.get()
                        kb.wait('dve', ofree)
                        ev_o = kb.mark('dve', nc.vector.scalar_tensor_tensor(out=ob[:], in0=bank[:], scalar=qkg[:, g:g + 1],
                                                                           in1=r[:], op0=ALU.mult, op1=ALU.mult))
                        sbank.release(pi, ev_o)
                        rr.release(ri, ev_o)
                        banks.release(bi, ev_o)
                        stb.store(oi, dst[cc, :, tt * 512:(tt + 1) * 512], ob[:], ev_o)
                    pending.append(stats)
            flush_pending()
            ws.release(g, ev_mm)

        wt, ev = ws.get(2)
        kb.wait('pe', ev)
        for cc in range(4):
            for tt in range(4):
                bi, bank, bfree = banks.get()
                kb.wait('pe', bfree)
                ev_mm = mm_fm(wt, cc, tt, bank)
                e = cp_eng()
                oi, ob, ofree = stf.get()
                kb.wait(e, ev_mm, ofree)
                ev_o = kb.mark(e, copy_out(e, ob[:], bank[:]))
                banks.release(bi, ev_o)
                stf.store(oi, dr['gqkT'][cc, :, tt * 512:(tt + 1) * 512], ob[:], ev_o)
        ws.release(2, ev_mm)

        for g in (3, 4):
            wt, ev = ws.get(g)
            kb.wait('pe', ev)
            for pr in range(2):
                for tt in range(4):
                    bia, banka, bfree = banks.get()
                    kb.wait('pe', bfree)
                    ev_a = mm_fm(wt, 2 * pr, tt, banka)
                    big, bankg, bfree = banks.get()
                    kb.wait('pe', bfree)
                    ev_gm = mm_fm(wt, 2 * pr + 1, tt, bankg)
                    gi, sg, gfree = sig.get()
                    kb.wait('act', ev_gm, gfree)
                    ev_s = kb.mark('act', nc.scalar.activation(out=sg[:], in_=bankg[:], func=AF.Sigmoid))
                    banks.release(big, ev_s)
                    oi, ob, ofree = stf.get()
                    kb.wait('dve', ev_s, ev_a, ofree)
                    ev_o = kb.mark('dve', nc.vector.tensor_tensor(out=ob[:], in0=banka[:], in1=sg[:], op=ALU.mult))
                    banks.release(bia, ev_o)
                    sig.release(gi, ev_o)
                    c = (g - 3) * 2 + pr
                    stf.store(oi, dr['gluT'][c, :, tt * 512:(tt + 1) * 512], ob[:], ev_o)
            ws.release(g, ev_gm)

        for g in (5, 6):
            wt, ev = ws.get(g)
            kb.wait('pe', ev)
            for cc in range(4):
                for tt in range(4):
                    bi, bank, bfree = banks.get()
                    kb.wait('pe', bfree)
                    ev_mm = mm_fm(wt, cc, tt, bank)
                    e = cp_eng()
                    oi, ob, ofree = stf.get()
                    kb.wait(e, ev_mm, ofree)
                    ev_o = kb.mark(e, copy_out(e, ob[:], bank[:]))
                    banks.release(bi, ev_o)
                    stf.store(oi, dr['xbcT'][(g - 5) * 4 + cc, :, tt * 512:(tt + 1) * 512], ob[:], ev_o)
            ws.release(g, ev_mm)

        wt, ev = ws.get(7)
        kb.wait('pe', ev)
        for tt in range(4):
            bi, bank, bfree = banks.get()
            kb.wait('pe', bfree)
            ev_mm = mm_fm(wt, 0, tt, bank, m=16)
            e = cp_eng()
            oi, ob, ofree = stf.get()
            kb.wait(e, ev_mm, ofree)
            ev_o = kb.mark(e, copy_out(e, ob[0:16, :], bank[0:16, :]))
            banks.release(bi, ev_o)
            stf.store(oi, dr['glowT'][:, tt * 512:(tt + 1) * 512], ob[0:16, :], ev_o)
        ws.release(7, ev_mm)

        tm_dst = {8: ('v', BF16), 9: ('gv', BF16), 10: ('gout', F32), 11: ('z', F32)}
        for g in range(8, 13):
            wt, ev = ws.get(g)
            kb.wait('pe', ev)
            n = WIN_NCOLS[g]
            for tt in range(T // 128):
                bi, bank, bfree = banks.get()
                kb.wait('pe', bfree)
                for k in range(KC):
                    ins = nc.tensor.matmul(bank[:, 0:n], lhsT=hT[:, k, tt * 128:(tt + 1) * 128], rhs=wt[:, k, 0:n],
                                           start=(k == 0), stop=(k == KC - 1))
                ev_mm = kb.mark('pe', ins)
                e = cp_eng()
                if g == 12:
                    oi, ob, ofree = stf.get()
                    kb.wait(e, ev_mm, ofree)
                    ev_o = kb.mark(e, copy_out(e, ob[:, 0:n], bank[:, 0:n]))
                    banks.release(bi, ev_o)
                    kb.wait('sp', ev_o)
                    kb.dma('sp', dr['gk'][tt * 128:(tt + 1) * 128, :], ob[:, 0:256], stf.keys[oi])
                    stf.store(oi, dr['dt'][tt * 128:(tt + 1) * 128, :], ob[:, 256:264], ev_o)
                else:
                    name, dt_ = tm_dst[g]
                    st = stb if dt_ == BF16 else stf
                    oi, ob, ofree = st.get()
                    kb.wait(e, ev_mm, ofree)
                    ev_o = kb.mark(e, copy_out(e, ob[:], bank[:]))
                    banks.release(bi, ev_o)
                    st.store(oi, dr[name][tt * 128:(tt + 1) * 128, :], ob[:], ev_o)
            ws.release(g, ev_mm)
        kb.barrier(extra=stf.last + stb.last)


def mixer_conv_gen(kb, l, dr, cst):
    nc = kb.nc
    with ExitStack() as es:
        cw = kb.sb("cw", [128, 4, 31], F32, es)
        cb = kb.sb("cb", [128, 4], F32, es)
        lw = kb.sb("lw", [128, 4], F32, es)
        lb = kb.sb("lb", [128, 4], F32, es)
        kp = kb.newsem("cp")
        kb.dma('sp', cw[:], dr['cdw'][l], kp)
        kb.dma('sp', cb[:], dr['cdb'][l], kp)
        kb.dma('sp', lw[:], dr['clnw'][l], kp)
        ev_p = kb.dma('sp', lb[:], dr['clnb'][l], kp)
        glu = [kb.sb("glu", [128, 30 + T], F32, es) for _ in range(4)]
        acc = [kb.sb("acc", [128, T], F32, es) for _ in range(4)]
        kg = kb.newsem("cg")
        ev_acc = []
        for c in range(4):
            e = 'dve'
            eng = kb.E[e]
            ev_z = kb.mark(e, eng.memset(glu[c][:, 0:30], 0.0))
            ev_l = kb.dma('sp', glu[c][:, 30:30 + T], dr['gluT'][c], kg)
            kb.wait(e, ev_l, ev_p, ev_z)
            eng.tensor_scalar(out=acc[c][:], in0=glu[c][:, 0:T], scalar1=cw[:, c, 0:1], scalar2=cb[:, c:c + 1],
                              op0=ALU.mult, op1=ALU.add)
            for j in range(1, 31):
                ins = eng.scalar_tensor_tensor(out=acc[c][:], in0=glu[c][:, j:j + T], scalar=cw[:, c, j:j + 1],
                                               in1=acc[c][:], op0=ALU.mult, op1=ALU.add)
                if j < 30:
                    yield
            ev_acc.append(kb.mark(e, ins))
        ps1 = kb.ps("cs1", [128, 512], F32, es)
        ps2 = kb.ps("cs2", [128, 512], F32, es)
        sq = Ring([kb.sb("csq", [128, 512], F32, es) for _ in range(2)])
        mean = kb.sb("cmean", [128, 512], F32, es)
        msq = kb.sb("cmsq", [128, 512], F32, es)
        rstd = kb.sb("crstd", [128, 512], F32, es)
        tmp = Ring([kb.sb("ctmp", [128, 512], F32, es) for _ in range(2)])
        st = Stager(kb, [kb.sb("cst", [128, 512], BF16, es) for _ in range(2)])
        ev_prev = None
        for tt in range(4):
            sl = slice(tt * 512, (tt + 1) * 512)
            kb.wait('pe', ev_prev)
            for c in range(4):
                kb.wait('pe', ev_acc[c])
                nc.tensor.matmul(ps1[:], lhsT=cst['ones_f'][:], rhs=acc[c][:, sl], start=(c == 0), stop=(c == 3))
            ev_s1 = None
            for c in range(4):
                si, sb_, sfree = sq.get()
                kb.wait('act', ev_acc[c], sfree)
                ev_q = kb.mark('act', nc.scalar.activation(out=sb_[:], in_=acc[c][:, sl], func=AF.Square))
                kb.wait('pe', ev_q)
                ev_m = kb.mark('pe', nc.tensor.matmul(ps2[:], lhsT=cst['ones_f'][:], rhs=sb_[:], start=(c == 0), stop=(c == 3)))
                sq.release(si, ev_m)
            kb.wait('dve', ev_m)
            nc.vector.tensor_scalar(out=mean[:], in0=ps1[:], scalar1=1.0 / 512, scalar2=None, op0=ALU.mult)
            nc.vector.tensor_tensor(out=msq[:], in0=mean[:], in1=mean[:], op=ALU.mult)
            ev_v = kb.mark('dve', nc.vector.scalar_tensor_tensor(out=rstd[:], in0=ps2[:], scalar=1.0 / 512, in1=msq[:],
                                                                op0=ALU.mult, op1=ALU.subtract))
            kb.wait('act', ev_v)
            ev_sd = kb.mark('act', nc.scalar.activation(out=rstd[:], in_=rstd[:], func=AF.Sqrt, bias=cst['eps'][:]))
            kb.wait('dve', ev_sd)
            nc.vector.reciprocal(out=rstd[:], in_=rstd[:])
            for c in range(4):
                ti, tb, tfree = tmp.get()
                kb.wait('dve', tfree)
                nc.vector.tensor_tensor(out=tb[:], in0=acc[c][:, sl], in1=mean[:], op=ALU.subtract)
                ev_t = kb.mark('dve', nc.vector.tensor_tensor(out=tb[:], in0=tb[:], in1=rstd[:], op=ALU.mult))
                oi, ob, ofree = st.get()
                kb.wait('act', ev_t, ofree, ev_p)
                ev_y = kb.mark('act', nc.scalar.activation(out=ob[:], in_=tb[:], func=AF.Silu, scale=lw[:, c:c + 1], bias=lb[:, c:c + 1]))
                tmp.release(ti, ev_y)
                st.store(oi, dr['yT'][8 + c, :, sl], ob[:], ev_y)
                yield
            ev_prev = ev_t
        kb.bg_extra = list(st.last)


def mixer_conv(kb, l, dr, cst):
    for _ in mixer_conv_gen(kb, l, dr, cst):
        pass
    kb.barrier(extra=kb.bg_extra)


def phase_gates(kb, l, hT, dr, cst):
    nc = kb.nc
    with ExitStack() as es:
        yT = kb.sb("yT", [128, 16, T], BF16, es)
        ky = kb.newsem("yT")
        for c in range(16):
            ev_y = kb.dma('sp', yT[:, c, :], dr['yT'][c], ky)
        bg = kb.sb("bg", [128, 4, 16], F32, es)
        ev_b = kb.dma('sp', bg[:], dr['bgate'][l], ky)
        wsl = [kb.sb("wg", [128, 20, 128], BF16, es) for _ in range(2)]
        loads = [(dr['wgb'][l, cc, i], (lambda s: s[:])) for cc in range(16) for i in range(4)]
        ws = WStream(kb, wsl, loads)
        gb = Ring([kb.ps("gb", [128, 512], F32, es) for _ in range(4)])
        bb = Ring([kb.ps("bb", [128, 512], F32, es) for _ in range(4)])
        sig = Ring([kb.sb("gsig", [128, 512], F32, es) for _ in range(2)])
        macc = Ring([kb.sb("macc", [128, T], F32, es) for _ in range(2)])
        tmp = kb.sb("gtmp", [128, 512], F32, es)
        st = Stager(kb, [kb.sb("gst", [128, T], BF16, es) for _ in range(2)])
        kb.wait('pe', ev_y)
        kb.wait('act', ev_b)
        n = 0
        for cc in range(16):
            mi, mb, mfree = macc.get()
            for i in range(4):
                wt, ev = ws.get(n)
                kb.wait('pe', ev)
                for tt in range(4):
                    sl = slice(tt * 512, (tt + 1) * 512)
                    gi, gbank, gfree = gb.get()
                    kb.wait('pe', gfree)
                    for k in range(KC):
                        ins = nc.tensor.matmul(gbank[:], lhsT=wt[:, k, :], rhs=hT[:, k, sl], start=(k == 0), stop=(k == KC - 1))
                    ev_g = kb.mark('pe', ins)
                    bi, bbank, bfree = bb.get()
                    kb.wait('pe', bfree)
                    for k in range(4):
                        ins = nc.tensor.matmul(bbank[:], lhsT=wt[:, 16 + k, :], rhs=yT[:, 4 * i + k, sl], start=(k == 0), stop=(k == 3))
                    ev_br = kb.mark('pe', ins)
                    si, sg, sfree = sig.get()
                    kb.wait('act', ev_g, sfree)
                    ev_s = kb.mark('act', nc.scalar.activation(out=sg[:], in_=gbank[:], func=AF.Sigmoid, bias=bg[:, i, cc:cc + 1]))
                    gb.release(gi, ev_s)
                    kb.wait('dve', ev_s, ev_br)
                    if i == 0:
                        kb.wait('dve', mfree)
                        ev_m = kb.mark('dve', nc.vector.tensor_tensor(out=mb[:, sl], in0=bbank[:], in1=sg[:], op=ALU.mult))
                    else:
                        nc.vector.tensor_tensor(out=tmp[:], in0=bbank[:], in1=sg[:], op=ALU.mult)
                        ev_m = kb.mark('dve', nc.vector.tensor_tensor(out=mb[:, sl], in0=mb[:, sl], in1=tmp[:], op=ALU.add))
                    bb.release(bi, ev_m)
                    sig.release(si, ev_m)
                ws.release(n, ev_br)
                n += 1
            oi, ob, ofree = st.get()
            kb.wait('act', ev_m, ofree)
            ev_c = kb.mark('act', nc.scalar.copy(out=ob[:], in_=mb[:]))
            macc.release(mi, ev_c)
            st.store(oi, dr['mT'][cc], ob[:], ev_c)
        kb.barrier(extra=st.last)


def gates_alloc(kb, l, dr, es):
    res = {}
    res['wsl'] = [kb.sb("gwg", [128, 16, 128], BF16, es) for _ in range(2)]
    res['bank'] = kb.ps("ggb", [128, 512], F32, es)
    res['stg'] = [kb.sb("ggs", [128, T], BF16, es) for _ in range(2)]
    res['bg'] = kb.sb("gbg", [128, 4, 16], F32, es)
    return res


def gates_task(kb, l, hT, dr, cst, res):
    nc = kb.nc
    kb.pinning = True
    kbg = kb.newsem("gbg")
    kb.wait('sp', getattr(kb, 'last_barrier', None))
    ev_b = kb.dma('sp', res['bg'][:], dr['bgate'][l], kbg)
    loads = [(dr['wgb'][l, cc, i, :, 0:16, :], (lambda s: s[:])) for cc in range(16) for i in range(4)]
    ws = WStream(kb, res['wsl'], loads)
    st = Stager(kb, res['stg'])
    kb.pinning = False
    bank = res['bank']
    bfree = None
    n = 0
    for cc in range(16):
        for i in range(4):
            wt, ev = ws.get(n)
            oi, ob, ofree = st.get()
            for tt in range(4):
                sl = slice(tt * 512, (tt + 1) * 512)
                kb.wait('pe', ev, bfree)
                for k in range(KC):
                    ins = nc.tensor.matmul(bank[:], lhsT=wt[:, k, :], rhs=hT[:, k, sl], start=(k == 0), stop=(k == KC - 1))
                ev_g = kb.mark('pe', ins)
                kb.wait('act', ev_g, ofree, ev_b)
                ev_s = kb.mark('act', nc.scalar.activation(out=ob[:, sl], in_=bank[:], func=AF.Sigmoid, bias=res['bg'][:, i, cc:cc + 1]))
                bfree = ev_s
                yield
            ws.release(n, ev_g)
            n += 1
            st.store(oi, dr['gT'][i, cc], ob[:], ev_s)
    kb.bg_extra = list(st.last)


def phase_branch(kb, l, dr, cst):
    nc = kb.nc
    with ExitStack() as es:
        yT = kb.sb("yT", [128, 16, T], BF16, es)
        ky = kb.newsem("yT")
        for c in range(16):
            ev_y = kb.dma('sp', yT[:, c, :], dr['yT'][c], ky)
        wsl = [kb.sb("wb", [128, 4, 128], BF16, es) for _ in range(3)]
        loads = [(dr['wgb'][l, cc, i, :, 16:20, :], (lambda s: s[:])) for cc in range(16) for i in range(4)]
        ws = WStream(kb, wsl, loads)
        bb = Ring([kb.ps("bb", [128, 512], F32, es) for _ in range(4)])
        lg = Loader(kb, [kb.sb("bgt", [128, T], BF16, es) for _ in range(3)])
        macc = Ring([kb.sb("macc", [128, T], F32, es) for _ in range(2)])
        tmp = Ring([kb.sb("gtmp", [128, 512], F32, es) for _ in range(2)])
        st = Stager(kb, [kb.sb("gst", [128, T], BF16, es) for _ in range(2)])
        seq = [(cc, i) for cc in range(16) for i in range(4)]
        g_t = {}

        def issue_g(n):
            if n < len(seq):
                cc_, i_ = seq[n]
                g_t[n] = lg.load(lambda b: [(b[:], dr['gT'][i_, cc_])])

        issue_g(0)
        issue_g(1)
        kb.wait('pe', ev_y)
        for n, (cc, i) in enumerate(seq):
            if i == 0:
                mi, mb, mfree = macc.get()
            issue_g(n + 2)
            gi, gb, ev_gl = g_t.pop(n)
            wt, ev = ws.get(n)
            kb.wait('pe', ev)
            for tt in range(4):
                sl = slice(tt * 512, (tt + 1) * 512)
                bi, bbank, bfree = bb.get()
                kb.wait('pe', bfree)
                for k in range(4):
                    ins = nc.tensor.matmul(bbank[:], lhsT=wt[:, k, :], rhs=yT[:, 4 * i + k, sl], start=(k == 0), stop=(k == 3))
                ev_br = kb.mark('pe', ins)
                kb.wait('dve', ev_br, ev_gl)
                if i == 0:
                    kb.wait('dve', mfree)
                    ev_m = kb.mark('dve', nc.vector.tensor_tensor(out=mb[:, sl], in0=bbank[:], in1=gb[:, sl], op=ALU.mult))
                else:
                    ti, tb_, tfree = tmp.get()
                    nc.vector.tensor_tensor(out=tb_[:], in0=bbank[:], in1=gb[:, sl], op=ALU.mult)
                    ev_m = kb.mark('dve', nc.vector.tensor_tensor(out=mb[:, sl], in0=mb[:, sl], in1=tb_[:], op=ALU.add))
                bb.release(bi, ev_m)
            ws.release(n, ev_br)
            lg.release(gi, ev_m)
            if i == 3:
                oi, ob, ofree = st.get()
                kb.wait('act', ev_m, ofree)
                ev_c = kb.mark('act', nc.scalar.copy(out=ob[:], in_=mb[:]))
                macc.release(mi, ev_c)
                st.store(oi, dr['mT'][cc], ob[:], ev_c)
        kb.barrier(extra=st.last)


def phase_outproj(kb, l, x_src, x_dst, dr, cst):
    nc = kb.nc
    with ExitStack() as es:
        mT = kb.sb("mT", [128, 16, T], BF16, es)
        km = kb.newsem("mT")
        for c in range(16):
            ev_m = kb.dma('sp', mT[:, c, :], dr['mT'][c], km)
        wsl = [kb.sb("wo", [128, 16, 512], BF16, es) for _ in range(2)]
        loads = [(dr['wout'][l, g], (lambda s: s[:])) for g in range(4)]
        ws = WStream(kb, wsl, loads)
        banks = Ring([kb.ps("ob", [128, 512], F32, es) for _ in range(4)])
        ld = Loader(kb, [kb.sb("ox", [128, 512], F32, es) for _ in range(4)])
        st = Stager(kb, [kb.sb("oo", [128, 512], F32, es) for _ in range(4)])
        kb.wait('pe', ev_m)
        xseq = [(g, tt) for g in range(4) for tt in range(16)]
        x_t = {}

        def issue_x(m):
            if m < len(xseq):
                g_, tt_ = xseq[m]
                x_t[m] = ld.load(lambda b: [(b[:], x_src[tt_ * 128:(tt_ + 1) * 128, g_ * 512:(g_ + 1) * 512])])

        issue_x(0)
        issue_x(1)
        m = 0
        for g in range(4):
            wt, ev = ws.get(g)
            kb.wait('pe', ev)
            for tt in range(16):
                rows = slice(tt * 128, (tt + 1) * 128)
                cols = slice(g * 512, (g + 1) * 512)
                issue_x(m + 2)
                li, xb, ev_x = x_t.pop(m)
                m += 1
                bi, bank, bfree = banks.get()
                kb.wait('pe', bfree)
                for k in range(KC):
                    ins = nc.tensor.matmul(bank[:], lhsT=mT[:, k, rows], rhs=wt[:, k, :], start=(k == 0), stop=(k == KC - 1))
                ev_mm = kb.mark('pe', ins)
                oi, ob, ofree = st.get()
                kb.wait('dve', ev_mm, ev_x, ofree)
                ev_o = kb.mark('dve', nc.vector.tensor_tensor(out=ob[:], in0=bank[:], in1=xb[:], op=ALU.add))
                banks.release(bi, ev_o)
                ld.release(li, ev_o)
                st.store(oi, x_dst[rows, cols], ob[:], ev_o)
            ws.release(g, ev_mm)
        kb.barrier(extra=st.last)


def phase_ffn_up(kb, l, hT, dr, cst):
    nc = kb.nc
    with ExitStack() as es:
        fw = kb.sb("fw", [128, 88, 3], F32, es)
        fb = kb.sb("fb", [128, 88], F32, es)
        kf = kb.newsem("fp")
        kb.dma('sp', fw[:], dr['fcw'][l], kf)
        ev_p = kb.dma('sp', fb[:], dr['fcb'][l], kf)
        wsl = [kb.sb("wu", [128, 16, 512], BF16, es) for _ in range(2)]
        loads = [(dr['wup'][l, g], (lambda s: s[:])) for g in range(22)]
        ws = WStream(kb, wsl, loads)
        banks = Ring([kb.ps("ub", [128, 512], F32, es) for _ in range(8)])
        ubuf = Ring([kb.sb("uu", [128, 2 + T], F32, es) for _ in range(4)])
        accg = kb.sb("accg", [128, T], F32, es)
        accv = kb.sb("accv", [128, T], F32, es)
        st = Stager(kb, [kb.sb("ast", [128, 2, T], BF16, es) for _ in range(2)])
        for i in range(4):
            ev_z = kb.mark('dve', nc.vector.memset(ubuf.bufs[i][:, 0:2], 0.0))
        kb.wait('act', ev_z)
        kb.wait('dve', ev_p)
        sti = None
        for g in range(22):
            wt, ev = ws.get(g)
            kb.wait('pe', ev)
            for pr in range(2):
                j = 2 * g + pr
                us = []
                for half in range(2):
                    cc = 2 * pr + half
                    ui, ub, ufree = ubuf.get()
                    evs = []
                    for tt in range(4):
                        bi, bank, bfree = banks.get()
                        kb.wait('pe', bfree)
                        for k in range(KC):
                            ins = nc.tensor.matmul(bank[:], lhsT=wt[:, k, cc * 128:(cc + 1) * 128], rhs=hT[:, k, tt * 512:(tt + 1) * 512],
                                                   start=(k == 0), stop=(k == KC - 1))
                        ev_mm = kb.mark('pe', ins)
                        kb.wait('act', ev_mm, ufree)
                        ev_c = kb.mark('act', nc.scalar.copy(out=ub[:, 2 + tt * 512:2 + (tt + 1) * 512], in_=bank[:]))
                        banks.release(bi, ev_c)
                    us.append((ui, ub, ev_c))
                outs = []
                for half, accb in ((0, accg), (1, accv)):
                    ui, ub, ev_c = us[half]
                    q = 2 * j + half
                    kb.wait('dve', ev_c)
                    nc.vector.tensor_scalar(out=accb[:], in0=ub[:, 0:T], scalar1=fw[:, q, 0:1], scalar2=fb[:, q:q + 1],
                                            op0=ALU.mult, op1=ALU.add)
                    nc.vector.scalar_tensor_tensor(out=accb[:], in0=ub[:, 1:1 + T], scalar=fw[:, q, 1:2], in1=accb[:],
                                                   op0=ALU.mult, op1=ALU.add)
                    ev_a = kb.mark('dve', nc.vector.scalar_tensor_tensor(out=accb[:], in0=ub[:, 2:2 + T], scalar=fw[:, q, 2:3],
                                                                        in1=accb[:], op0=ALU.mult, op1=ALU.add))
                    outs.append(ev_a)
                ui_g, ub_g, _ = us[0]
                kb.wait('act', outs[0])
                ev_s = kb.mark('act', nc.scalar.activation(out=ub_g[:, 2:2 + T], in_=accg[:], func=AF.Silu))
                if pr == 0:
                    sti, sob, sofree = st.get()
                kb.wait('dve', ev_s, sofree)
                ev_o = kb.mark('dve', nc.vector.tensor_tensor(out=sob[:, pr, :], in0=ub_g[:, 2:2 + T], in1=accv[:], op=ALU.mult))
                ubuf.release(us[0][0], ev_o)
                ubuf.release(us[1][0], outs[1])
            kb.wait('sp', ev_o)
            j0 = 2 * g
            for t8 in range(8):
                e_st = kb.dma('sp', dr['aT'][t8, :, j0:j0 + 2, :], sob[:, :, t8 * 256:(t8 + 1) * 256], st.keys[sti])
            st.ring.release(sti, e_st)
            st.last.append(e_st)
            ws.release(g, ev_mm)
        kb.barrier(extra=st.last[-2:])


def phase_ffn_down(kb, l, x_src, x_dst, dr, cst):
    nc = kb.nc
    with ExitStack() as es:
        wsl = [kb.sb("wd", [128, 44, 512], BF16, es) for _ in range(2)]
        loads = [(dr['wdn'][l, g], (lambda s: s[:])) for g in range(4)]
        ws = WStream(kb, wsl, loads)
        banks = Ring([kb.ps("db", [128, 512], F32, es) for _ in range(4)])
        la = Loader(kb, [kb.sb("da", [128, 44, 256], BF16, es) for _ in range(2)])
        ld = Loader(kb, [kb.sb("dx", [128, 512], F32, es) for _ in range(3)])
        st = Stager(kb, [kb.sb("do", [128, 512], F32, es) for _ in range(3)])
        seq = [(g, t8) for g in range(4) for t8 in range(8)]
        xseq = [(g, t8, h2) for (g, t8) in seq for h2 in range(2)]
        a_t = {}
        x_t = {}

        def issue_a(n):
            if n < len(seq):
                a_t[n] = la.load(lambda b: [(b[:], dr['aT'][seq[n][1]])])

        def issue_x(m):
            if m < len(xseq):
                g_, t8_, h2_ = xseq[m]
                r_ = slice(t8_ * 256 + h2_ * 128, t8_ * 256 + (h2_ + 1) * 128)
                x_t[m] = ld.load(lambda b: [(b[:], x_src[r_, g_ * 512:(g_ + 1) * 512])])

        issue_a(0)
        issue_x(0)
        m = 0
        for n, (g, t8) in enumerate(seq):
            if t8 == 0:
                wt, ev = ws.get(g)
                kb.wait('pe', ev)
            cols = slice(g * 512, (g + 1) * 512)
            issue_a(n + 1)
            ai, ab, ev_a = a_t.pop(n)
            kb.wait('pe', ev_a)
            for h2 in range(2):
                rows = slice(t8 * 256 + h2 * 128, t8 * 256 + (h2 + 1) * 128)
                issue_x(m + 1)
                li, xb, ev_x = x_t.pop(m)
                m += 1
                bi, bank, bfree = banks.get()
                kb.wait('pe', bfree)
                for k in range(NPAIR):
                    ins = nc.tensor.matmul(bank[:], lhsT=ab[:, k, h2 * 128:(h2 + 1) * 128], rhs=wt[:, k, :],
                                           start=(k == 0), stop=(k == NPAIR - 1))
                ev_mm = kb.mark('pe', ins)
                oi, ob, ofree = st.get()
                kb.wait('dve', ev_mm, ev_x, ofree)
                ev_o = kb.mark('dve', nc.vector.tensor_tensor(out=ob[:], in0=bank[:], in1=xb[:], op=ALU.add))
                banks.release(bi, ev_o)
                ld.release(li, ev_o)
                st.store(oi, x_dst[rows, cols], ob[:], ev_o)
            la.release(ai, ev_mm)
            if t8 == 7:
                ws.release(g, ev_mm)
        kb.barrier(extra=st.last)


def mixer_attn(kb, l, dr, cst, lam_init, bg=None):
    nc = kb.nc
    with ExitStack() as es:
        lam = kb.sb("lam", [128, 256], F32, es)
        sub = kb.sb("subln", [128, 128], F32, es)
        kp = kb.newsem("ap")
        kb.dma('sp', lam[:], dr['lam'][l], kp)
        ev_p = kb.dma('sp', sub[:], dr['subln'][l], kp)
        prod = kb.sb("aprod", [128, 2, 64], F32, es)
        s2 = kb.sb("as2", [128, 2], F32, es)
        nl = kb.sb("anl", [128, 1], F32, es)
        kb.wait('dve', ev_p)
        nc.vector.tensor_tensor(out=prod[:, 0, :], in0=lam[:, 0:64], in1=lam[:, 64:128], op=ALU.mult)
        nc.vector.tensor_tensor(out=prod[:, 1, :], in0=lam[:, 128:192], in1=lam[:, 192:256], op=ALU.mult)
        ev = kb.mark('dve', nc.vector.reduce_sum(out=s2[:], in_=prod[:], axis=AX.X))
        kb.wait('act', ev)
        ev = kb.mark('act', nc.scalar.activation(out=s2[:], in_=s2[:], func=AF.Exp))
        kb.wait('dve', ev)
        ev = kb.mark('dve', nc.vector.tensor_tensor(out=nl[:], in0=s2[:, 1:2], in1=s2[:, 0:1], op=ALU.subtract))
        kb.wait('dve', ev)
        nc.vector.tensor_scalar(out=nl[:], in0=nl[:], scalar1=-lam_init, scalar2=None, op0=ALU.add)
        ev_nl = kb.mark('dve', nc.vector.tensor_scalar(out=sub[:], in0=sub[:], scalar1=1.0 - lam_init, scalar2=None, op0=ALU.mult))
        kb.wait('dve', ev_nl)

        qk = Loader(kb, [kb.sb("aqk", [128, 2, T], BF16, es) for _ in range(2)])
        va = Loader(kb, [kb.sb("ava", [128, 16, 129], BF16, es) for _ in range(2)])
        for b in va.ring.bufs:
            ev_one = kb.mark('dve', nc.vector.memset(b[:, :, 128:129], 1.0))
        kb.wait('pe', ev_one)
        sbank = Ring([kb.ps("asb", [128, 512], F32, es) for _ in range(3)])
        obank = Ring([kb.ps("aob", [128, 2, 129], F32, es) for _ in range(2)])
        tb_all = kb.ps("atb", [128, 2, 128], BF16, es)
        tbank = Ring([tb_all[:, 0, :], tb_all[:, 1, :]])
        pT = Ring([kb.sb("apT", [128, 512], BF16, es) for _ in range(3)])
        rc = Ring([kb.sb("arc", [128, 4], F32, es) for _ in range(2)])
        t1 = kb.sb("at1", [128, 128], F32, es)
        ob = Ring([kb.sb("ao", [128, 128], F32, es) for _ in range(2)])
        junk = kb.sb("ajunk", [128, 128], F32, es)
        yb = Ring([kb.sb("ay", [128, 128], BF16, es) for _ in range(2)])
        st = Stager(kb, [kb.sb("ayT", [128, T], BF16, es) for _ in range(2)])
        vview = dr['v'].rearrange("(j p) c -> p j c", p=128)
        kb.bg_extra = []
        kb.bg = bg
        groups = [(i, m, jg) for i in range(16) for m in range(2) for jg in range(0, i + 1, 4)]
        for h in range(4):
            qi, qkb, ev_q = qk.load(lambda b: [(b[:, 0, :], dr['qT'][h]), (b[:, 1, :], dr['kT'][h])])
            vi, vb, ev_v = va.load(lambda b: [(b[:, :, 0:128], vview[:, :, h * 128:(h + 1) * 128])])
            kb.wait('pe', ev_q, ev_v)
            sti, yTh, yfree = st.get()
            cur_ob = {}
            last = {}

            def emit_scores(i, m, jg):
                rows = slice(m * 64, (m + 1) * 64)
                je = min(jg + 4, i + 1)
                w = (je - jg) * 128
                si, sbk, sfree = sbank.get()
                kb.wait('pe', sfree)
                for jj in range(jg, je):
                    ins = nc.tensor.matmul(sbk[:, (jj - jg) * 128:(jj - jg + 1) * 128], lhsT=qkb[rows, 1, jj * 128:(jj + 1) * 128],
                                           rhs=qkb[rows, 0, i * 128:(i + 1) * 128], start=True, stop=True)
                ev_s = kb.mark('pe', ins)
                pi, pb, pfree = pT.get()
                kb.wait('act', ev_s, pfree)
                ev_e = kb.mark('act', nc.scalar.activation(out=pb[:, 0:w], in_=sbk[:, 0:w], func=AF.Exp))
                sbank.release(si, ev_e)
                if je == i + 1:
                    off = (i - jg) * 128
                    kb.wait('dve', ev_e)
                    ev_e = kb.mark('dve', nc.vector.memset(pb[64:128, off:off + 64], 0.0))
                return (i, m, jg, je, pi, pb, ev_e)

            def emit_tail(i, oi, obk, ev_pv):
                ri, rcb, rfree = rc.get()
                kb.wait('dve', ev_pv, rfree)
                ev = kb.mark('dve', nc.vector.reciprocal(out=rcb[:, 0:2], in_=obk[:, :, 128]))
                kb.wait('dve', ev)
                ev = kb.mark('dve', nc.vector.tensor_tensor(out=rcb[:, 1:2], in0=rcb[:, 1:2], in1=nl[:], op=ALU.mult))
                kb.wait('dve', ev)
                nc.vector.tensor_scalar(out=t1[:], in0=obk[:, 0, 0:128], scalar1=rcb[:, 0:1], scalar2=None, op0=ALU.mult)
                bi, obuf, bfree = ob.get()
                kb.wait('dve', bfree)
                ev_o = kb.mark('dve', nc.vector.scalar_tensor_tensor(out=obuf[:], in0=obk[:, 1, 0:128], scalar=rcb[:, 1:2], in1=t1[:],
                                                                    op0=ALU.mult, op1=ALU.add))
                obank.release(oi, ev_o)
                kb.wait('act', ev_o)
                ev = kb.mark('act', nc.scalar.activation(out=junk[:], in_=obuf[:], func=AF.Square, accum_out=rcb[:, 2:3]))
                kb.wait('act', ev)
                ev = kb.mark('act', nc.scalar.activation(out=rcb[:, 3:4], in_=rcb[:, 2:3], func=AF.Sqrt, scale=1.0 / 128, bias=cst['eps'][:]))
                kb.wait('dve', ev)
                ev = kb.mark('dve', nc.vector.reciprocal(out=rcb[:, 3:4], in_=rcb[:, 3:4]))
                kb.wait('dve', ev)
                yi, ybuf, yfree2 = yb.get()
                kb.wait('dve', yfree2)
                ev_y = kb.mark('dve', nc.vector.scalar_tensor_tensor(out=ybuf[:], in0=obuf[:], scalar=rcb[:, 3:4], in1=sub[:],
                                                                    op0=ALU.mult, op1=ALU.mult))
                ob.release(bi, ev_y)
                rc.release(ri, ev_y)

                def tr():
                    ti, tb, tfree = tbank.get()
                    kb.wait('pe', ev_y, tfree)
                    ev_t = kb.mark('pe', nc.tensor.transpose(out=tb[:], in_=ybuf[:], identity=cst['ident'][:]))
                    yb.release(yi, ev_t)
                    kb.wait('act', ev_t, yfree)
                    ev_c = kb.mark('act', nc.scalar.copy(out=yTh[:, i * 128:(i + 1) * 128], in_=tb[:]))
                    tbank.release(ti, ev_c)
                    last['ev_c'] = ev_c
                return tr

            trs = []

            def emit_pv(stt):
                i, m, jg, je, pi, pb, ev_e = stt
                if m == 0 and jg == 0:
                    oi, obk, ofree = obank.get()
                    kb.wait('pe', ofree)
                    cur_ob['v'] = (oi, obk)
                oi, obk = cur_ob['v']
                kb.wait('pe', ev_e)
                for jj in range(jg, je):
                    ins = nc.tensor.matmul(obk[:, m, :], lhsT=pb[:, (jj - jg) * 128:(jj - jg + 1) * 128], rhs=vb[:, jj, :],
                                           start=(jj == 0), stop=(jj == i))
                ev_pv = kb.mark('pe', ins)
                pT.release(pi, ev_pv)
                last['ev_pv'] = ev_pv
                if m == 1 and je == i + 1:
                    trs.append(emit_tail(i, oi, obk, ev_pv))

            prev = None
            for gidx, (i, m, jg) in enumerate(groups):
                cur = emit_scores(i, m, jg)
                ready = trs[:]
                del trs[:]
                if prev is not None:
                    emit_pv(prev)
                for t_ in ready:
                    t_()
                prev = cur
                kb.tick(1)
            emit_pv(prev)
            for t_ in trs:
                t_()
            del trs[:]
            qk.release(qi, last['ev_pv'])
            va.release(vi, last['ev_pv'])
            st.store(sti, dr['yT'][h], yTh[:], last['ev_c'])
        kb.drain()
        kb.barrier(extra=st.last + kb.bg_extra)


def mixer_gla(kb, l, dr, cst):
    nc = kb.nc
    with ExitStack() as es:
        kp = kb.newsem("bp")
        w2f = kb.sb("w2f", [32, 256], F32, es)
        w2a = kb.sb("w2a", [32, 256], BF16, es)
        gn = kb.sb("gn", [64, 512], F32, es)
        glf = kb.sb("glf", [32, T], F32, es)
        glb = kb.sb("glb", [32, T], BF16, es)
        ev_m = kb.mark('dve', nc.vector.memset(glf[:], 1.0))
        kb.wait('sp', ev_m)
        kb.dma('sp', w2f[:], dr['w2aug'][l], kp)
        kb.dma('sp', gn[:], dr['glan'][l, 0:64, :], kp)
        ev_p = kb.dma('sp', glf[0:16, :], dr['glowT'][:, :], kp)
        kb.wait('dve', ev_p)
        nc.vector.tensor_copy(out=w2a[:], in_=w2f[:])
        ev_gl = kb.mark('dve', nc.vector.tensor_copy(out=glb[:], in_=glf[:]))
        g_all = kb.sb("g_all", [64, 32, 256], BF16, es)
        qtT = [kb.sb("qtT", [128, T], BF16, es) for _ in range(2)]
        ktT = [kb.sb("ktT", [128, T], BF16, es) for _ in range(2)]
        elast = kb.sb("elast", [128, 2, 32], F32, es)
        S = [kb.sb("S", [128, 256], F32, es) for _ in range(2)]
        Sbf = [kb.sb("Sbf", [128, 33, 128], BF16, es) for _ in range(2)]
        yTst = kb.sb("byT", [128, 4, T], BF16, es)
        for hp in range(2):
            nc.vector.memset(S[hp][:], 0.0)
            ev_z = kb.mark('dve', nc.vector.memset(Sbf[hp][:, 0, :], 0.0))
        banks = Ring([kb.ps("bb", [128, 512], F32, es) for _ in range(6)])
        btb_all = kb.ps("btb", [128, 2, 4, 64], BF16, es)
        tbank = Ring([btb_all[:, 0, :, :], btb_all[:, 1, :, :]])
        f1 = Ring([kb.sb("bf1", [128, 512], F32, es) for _ in range(3)])
        f2 = Ring([kb.sb("bf2", [128, 512], F32, es) for _ in range(3)])
        kb.wait('pe', ev_gl)
        for c in range(32):
            bi, bank, bfree = banks.get()
            kb.wait('pe', bfree)
            ev_mm = kb.mark('pe', nc.tensor.matmul(bank[0:64, 0:256], lhsT=glb[:, c * 64:(c + 1) * 64], rhs=w2a[:], start=True, stop=True))
            fi, fb, ffree = f1.get()
            kb.wait('act', ev_mm, ffree)
            nc.scalar.activation(out=fb[0:64, 0:256], in_=bank[0:64, 0:256], func=AF.Exp, scale=-1.0)
            ev_a = kb.mark('act', nc.scalar.activation(out=fb[0:64, 0:256], in_=fb[0:64, 0:256], func=AF.Ln, bias=cst['one'][0:64, :]))
            banks.release(bi, ev_a)
            kb.wait('dve', ev_a)
            ev_g = kb.mark('dve', nc.vector.tensor_scalar(out=g_all[:, c, :], in0=fb[0:64, 0:256], scalar1=-1.0 / 16, scalar2=None, op0=ALU.mult))
            f1.release(fi, ev_g)
            kb.tick(1)
        import os
        if os.environ.get('BSTAGE') == '1':
            kb.barrier()
            return
        ldq = Loader(kb, [kb.sb("bldq", [128, 2, 512], F32, es) for _ in range(2)])
        kb.wait('pe', ev_g)
        for t4 in range(4):
            sl = slice(t4 * 512, (t4 + 1) * 512)
            for hp in range(2):
                li, qb, ev_q = ldq.load(lambda b: [(b[:, 0, :], dr['gqkT'][hp, :, sl]), (b[:, 1, :], dr['gqkT'][2 + hp, :, sl])])
                bi, bank, bfree = banks.get()
                kb.wait('pe', bfree)
                for cj in range(8):
                    c = t4 * 8 + cj
                    ins = nc.tensor.matmul(bank[:, cj * 64:(cj + 1) * 64], lhsT=g_all[:, c, hp * 128:(hp + 1) * 128], rhs=cst['tri_b'][:],
                                           start=True, stop=True)
                ev_mm = kb.mark('pe', ins)
                ai, eq, afree = f1.get()
                ci, ek, cfree = f2.get()
                kb.wait('act', ev_mm, afree, cfree)
                nc.scalar.activation(out=eq[:], in_=bank[:], func=AF.Exp)
                ev_e = kb.mark('act', nc.scalar.activation(out=ek[:], in_=bank[:], func=AF.Exp, scale=-1.0))
                banks.release(bi, ev_e)
                kb.wait('dve', ev_e, ev_q)
                nc.vector.scalar_tensor_tensor(out=qtT[hp][:, sl], in0=qb[:, 0, :], scalar=0.125, in1=eq[:], op0=ALU.mult, op1=ALU.mult)
                nc.vector.tensor_tensor(out=ktT[hp][:, sl], in0=qb[:, 1, :], in1=ek[:], op=ALU.mult)
                ev_d = kb.mark('dve', nc.vector.tensor_copy(out=elast[:, hp, t4 * 8:(t4 + 1) * 8], in_=eq[:, 63:512:64]))
                f1.release(ai, ev_d)
                f2.release(ci, ev_d)
                ldq.release(li, ev_d)
                kb.tick(2)
        kb.wait('pe', ev_d)
        kb.wait('dve', ev_d)
        if os.environ.get('BSTAGE') == '2':
            kb.barrier()
            return
        ldk = Loader(kb, [kb.sb("bldk", [64, 256], F32, es) for _ in range(2)])
        ldv = Loader(kb, [kb.sb("bldv", [64, 512], BF16, es) for _ in range(2)])
        ldo = Loader(kb, [kb.sb("bldo", [64, 512], F32, es) for _ in range(2)])
        khat = Ring([kb.sb("khat", [64, 256], BF16, es) for _ in range(2)])
        am = Ring([kb.sb("bam", [64, 4, 64], BF16, es) for _ in range(2)])
        sm = Ring([kb.sb("bsm", [64, 8], F32, es) for _ in range(2)])
        yb = Ring([kb.sb("bby", [64, 512], BF16, es) for _ in range(2)])
        bjunk = kb.sb("bjunk", [64, 128], F32, es)
        ev_state = [ev_z, ev_z]
        for c in range(32):
            rows = slice(c * 64, (c + 1) * 64)
            ki, kbuf, ev_k = ldk.load(lambda b: [(b[:], dr['gk'][rows, :])])
            vi, vbuf, ev_v = ldv.load(lambda b: [(b[:], dr['gv'][rows, :])])
            oi, obuf, ev_o = ldo.load(lambda b: [(b[:], dr['gout'][rows, :])])
            bi, bank, bfree = banks.get()
            kb.wait('pe', bfree)
            ev_mm = kb.mark('pe', nc.tensor.matmul(bank[0:64, 0:256], lhsT=cst['slow_b'][:], rhs=g_all[:, c, :], start=True, stop=True))
            fi, fb, ffree = f1.get()
            kb.wait('act', ev_mm, ffree)
            ev_e = kb.mark('act', nc.scalar.activation(out=fb[0:64, 0:256], in_=bank[0:64, 0:256], func=AF.Exp))
            banks.release(bi, ev_e)
            hi, kh, hfree = khat.get()
            kb.wait('dve', ev_e, ev_k, hfree)
            ev_kh = kb.mark('dve', nc.vector.tensor_tensor(out=kh[:], in0=kbuf[:], in1=fb[0:64, 0:256], op=ALU.mult))
            f1.release(fi, ev_kh)
            ldk.release(ki, ev_kh)
            sb_ = []
            for par in range(2):
                bi, sbk, bfree = banks.get()
                kb.wait('pe', bfree)
                r_ = slice(par * 64, (par + 1) * 64)
                for hh in range(2):
                    ins = nc.tensor.matmul(sbk[0:64, hh * 64:(hh + 1) * 64], lhsT=ktT[hh][r_, rows], rhs=qtT[hh][r_, rows], start=True, stop=True)
                sb_.append((bi, sbk))
            ev_s = kb.mark('pe', ins)
            kb.tick(1)
            ai, amb, afree = am.get()
            kb.wait('dve', ev_s, afree)
            for par in range(2):
                ev_am = kb.mark('dve', nc.vector.tensor_tensor(out=amb[:, par::2, :], in0=sb_[par][1][0:64, 0:128].rearrange("p (h t) -> p h t", h=2),
                                                                in1=cst['tri4'][:, 0:2, :], op=ALU.mult))
            banks.release(sb_[0][0], ev_am)
            banks.release(sb_[1][0], ev_am)
            bo, obk, bfree = banks.get()
            kb.wait('pe', bfree, ev_am, ev_v, ev_state[0], ev_state[1])
            for h in range(4):
                ins = nc.tensor.matmul(obk[0:64, h * 128:(h + 1) * 128], lhsT=amb[:, h, :], rhs=vbuf[:, h * 128:(h + 1) * 128], start=True, stop=True)
            o2 = []
            for par in range(2):
                b2, ob2, bfree = banks.get()
                kb.wait('pe', bfree)
                r_ = slice(par * 64, (par + 1) * 64)
                for hh in range(2):
                    ins = nc.tensor.matmul(ob2[0:64, hh * 128:(hh + 1) * 128], lhsT=qtT[hh][r_, rows], rhs=Sbf[hh][r_, c, :], start=True, stop=True)
                o2.append((b2, ob2))
            ev_ob = kb.mark('pe', ins)
            am.release(ai, ev_ob)
            kb.tick(1)
            kb.wait('pe', ev_kh)
            for hp in range(2):
                bk, kvb, bfree = banks.get()
                kb.wait('pe', bfree)
                ev_kv = kb.mark('pe', nc.tensor.matmul(kvb[:, 0:256], lhsT=kh[:, hp * 128:(hp + 1) * 128], rhs=vbuf[:, hp * 256:(hp + 1) * 256],
                                                       start=True, stop=True))
                kb.wait('dve', ev_kv)
                ev_S = kb.mark('dve', nc.vector.scalar_tensor_tensor(out=S[hp][:], in0=S[hp][:], scalar=elast[:, hp, c:c + 1], in1=kvb[:, 0:256],
                                                                    op0=ALU.mult, op1=ALU.add))
                banks.release(bk, ev_S)
                kb.wait('act', ev_S)
                nc.scalar.copy(out=Sbf[hp][0:64, c + 1, :], in_=S[hp][0:64, 0:128])
                ev_state[hp] = kb.mark('act', nc.scalar.copy(out=Sbf[hp][64:128, c + 1, :], in_=S[hp][64:128, 128:256]))
                kb.wait('dve', ev_state[hp])
            khat.release(hi, ev_kv)
            ldv.release(vi, ev_kv)
            fi, sq, ffree = f1.get()
            kb.wait('act', ev_ob, ffree)
            for par in range(2):
                ev_q = kb.mark('act', nc.scalar.copy(out=sq[0:64, :].rearrange("p (h d) -> p h d", h=4)[:, par::2, :],
                                                      in_=o2[par][1][0:64, 0:256].rearrange("p (h d) -> p h d", h=2)))
            banks.release(o2[0][0], ev_q)
            banks.release(o2[1][0], ev_q)
            kb.wait('dve', ev_q)
            ev_os = kb.mark('dve', nc.vector.tensor_tensor(out=sq[0:64, :], in0=obk[0:64, :], in1=sq[0:64, :], op=ALU.add))
            banks.release(bo, ev_os)
            si, smb, sfree = sm.get()
            kb.wait('act', ev_os, sfree)
            for h in range(4):
                ev = kb.mark('act', nc.scalar.activation(out=bjunk[:], in_=sq[0:64, h * 128:(h + 1) * 128], func=AF.Square, accum_out=smb[:, h:h + 1]))
            kb.wait('act', ev)
            ev = kb.mark('act', nc.scalar.activation(out=smb[:, 4:8], in_=smb[:, 0:4], func=AF.Sqrt, scale=1.0 / 128, bias=cst['eps'][0:64, :]))
            gi, sg, gfree = f2.get()
            kb.wait('act', ev_o, gfree)
            ev_sg = kb.mark('act', nc.scalar.activation(out=sg[0:64, :], in_=obuf[:], func=AF.Silu))
            ldo.release(oi, ev_sg)
            kb.wait('dve', ev)
            ev = kb.mark('dve', nc.vector.reciprocal(out=smb[:, 4:8], in_=smb[:, 4:8]))
            kb.wait('dve', ev)
            for h in range(4):
                nc.vector.tensor_scalar(out=sq[0:64, h * 128:(h + 1) * 128], in0=sq[0:64, h * 128:(h + 1) * 128], scalar1=smb[:, 4 + h:5 + h],
                                        scalar2=None, op0=ALU.mult)
            kb.wait('dve', ev_sg)
            nc.vector.tensor_tensor(out=sq[0:64, :], in0=sq[0:64, :], in1=sg[0:64, :], op=ALU.mult)
            yi, ybuf, yfree = yb.get()
            kb.wait('dve', yfree)
            ev_y = kb.mark('dve', nc.vector.tensor_tensor(out=ybuf[:], in0=sq[0:64, :], in1=gn[:], op=ALU.mult))
            f1.release(fi, ev_y)
            f2.release(gi, ev_y)
            sm.release(si, ev_y)
            kb.tick(1)
            ti, tb, tfree = tbank.get()
            kb.wait('pe', ev_y, tfree)
            for h in range(4):
                ins = nc.tensor.transpose(out=tb[:, h, :], in_=ybuf[:, h * 128:(h + 1) * 128], identity=cst['ident'][0:64, 0:64])
            ev_t = kb.mark('pe', ins)
            yb.release(yi, ev_t)
            kb.wait('act', ev_t)
            ev_c = kb.mark('act', nc.scalar.copy(out=yTst[:, :, rows], in_=tb[:]))
            tbank.release(ti, ev_c)
        kb.wait('sp', ev_c)
        ks = kb.newsem("bst")
        evs = [kb.dma('sp', dr['yT'][4 + h], yTst[:, h, :], ks) for h in range(4)]
        kb.barrier(extra=evs[-1:])


def mixer_ssd(kb, l, dr, cst):
    nc = kb.nc
    with ExitStack() as es:
        kp = kb.newsem("dp")
        scw = kb.sb("scw", [128, 8, 4], F32, es)
        scb = kb.sb("scb", [128, 8], F32, es)
        dtb = kb.sb("dtb", [64, 8], F32, es)
        aneg = kb.sb("aneg", [64, 8], F32, es)
        dsk = kb.sb("dsk", [64, 512], F32, es)
        nw = kb.sb("dnw", [64, 512], F32, es)
        kb.dma('sp', scw[:], dr['scw'][l], kp)
        kb.dma('sp', scb[:], dr['scb'][l], kp)
        kb.dma('sp', dtb[:], dr['dtb'][l, 0:64, :], kp)
        kb.dma('sp', aneg[:], dr['alog'][l, 0:64, :], kp)
        kb.dma('sp', dsk[:], dr['ssdd'][l, 0:64, :], kp)
        ev_p = kb.dma('sp', nw[:], dr['ssdn'][l, 0:64, :], kp)
        kb.wait('act', ev_p)
        ev = kb.mark('act', nc.scalar.activation(out=aneg[:], in_=aneg[:], func=AF.Exp))
        kb.wait('dve', ev, ev_p)
        ev_an = kb.mark('dve', nc.vector.tensor_scalar(out=aneg[:], in0=aneg[:], scalar1=-1.0, scalar2=None, op0=ALU.mult))
        kb.wait('pool', ev_p)
        xc = [kb.sb("xc", [128, T], BF16, es) for _ in range(8)]
        d0es = ExitStack()
        cbuf = [kb.sb("dcb", [128, 3 + T], F32, d0es) for _ in range(2)]
        cacc = [kb.sb("dca", [128, T], F32, d0es) for _ in range(2)]
        kc = [kb.newsem("dc"), kb.newsem("dc")]
        cfree = [None, None]
        afree = [None, None]
        ev_x = None
        for ch in range(8):
            p = ch % 2
            e = 'dve'
            eng = kb.E[e]
            kb.wait(e, cfree[p])
            ev_z = kb.mark(e, eng.memset(cbuf[p][:, 0:3], 0.0))
            kb.wait('sp', cfree[p])
            ev_l = kb.dma('sp', cbuf[p][:, 3:3 + T], dr['xbcT'][ch], kc[p])
            kb.wait(e, ev_l, ev_z, afree[p])
            eng.tensor_scalar(out=cacc[p][:], in0=cbuf[p][:, 0:T], scalar1=scw[:, ch, 0:1], scalar2=scb[:, ch:ch + 1], op0=ALU.mult, op1=ALU.add)
            for j in range(1, 4):
                ins = eng.scalar_tensor_tensor(out=cacc[p][:], in0=cbuf[p][:, j:j + T], scalar=scw[:, ch, j:j + 1], in1=cacc[p][:],
                                               op0=ALU.mult, op1=ALU.add)
            ev_a = kb.mark(e, ins)
            cfree[p] = ev_a
            kb.wait('act', ev_a)
            ev_x = kb.mark('act', nc.scalar.activation(out=xc[ch][:], in_=cacc[p][:], func=AF.Silu))
            afree[p] = ev_x
            kb.tick(2)
        for e_ in ('pe', 'act', 'dve', 'sp'):
            kb.wait(e_, ev_x, ev_a)
        d0es.close()
        H = [kb.sb("H", [128, 256], F32, es) for _ in range(2)]
        Hbf = [kb.sb("Hbf", [128, 256], BF16, es) for _ in range(2)]
        yTst = kb.sb("dyT", [128, 4, T], BF16, es)
        for g in range(2):
            nc.vector.memset(H[g][:], 0.0)
            ev_h0 = kb.mark('dve', nc.vector.memset(Hbf[g][:], 0.0))
        ev_hbf = [ev_h0, ev_h0]
        banks = Ring([kb.ps("db", [128, 512], F32, es) for _ in range(6)])
        tbank = Ring([kb.ps("dtb", [128, 1024], BF16, es) for _ in range(1)])
        ldt = Loader(kb, [kb.sb("dldt", [64, 8], F32, es) for _ in range(2)])
        ldz = Loader(kb, [kb.sb("dldz", [64, 512], F32, es) for _ in range(2)])
        xB = Ring([kb.sb("dxB", [64, 768], BF16, es) for _ in range(2)])
        sm = Ring([kb.sb("dsm", [128, 48], F32, es) for _ in range(2)])
        lab = Ring([kb.sb("dlab", [64, 8], BF16, es) for _ in range(2)])
        rseg = Ring([kb.sb("drs", [64, 8, 64], BF16, es) for _ in range(2)])
        Bw = Ring([kb.sb("dBw", [64, 8, 128], BF16, es) for _ in range(2)])
        Eb = Ring([kb.sb("dE", [64, 512], F32, es) for _ in range(2)])
        cbs = Ring([kb.sb("dcbs", [64, 128], F32, es) for _ in range(2)])
        MT = Ring([kb.sb("dMT", [64, 8, 64], BF16, es) for _ in range(2)])
        ydsb = Ring([kb.sb("dyd", [64, 512], F32, es) for _ in range(2)])
        ybuf = Ring([kb.sb("dy", [64, 512], F32, es) for _ in range(2)])
        szb = Ring([kb.sb("dsz", [64, 512], F32, es) for _ in range(2)])
        yo16 = Ring([kb.sb("dyo", [64, 512], BF16, es) for _ in range(2)])
        junk = kb.sb("djunk", [64, 256], F32, es)
        kb.wait('dve', ev_an)
        for c in range(32):
            rows = slice(c * 64, (c + 1) * 64)
            di, dtr, ev_d = ldt.load(lambda b: [(b[:], dr['dt'][rows, :])])
            zi, zb, ev_zl = ldz.load(lambda b: [(b[:], dr['z'][rows, :])])
            ti, tb, tfree = tbank.get()
            kb.wait('pe', tfree)
            for k in range(6):
                ins = nc.tensor.transpose(out=tb[0:64, k * 128:(k + 1) * 128], in_=xc[k][:, rows], identity=cst['ident'][:])
            ev_t = kb.mark('pe', ins)
            xi, xb, xfree = xB.get()
            kb.wait('act', ev_t, xfree)
            ev_xb = kb.mark('act', nc.scalar.copy(out=xb[:], in_=tb[0:64, 0:768]))
            tbank.release(ti, ev_xb)
            si, s_, sfree = sm.get()
            kb.wait('dve', ev_d, sfree)
            ev = kb.mark('dve', nc.vector.tensor_tensor(out=s_[0:64, 0:8], in0=dtr[:], in1=dtb[:], op=ALU.add))
            ldt.release(di, ev)
            kb.wait('act', ev)
            ev = kb.mark('act', nc.scalar.activation(out=s_[0:64, 0:8], in_=s_[0:64, 0:8], func=AF.Exp))
            kb.wait('act', ev)
            ev = kb.mark('act', nc.scalar.activation(out=s_[0:64, 0:8], in_=s_[0:64, 0:8], func=AF.Ln, bias=cst['one'][0:64, :]))
            kb.wait('dve', ev)
            ev = kb.mark('dve', nc.vector.tensor_tensor(out=s_[0:64, 8:16], in0=s_[0:64, 0:8], in1=aneg[:], op=ALU.mult))
            kb.wait('dve', ev)
            li, lb_, lfree = lab.get()
            kb.wait('dve', lfree)
            ev_la = kb.mark('dve', nc.vector.tensor_copy(out=lb_[:], in_=s_[0:64, 8:16]))
            ri, rs_, rfree = rseg.get()
            kb.wait('act', ev_la, rfree)
            for e8 in range(8):
                ins = nc.scalar.mul(out=rs_[:, e8, :], in_=cst['tri_f'][:], mul=s_[0:64, 8 + e8:9 + e8])
            ev_rs = kb.mark('act', ins)
            kb.tick(1)
            bi, cb_, bfree = banks.get()
            kb.wait('pe', ev_la, bfree)
            nc.tensor.matmul(cb_[0:64, 0:8], lhsT=cst['tri_b'][:], rhs=lb_[:], start=True, stop=True)
            nc.tensor.matmul(cb_[0:64, 8:16], lhsT=cst['slow_b'][:], rhs=lb_[:], start=True, stop=True)
            ev_cm = kb.mark('pe', nc.tensor.matmul(cb_[:, 16:24], lhsT=cst['ones_b'][0:64, :], rhs=lb_[:], start=True, stop=True))
            lab.release(li, ev_cm)
            kb.wait('act', ev_cm)
            nc.scalar.activation(out=s_[0:64, 16:32], in_=cb_[0:64, 0:16], func=AF.Exp)
            ev_ex = kb.mark('act', nc.scalar.activation(out=s_[:, 32:40], in_=cb_[:, 16:24], func=AF.Exp))
            banks.release(bi, ev_ex)
            kb.wait('dve', ev_ex)
            ev_w = kb.mark('dve', nc.vector.tensor_tensor(out=s_[0:64, 40:48], in0=s_[0:64, 24:32], in1=s_[0:64, 0:8], op=ALU.mult))
            wi, bw, wfree = Bw.get()
            kb.wait('act', ev_w, ev_xb, wfree)
            for e8 in range(8):
                g = e8 // 4
                ins = nc.scalar.mul(out=bw[:, e8, :], in_=xb[:, 512 + g * 128:512 + (g + 1) * 128], mul=s_[0:64, 40 + e8:41 + e8])
            ev_bw = kb.mark('act', ins)
            kb.tick(1)
            b1, segb, bfree = banks.get()
            kb.wait('pe', ev_rs, bfree)
            for g in range(2):
                nc.tensor.matmul(segb[0:64, g * 256:(g + 1) * 256], lhsT=cst['slow_b'][:], rhs=rs_[:, g * 4:(g + 1) * 4, :], start=True, stop=False)
                ins = nc.tensor.matmul(segb[0:64, g * 256:(g + 1) * 256], lhsT=cst['ident'][0:64, 0:64], rhs=cst['negm4'][:], start=False, stop=True)
            ev_sg = kb.mark('pe', ins)
            rseg.release(ri, ev_sg)
            b2, cbb, bfree = banks.get()
            kb.wait('pe', bfree)
            for g in range(2):
                ins = nc.tensor.matmul(cbb[0:64, g * 64:(g + 1) * 64], lhsT=xc[4 + g][:, rows], rhs=xc[6 + g][:, rows], start=True, stop=True)
            ev_cb = kb.mark('pe', ins)
            ei, E, efree = Eb.get()
            ci, cs_, cfree2 = cbs.get()
            kb.wait('act', ev_sg, ev_cb, efree, cfree2)
            nc.scalar.activation(out=E[:], in_=segb[0:64, :], func=AF.Exp)
            ev_E = kb.mark('act', nc.scalar.copy(out=cs_[:], in_=cbb[0:64, 0:128]))
            banks.release(b1, ev_E)
            banks.release(b2, ev_E)
            mi, mt, mfree = MT.get()
            kb.wait('dve', ev_E, mfree)
            for e8 in range(8):
                g = e8 // 4
                ins = nc.vector.scalar_tensor_tensor(out=mt[:, e8, :], in0=E[:, e8 * 64:(e8 + 1) * 64], scalar=s_[0:64, e8:e8 + 1],
                                                     in1=cs_[:, g * 64:(g + 1) * 64], op0=ALU.mult, op1=ALU.mult)
            ev_mt = kb.mark('dve', ins)
            Eb.release(ei, ev_mt)
            cbs.release(ci, ev_mt)
            kb.tick(1)
            b3, ydb, bfree = banks.get()
            kb.wait('pe', ev_mt, ev_xb, bfree)
            for e8 in range(8):
                ins = nc.tensor.matmul(ydb[0:64, e8 * 64:(e8 + 1) * 64], lhsT=mt[:, e8, :], rhs=xb[:, e8 * 64:(e8 + 1) * 64], start=True, stop=True)
            ev_yd = kb.mark('pe', ins)
            MT.release(mi, ev_yd)
            b4, yob, bfree = banks.get()
            kb.wait('pe', bfree, ev_hbf[0], ev_hbf[1])
            for g in range(2):
                ins = nc.tensor.matmul(yob[0:64, g * 256:(g + 1) * 256], lhsT=xc[6 + g][:, rows], rhs=Hbf[g][:], start=True, stop=True)
            ev_yo = kb.mark('pe', ins)
            b5, csb, bfree = banks.get()
            kb.wait('pe', ev_bw, bfree)
            for e8 in range(8):
                ins = nc.tensor.matmul(csb[:, e8 * 64:(e8 + 1) * 64], lhsT=bw[:, e8, :], rhs=xb[:, e8 * 64:(e8 + 1) * 64], start=True, stop=True)
            ev_cs = kb.mark('pe', ins)
            Bw.release(wi, ev_cs)
            kb.wait('dve', ev_cs)
            for e8 in range(8):
                g = e8 // 4
                cs2 = slice((e8 % 4) * 64, (e8 % 4 + 1) * 64)
                ins = nc.vector.scalar_tensor_tensor(out=H[g][:, cs2], in0=H[g][:, cs2], scalar=s_[:, 32 + e8:33 + e8],
                                                     in1=csb[:, e8 * 64:(e8 + 1) * 64], op0=ALU.mult, op1=ALU.add)
            ev_H = kb.mark('dve', ins)
            banks.release(b5, ev_H)
            kb.wait('act', ev_H, ev_yo)
            nc.scalar.copy(out=Hbf[0][:], in_=H[0][:])
            ev_hb = kb.mark('act', nc.scalar.copy(out=Hbf[1][:], in_=H[1][:]))
            ev_hbf = [ev_hb, ev_hb]
            kb.wait('dve', ev_hb)
            yi_, ydv, yfree = ydsb.get()
            kb.wait('act', ev_yd, yfree)
            ev_ydc = kb.mark('act', nc.scalar.copy(out=ydv[:], in_=ydb[0:64, :]))
            banks.release(b3, ev_ydc)
            yb_i, y, ybfree = ybuf.get()
            kb.wait('dve', ev_ydc, ev_yo, ybfree)
            for e8 in range(8):
                cs2 = slice(e8 * 64, (e8 + 1) * 64)
                ins = nc.vector.scalar_tensor_tensor(out=y[:, cs2], in0=yob[0:64, cs2], scalar=s_[0:64, 16 + e8:17 + e8], in1=ydv[:, cs2],
                                                     op0=ALU.mult, op1=ALU.add)
            ev_c1 = kb.mark('dve', ins)
            banks.release(b4, ev_c1)
            ydsb.release(yi_, ev_c1)
            zi2, sz, zfree = szb.get()
            kb.wait('act', ev_zl, zfree)
            ev_sz = kb.mark('act', nc.scalar.activation(out=sz[:], in_=zb[:], func=AF.Silu))
            ldz.release(zi, ev_sz)
            tmpd = ydv
            nc.vector.tensor_tensor(out=tmpd[:], in0=xb[:, 0:512], in1=dsk[:], op=ALU.mult)
            nc.vector.tensor_tensor(out=y[:], in0=y[:], in1=tmpd[:], op=ALU.add)
            kb.wait('dve', ev_sz)
            ev_y3 = kb.mark('dve', nc.vector.tensor_tensor(out=y[:], in0=y[:], in1=sz[:], op=ALU.mult))
            ydsb.release(yi_, ev_y3)
            xB.release(xi, ev_y3)
            szb.release(zi2, ev_y3)
            kb.wait('act', ev_y3)
            for g in range(2):
                ev = kb.mark('act', nc.scalar.activation(out=junk[:], in_=y[:, g * 256:(g + 1) * 256], func=AF.Square, accum_out=s_[0:64, 24 + g:25 + g]))
            kb.wait('act', ev)
            ev = kb.mark('act', nc.scalar.activation(out=s_[0:64, 26:28], in_=s_[0:64, 24:26], func=AF.Sqrt, scale=1.0 / 256, bias=cst['eps'][0:64, :]))
            kb.wait('dve', ev)
            ev = kb.mark('dve', nc.vector.reciprocal(out=s_[0:64, 26:28], in_=s_[0:64, 26:28]))
            kb.wait('dve', ev)
            oi, yo_, ofree = yo16.get()
            kb.wait('dve', ofree)
            for g in range(2):
                ins = nc.vector.scalar_tensor_tensor(out=yo_[:, g * 256:(g + 1) * 256], in0=y[:, g * 256:(g + 1) * 256], scalar=s_[0:64, 26 + g:27 + g],
                                                     in1=nw[:, g * 256:(g + 1) * 256], op0=ALU.mult, op1=ALU.mult)
            ev_yf = kb.mark('dve', ins)
            ybuf.release(yb_i, ev_yf)
            sm.release(si, ev_yf)
            ti, tb, tfree = tbank.get()
            kb.wait('pe', ev_yf, tfree)
            for h in range(4):
                ins = nc.tensor.transpose(out=tb[:, h * 64:(h + 1) * 64], in_=yo_[:, h * 128:(h + 1) * 128], identity=cst['ident'][0:64, 0:64])
            ev_t2 = kb.mark('pe', ins)
            yo16.release(oi, ev_t2)
            kb.wait('act', ev_t2)
            ev_c = kb.mark('act', nc.scalar.copy(out=yTst[:, :, rows], in_=tb[:, 0:256].rearrange("p (h t) -> p h t", h=4)))
            tbank.release(ti, ev_c)
        kb.wait('sp', ev_c)
        ks = kb.newsem("dst")
        evs = [kb.dma('sp', dr['yT'][12 + h], yTst[:, h, :], ks) for h in range(4)]
        kb.drain()
        kb.barrier(extra=evs[-1:] + [('pool', kb.pcnt['pool'])] + list(getattr(kb, 'bg_extra', [])))

def make_consts(kb):
    nc = kb.nc
    c = {}
    g = nc.gpsimd
    ones_f = kb.sb("onesf", [128, 128], F32)
    g.memset(ones_f[:], 1.0)
    zeros_f = kb.sb("zerosf", [128, 256], F32)
    g.memset(zeros_f[:], 0.0)
    ident_f = kb.sb("identf", [128, 128], F32)
    g.affine_select(out=ident_f[:], in_=ones_f[:], pattern=[[-1, 128]], compare_op=ALU.is_equal, fill=0.0, base=0, channel_multiplier=1)
    ident = kb.sb("ident", [128, 128], BF16)
    g.tensor_copy(out=ident[:], in_=ident_f[:])
    ones_b = kb.sb("onesb", [128, 128], BF16)
    g.tensor_copy(out=ones_b[:], in_=ones_f[:])
    ob = kb.sb("onesblk", [128, 128], BF16)
    g.memset(ob[:], 0.0)
    g.memset(ob[0:64, 0:64], 1.0)
    g.memset(ob[64:128, 64:128], 1.0)
    eps = kb.sb("epsc", [128, 1], F32)
    g.memset(eps[:], EPS)
    one = kb.sb("onec", [128, 1], F32)
    g.memset(one[:], 1.0)
    tri_f = kb.sb("trif", [64, 64], F32)
    g.affine_select(out=tri_f[:], in_=ones_f[0:64, 0:64], pattern=[[1, 64]], compare_op=ALU.is_ge, fill=0.0, base=0, channel_multiplier=-1)
    tri_b = kb.sb("trib", [64, 64], BF16)
    g.tensor_copy(out=tri_b[:], in_=tri_f[:])
    slow_f = kb.sb("slowf", [64, 64], F32)
    g.affine_select(out=slow_f[:], in_=ones_f[0:64, 0:64], pattern=[[-1, 64]], compare_op=ALU.is_gt, fill=0.0, base=0, channel_multiplier=1)
    slow_b = kb.sb("slowb", [64, 64], BF16)
    g.tensor_copy(out=slow_b[:], in_=slow_f[:])
    tri4 = kb.sb("tri4", [64, 4, 64], F32)
    for a in range(4):
        g.tensor_copy(out=tri4[:, a, :], in_=tri_f[:])
    negf = kb.sb("negf", [64, 64], F32)
    g.affine_select(out=negf[:], in_=zeros_f[0:64, 0:64], pattern=[[1, 64]], compare_op=ALU.is_ge, fill=NEG, base=0, channel_multiplier=-1)
    negm4 = kb.sb("negm4", [64, 256], BF16)
    for a in range(4):
        ins = g.tensor_copy(out=negm4[:, a * 64:(a + 1) * 64], in_=negf[:])
    ev = kb.mark('pool', ins)
    c.update(ident=ident, ident_f=ident_f, ones_f=ones_f, ones_b=ones_b, onesblk=ob, eps=eps, one=one, tri_f=tri_f, tri_b=tri_b,
             slow_b=slow_b, tri4=tri4, negm4=negm4, ev=ev)
    return c


def prep_host(inputs):
    f = np.float32
    P = {}
    rep = lambda a, n=128: np.ascontiguousarray(np.broadcast_to(a[:, None, :], (a.shape[0], n, a.shape[-1])))
    w_in = inputs['w_in']
    win = np.zeros((L, 13, 128, 16, 512), f)
    for g, cols in enumerate(WIN_GROUPS):
        sub = w_in[:, :, cols]
        win[:, g, :, :, 0:len(cols)] = sub.reshape(L, 16, 128, len(cols)).transpose(0, 2, 1, 3)
    P['win'] = win
    q = inputs['diff_qk_norm']
    P['qkg'] = np.ascontiguousarray(np.concatenate([q, q], axis=2).transpose(0, 2, 1))
    P['mixn'] = rep(inputs['mix_norm'])
    P['ffnn'] = rep(inputs['ffn_norm'])
    P['lam'] = rep(inputs['diff_lambda'].reshape(L, 256))
    P['subln'] = rep(inputs['diff_subln'])
    w2a = np.zeros((L, 32, 256), f)
    w2a[:, 0:16] = inputs['gla_gk_w2']
    w2a[:, 16] = inputs['gla_gk_b']
    P['w2aug'] = w2a
    P['glan'] = rep(np.tile(inputs['gla_norm'], (1, 4)))
    P['cdw'] = np.ascontiguousarray(inputs['conv_dw_w'].reshape(L, 31, 4, 128).transpose(0, 3, 2, 1))
    fm = lambda a, n: np.ascontiguousarray(a.reshape(L, n, 128).transpose(0, 2, 1))
    P['cdb'] = fm(inputs['conv_dw_b'], 4)
    P['clnw'] = fm(inputs['conv_ln_w'], 4)
    P['clnb'] = fm(inputs['conv_ln_b'], 4)
    P['scw'] = np.ascontiguousarray(inputs['ssd_conv_w'].reshape(L, 4, 8, 128).transpose(0, 3, 2, 1))
    P['scb'] = fm(inputs['ssd_conv_b'], 8)
    P['dtb'] = rep(inputs['ssd_dt_bias'])
    P['alog'] = rep(inputs['ssd_a_log'])
    P['ssdd'] = rep(np.repeat(inputs['ssd_d'], 64, axis=1))
    P['ssdn'] = rep(inputs['ssd_norm'])
    wg = inputs['w_gate'].reshape(L, 4, 16, 128, 16, 128)
    wb = inputs['w_branch'].reshape(L, 4, 4, 128, 16, 128)
    wgb = np.empty((L, 16, 4, 128, 20, 128), f)
    wgb[:, :, :, :, 0:16, :] = wg.transpose(0, 4, 1, 3, 2, 5)
    wgb[:, :, :, :, 16:20, :] = wb.transpose(0, 4, 1, 3, 2, 5)
    P['wgb'] = wgb
    P['bgate'] = np.ascontiguousarray(inputs['b_gate'].reshape(L, 4, 16, 128).transpose(0, 3, 1, 2))
    P['wout'] = np.ascontiguousarray(inputs['w_out'].reshape(L, 16, 128, 4, 512).transpose(0, 3, 2, 1, 4))
    up = inputs['ffn_w_up'].reshape(L, 16, 128, 2, NPAIR, 128)
    wup = np.empty((L, 22, 128, 16, 4, 128), f)
    upr = up.transpose(0, 4, 2, 1, 3, 5)
    upr = upr.reshape(L, 22, 2, 128, 16, 2, 128)
    wup[:] = upr.transpose(0, 1, 3, 4, 2, 5, 6).reshape(L, 22, 128, 16, 4, 128)
    P['wup'] = wup.reshape(L, 22, 128, 16, 512)
    fcw = inputs['ffn_conv_w'].reshape(L, 3, 2, NPAIR, 128)
    P['fcw'] = np.ascontiguousarray(fcw.transpose(0, 4, 3, 2, 1).reshape(L, 128, 88, 3))
    fcb = inputs['ffn_conv_b'].reshape(L, 2, NPAIR, 128)
    P['fcb'] = np.ascontiguousarray(fcb.transpose(0, 3, 2, 1).reshape(L, 128, 88))
    P['wdn'] = np.ascontiguousarray(inputs['ffn_w_down'].reshape(L, NPAIR, 128, 4, 512).transpose(0, 3, 2, 1, 4))
    return P


SCRATCH = dict(
    qT=([4, 128, T], BF16), kT=([4, 128, T], BF16), v=([T, 512], BF16),
    gqkT=([4, 128, T], F32), gk=([T, 256], F32), gv=([T, 512], BF16), gout=([T, 512], F32), glowT=([16, T], F32),
    gluT=([4, 128, T], F32), xbcT=([8, 128, T], F32), z=([T, 512], F32), dt=([T, 8], F32),
    yT=([16, 128, T], BF16), mT=([16, 128, T], BF16), gT=([4, 16, 128, T], BF16), aT=([8, 128, NPAIR, 256], BF16),
    xm=([T, D], F32), xc=([T, D], F32),
)

PARAM_SHAPES = dict(
    win=[L, 13, 128, 16, 512], qkg=[L, 128, 2], mixn=[L, 128, D], ffnn=[L, 128, D], lam=[L, 128, 256], subln=[L, 128, 128],
    w2aug=[L, 32, 256], glan=[L, 128, 512], cdw=[L, 128, 4, 31], cdb=[L, 128, 4], clnw=[L, 128, 4], clnb=[L, 128, 4],
    scw=[L, 128, 8, 4], scb=[L, 128, 8], dtb=[L, 128, 8], alog=[L, 128, 8], ssdd=[L, 128, 512], ssdn=[L, 128, 512],
    wgb=[L, 16, 4, 128, 20, 128], bgate=[L, 128, 4, 16], wout=[L, 4, 128, 16, 512], wup=[L, 22, 128, 16, 512],
    fcw=[L, 128, 88, 3], fcb=[L, 128, 88], wdn=[L, 4, 128, NPAIR, 512],
)


def build(dbg_outs=(), nlayers=L, stop_after=None, skip=()):
    nc = bass.Bass("TRN2", target_bir_lowering=False)
    dr = {}
    x_in = nc.dram_tensor("x", [T, D], F32, kind="ExternalInput").ap()
    for name, shape in PARAM_SHAPES.items():
        dr[name] = nc.dram_tensor(name, shape, F32, kind="ExternalInput").ap()
    y_out = nc.dram_tensor("y", [T, D], F32, kind="ExternalOutput").ap()
    for name, (shape, dt_) in SCRATCH.items():
        kind = "ExternalOutput" if name in dbg_outs else "Internal"
        dr[name] = nc.dram_tensor("s_" + name, shape, dt_, kind=kind).ap()
    kb = KB(nc)
    cst = make_consts(kb)
    for e in ('pe', 'act', 'dve', 'sp'):
        kb.wait(e, cst['ev'])
    for l in range(nlayers):
        last = (l == nlayers - 1)
        lam_init = 0.8 - 0.6 * math.exp(-0.3 * l)
        x_src = x_in if l == 0 else dr['xc']
        x_dst = y_out if last else dr['xc']
        with ExitStack() as hes:
            hT = kb.sb("hT", [128, KC, T], BF16, hes)
            phase_norm(kb, x_src, dr['mixn'][l], hT, cst['ident'], cst['eps'])
            phase_inproj(kb, l, hT, dr, cst)
            if 'A' not in skip:
                mixer_attn(kb, l, dr, cst, lam_init, bg=(mixer_conv_gen(kb, l, dr, cst) if 'C' not in skip else None))
            elif 'C' not in skip:
                mixer_conv(kb, l, dr, cst)
            with ExitStack() as ges:
                gres = gates_alloc(kb, l, dr, ges)
                kb.bg_extra = []
                kb.bg = gates_task(kb, l, hT, dr, cst, gres)
                mixer_gla(kb, l, dr, cst)
                mixer_ssd(kb, l, dr, cst)
            kb.freelist.extend(kb.pinned)
            kb.pinned = []
        phase_branch(kb, l, dr, cst)
        phase_outproj(kb, l, x_src, dr['xm'], dr, cst)
        with ExitStack() as hes:
            hT = kb.sb("hT", [128, KC, T], BF16, hes)
            phase_norm(kb, dr['xm'], dr['ffnn'][l], hT, cst['ident'], cst['eps'])
            phase_ffn_up(kb, l, hT, dr, cst)
        phase_ffn_down(kb, l, dr['xm'], x_dst, dr, cst)
    kb.es.close()
    return nc


_NC_CACHE = {}


def kernel(**inputs):
    inputs = {k: np.asarray(v) for k, v in inputs.items()}
    P = prep_host(inputs)
    if 'nc' not in _NC_CACHE:
        _NC_CACHE['nc'] = build()
    nc = _NC_CACHE['nc']
    x = np.ascontiguousarray(inputs['x'], dtype=np.float32)
    n_cores = 4
    in_maps = []
    for c in range(n_cores):
        m = dict(P)
        m['x'] = x[c]
        in_maps.append(m)
    res = run_bass_kernel_spmd(nc, in_maps, core_ids=list(range(n_cores)))
    out = np.stack([np.asarray(res.results[c]['y'], dtype=np.float32) for c in range(n_cores)], axis=0)
    return out
```
